# Optimizing a Trainium2 kernel written in Bass

```python
import math
import jax, jax.numpy as jnp
from jax import lax
import numpy as np

D_MODEL = 1024
BATCH = 8
SEQ = 4096
DEPTH = 4
DEC_BATCH = 32
DEC_SEQ = 32
PAST_LEN = 1024

CHUNK = 64
N_BRANCH = 4
W_BR = D_MODEL
W_A = W_BR
W_B = W_BR
W_C = W_BR
H_D = 8
DK = D_MODEL // H_D
DV = W_BR // H_D
W_QK = H_D * DK
W_D = H_D * DV
K_A = 3
K_B = 4
H_B = 8
BW_B = W_B // H_B
K_C = 31
K_D = 4
LRU_C = 8.0
ALPHA = (2 * DEPTH) ** 0.25
BETA_INIT = (8 * DEPTH) ** -0.25
LN_EPS = 1e-5
RMS_EPS = 1e-6
L2_EPS = 1e-6

OFF_A = 0
OFF_B = OFF_A + 4 * W_A
OFF_C = OFF_B + 2 * W_B
OFF_D = OFF_C + 3 * W_C
OFF_DZ = OFF_D + 2 * W_QK + W_D
OFF_DA = OFF_DZ + W_D
OFF_DB = OFF_DA + H_D
OFF_G = OFF_DB + H_D
N_IN = OFF_G + N_BRANCH * D_MODEL

kernel_name = 'hybrid_streaming_encoder_step'

f32 = jnp.float32


def layer_norm(x, g, b):
    xf = x.astype(f32)
    mu = jnp.mean(xf, axis=-1, keepdims=True)
    var = jnp.mean(jnp.square(xf - mu), axis=-1, keepdims=True)
    return ((xf - mu) * lax.rsqrt(var + LN_EPS) * g + b).astype(x.dtype)


def l2norm(x):
    return x * lax.rsqrt(jnp.sum(x * x, axis=-1, keepdims=True) + L2_EPS)


def causal_dwconv(u, buf, w):
    xp = jnp.concatenate([buf.astype(u.dtype), u], axis=1)
    y = lax.conv_general_dilated(xp, w[:, None, :].astype(u.dtype), window_strides=(1,),
                                 padding='VALID', dimension_numbers=('NWC', 'WIO', 'NWC'),
                                 feature_group_count=u.shape[-1])
    return y, xp[:, xp.shape[1] - (w.shape[0] - 1):]


def rg_lru(x, h0, wx, bx, wa, ba, lam):
    bsz, l, w = x.shape
    xf = x.astype(f32)
    xh = xf.reshape(bsz, l, H_B, BW_B)
    gate_x = jax.nn.sigmoid(jnp.einsum('blhi,hij->blhj', xh, wx) + bx).reshape(bsz, l, w)
    gate_a = jax.nn.sigmoid(jnp.einsum('blhi,hij->blhj', xh, wa) + ba).reshape(bsz, l, w)
    log_a = -LRU_C * gate_a * jax.nn.softplus(-lam.astype(f32))
    a = jnp.exp(log_a)
    bterm = jnp.sqrt(-jnp.expm1(2.0 * log_a)) * gate_x * xf
    bterm = bterm.at[:, 0].add(a[:, 0] * h0.astype(f32))

    def combine(e1, e2):
        a1, b1 = e1
        a2, b2 = e2
        return a1 * a2, a2 * b1 + b2

    _, h = lax.associative_scan(combine, (a, bterm), axis=1)
    return h.astype(x.dtype), h[:, -1].astype(x.dtype)


def gated_delta_chunked(q, k, v, g, beta, s0):
    bsz, l, h, dk = q.shape
    dv = v.shape[-1]
    c = min(CHUNK, l)
    n = -(-l // c)
    pad = n * c - l
    if pad:
        p4 = ((0, 0), (0, pad), (0, 0), (0, 0))
        q, k, v = jnp.pad(q, p4), jnp.pad(k, p4), jnp.pad(v, p4)
        g, beta = jnp.pad(g, p4[:3]), jnp.pad(beta, p4[:3])

    def blocks(t):
        t = t.reshape((bsz, n, c, h) + t.shape[3:])
        return jnp.moveaxis(t, (1, 3), (0, 2))

    q = blocks(q) * (dk ** -0.5)
    k, v, g, beta = blocks(k), blocks(v), blocks(g), blocks(beta)
    gc = jnp.cumsum(g, axis=-1)
    idx = jnp.arange(c)
    incl = idx[:, None] >= idx[None, :]
    strict = idx[:, None] > idx[None, :]
    decay = jnp.exp(jnp.where(incl, gc[..., :, None] - gc[..., None, :], -jnp.inf))
    kb = k * beta[..., None]
    a_mat = jnp.where(strict, jnp.einsum('nbhid,nbhjd->nbhij', kb, k) * decay, 0.0)
    t_mat = a_mat + jnp.eye(c, dtype=a_mat.dtype)
    u_base = lax.linalg.triangular_solve(t_mat, v * beta[..., None], left_side=True,
                                         lower=True, unit_diagonal=True)
    w_mat = lax.linalg.triangular_solve(t_mat, kb * jnp.exp(gc)[..., None], left_side=True,
                                        lower=True, unit_diagonal=True)
    p_mat = jnp.einsum('nbhid,nbhjd->nbhij', q, k) * decay
    q_dec = q * jnp.exp(gc)[..., None]
    k_end = k * jnp.exp(gc[..., -1:] - gc)[..., None]
    g_end = jnp.exp(gc[..., -1])

    def step(s, xs):
        u0, w_, qd, pm, ke, ge = xs
        u = u0 - jnp.einsum('bhcd,bhde->bhce', w_, s)
        o = jnp.einsum('bhcd,bhde->bhce', qd, s) + jnp.einsum('bhij,bhje->bhie', pm, u)
        s = s * ge[..., None, None] + jnp.einsum('bhcd,bhce->bhde', ke, u)
        return s, o

    s_fin, o = lax.scan(step, s0, (u_base, w_mat, q_dec, p_mat, k_end, g_end))
    o = jnp.moveaxis(o, (0, 2), (1, 3)).reshape(bsz, n * c, h, dv)[:, :l]
    return o, s_fin


def mixer_layer(x, states, p):
    conv_a, conv_b, lru, conv_c, conv_d, delta = states
    bsz, l, _ = x.shape
    proj = x @ p['w_in']

    def col(off, w):
        return proj[..., off:off + w]

    a_conv, new_conv_a = causal_dwconv(col(OFF_A + W_A, W_A) * col(OFF_A + 2 * W_A, W_A),
                                       conv_a, p['a_conv_w'])
    y_a = col(OFF_A, W_A) * a_conv * jax.nn.silu(col(OFF_A + 3 * W_A, W_A))

    b_xc, new_conv_b = causal_dwconv(col(OFF_B, W_B), conv_b, p['b_conv_w'])
    h, new_lru = rg_lru(b_xc + p['b_conv_b'], lru, p['b_wx'], p['b_bx'], p['b_wa'],
                        p['b_ba'], p['b_lambda'])
    y_b = h * jax.nn.silu(col(OFF_B + W_B, W_B))

    c_glu = col(OFF_C, W_C) * jax.nn.sigmoid(col(OFF_C + W_C, W_C))
    c_cv, new_conv_c = causal_dwconv(c_glu, conv_c, p['c_conv_w'])
    y_c = (jax.nn.silu(layer_norm(c_cv + p['c_conv_b'], p['c_ln_g'], p['c_ln_b']))
           * jax.nn.silu(col(OFF_C + 2 * W_C, W_C)))

    qkv, new_conv_d = causal_dwconv(col(OFF_D, 2 * W_QK + W_D), conv_d, p['d_conv_w'])
    qkv = jax.nn.silu(qkv.astype(f32))
    q = l2norm(qkv[..., :W_QK].reshape(bsz, l, H_D, DK))
    k = l2norm(qkv[..., W_QK:2 * W_QK].reshape(bsz, l, H_D, DK))
    v = qkv[..., 2 * W_QK:].reshape(bsz, l, H_D, DV)
    beta = jax.nn.sigmoid(col(OFF_DB, H_D).astype(f32))
    g = -jnp.exp(p['d_a_log'].astype(f32)) * jax.nn.softplus(col(OFF_DA, H_D).astype(f32)
                                                              + p['d_dt_bias'])
    o, s_new = gated_delta_chunked(q, k, v, g, beta, delta.astype(f32))
    z = col(OFF_DZ, W_D).astype(f32).reshape(bsz, l, H_D, DV)
    o = o * lax.rsqrt(jnp.mean(o * o, axis=-1, keepdims=True) + RMS_EPS) * p['d_norm_g'] * jax.nn.silu(z)
    y_d = o.reshape(bsz, l, W_D).astype(x.dtype)

    ys = jnp.stack([y_a, y_b, y_c, y_d], axis=2)
    yb = jnp.einsum('blnw,nwd->blnd', ys, p['w_branch'])
    gates = jax.nn.sigmoid(col(OFF_G, N_BRANCH * D_MODEL).reshape(bsz, l, N_BRANCH, D_MODEL)
                           + p['b_gate'])
    mixed = jnp.einsum('blnd,blnd->bld', gates, yb)
    out = mixed @ p['w_out']
    x = layer_norm(ALPHA * x + out, p['ln_g'], p['ln_b'])
    return x, (new_conv_a, new_conv_b, new_lru, new_conv_c, new_conv_d, s_new.astype(x.dtype))


def run_trunk(x, states, ln_in_g, ln_in_b, params):
    x = layer_norm(x, ln_in_g, ln_in_b)
    new = [[] for _ in states]
    for i in range(DEPTH):
        p = {name: arr[i] for name, arr in params.items()}
        x, layer_states = mixer_layer(x, tuple(s[i] for s in states), p)
        for acc, s in zip(new, layer_states):
            acc.append(s)
    return x, tuple(jnp.stack(acc) for acc in new)


def setup_inputs(seed: int = 0) -> dict:
    key = jax.random.key(seed)
    ks = jax.random.split(key, 40)

    def nrm(i, shape, scale):
        return scale * jax.random.normal(ks[i], shape, f32)

    a0 = jax.random.uniform(ks[30], (DEPTH, W_B), f32, minval=0.9, maxval=0.999)
    r = a0 ** (1.0 / LRU_C)
    b_lambda = jnp.log(r) - jnp.log1p(-r)
    d_a_log = jnp.log(jax.random.uniform(ks[31], (DEPTH, H_D), f32, minval=1.0, maxval=16.0))
    dt = jnp.exp(jax.random.uniform(ks[32], (DEPTH, H_D), f32,
                                    minval=math.log(1e-3), maxval=math.log(1e-1)))
    d_dt_bias = dt + jnp.log(-jnp.expm1(-dt))
    return {
        'x_prompt': nrm(0, (BATCH, SEQ, D_MODEL), 1.0),
        'x_sample': nrm(1, (DEC_BATCH, DEC_SEQ, D_MODEL), 1.0),
        'state_conv_a': nrm(2, (DEPTH, DEC_BATCH, K_A - 1, W_A), 1.0),
        'state_conv_b': nrm(3, (DEPTH, DEC_BATCH, K_B - 1, W_B), 1.0),
        'state_lru': nrm(4, (DEPTH, DEC_BATCH, W_B), 0.5),
        'state_conv_c': nrm(5, (DEPTH, DEC_BATCH, K_C - 1, W_C), 1.0),
        'state_conv_d': nrm(6, (DEPTH, DEC_BATCH, K_D - 1, 2 * W_QK + W_D), 1.0),
        'state_delta': nrm(7, (DEPTH, DEC_BATCH, H_D, DK, DV), 0.1),
        'ln_in_g': 1.0 + nrm(8, (D_MODEL,), 0.02),
        'ln_in_b': nrm(9, (D_MODEL,), 0.02),
        'w_in': nrm(10, (DEPTH, D_MODEL, N_IN), D_MODEL ** -0.5),
        'b_gate': nrm(11, (DEPTH, N_BRANCH, D_MODEL), 0.1),
        'a_conv_w': nrm(12, (DEPTH, K_A, W_A), K_A ** -0.5),
        'b_conv_w': nrm(13, (DEPTH, K_B, W_B), K_B ** -0.5),
        'b_conv_b': nrm(14, (DEPTH, W_B), 0.02),
        'b_wx': nrm(15, (DEPTH, H_B, BW_B, BW_B), BW_B ** -0.5),
        'b_bx': nrm(16, (DEPTH, H_B, BW_B), 0.02),
        'b_wa': nrm(17, (DEPTH, H_B, BW_B, BW_B), BW_B ** -0.5),
        'b_ba': nrm(18, (DEPTH, H_B, BW_B), 0.02),
        'b_lambda': b_lambda,
        'c_conv_w': nrm(19, (DEPTH, K_C, W_C), K_C ** -0.5),
        'c_conv_b': nrm(20, (DEPTH, W_C), 0.02),
        'c_ln_g': 1.0 + nrm(21, (DEPTH, W_C), 0.02),
        'c_ln_b': nrm(22, (DEPTH, W_C), 0.02),
        'd_conv_w': nrm(23, (DEPTH, K_D, 2 * W_QK + W_D), K_D ** -0.5),
        'd_a_log': d_a_log,
        'd_dt_bias': d_dt_bias,
        'd_norm_g': 1.0 + nrm(24, (DEPTH, DV), 0.02),
        'w_branch': nrm(25, (DEPTH, N_BRANCH, W_BR, D_MODEL), BETA_INIT * W_BR ** -0.5),
        'w_out': nrm(26, (DEPTH, D_MODEL, D_MODEL), BETA_INIT * D_MODEL ** -0.5),
        'ln_g': 1.0 + nrm(27, (DEPTH, D_MODEL), 0.02),
        'ln_b': nrm(28, (DEPTH, D_MODEL), 0.02),
    }


def reference(x_prompt, x_sample, state_conv_a, state_conv_b, state_lru, state_conv_c,
              state_conv_d, state_delta, ln_in_g, ln_in_b, w_in, b_gate, a_conv_w, b_conv_w,
              b_conv_b, b_wx, b_bx, b_wa, b_ba, b_lambda, c_conv_w, c_conv_b, c_ln_g, c_ln_b,
              d_conv_w, d_a_log, d_dt_bias, d_norm_g, w_branch, w_out, ln_g, ln_b):
    params = dict(w_in=w_in, b_gate=b_gate, a_conv_w=a_conv_w, b_conv_w=b_conv_w,
                  b_conv_b=b_conv_b, b_wx=b_wx, b_bx=b_bx, b_wa=b_wa, b_ba=b_ba,
                  b_lambda=b_lambda, c_conv_w=c_conv_w, c_conv_b=c_conv_b, c_ln_g=c_ln_g,
                  c_ln_b=c_ln_b, d_conv_w=d_conv_w, d_a_log=d_a_log, d_dt_bias=d_dt_bias,
                  d_norm_g=d_norm_g, w_branch=w_branch, w_out=w_out, ln_g=ln_g, ln_b=ln_b)
    bp = x_prompt.shape[0]
    dt = x_prompt.dtype

    def zeros(*shape):
        return jnp.zeros((DEPTH, bp) + shape, dt)

    prompt_states = (zeros(K_A - 1, W_A), zeros(K_B - 1, W_B), zeros(W_B),
                     zeros(K_C - 1, W_C), zeros(K_D - 1, 2 * W_QK + W_D), zeros(H_D, DK, DV))
    y_prompt, p_st = run_trunk(x_prompt, prompt_states, ln_in_g, ln_in_b, params)
    sample_states = (state_conv_a, state_conv_b, state_lru, state_conv_c, state_conv_d, state_delta)
    y_sample, s_st = run_trunk(x_sample, sample_states, ln_in_g, ln_in_b, params)

    p_conv_a, p_conv_b, p_lru, p_conv_c, p_conv_d, p_delta = p_st
    s_conv_a, s_conv_b, s_lru, s_conv_c, s_conv_d, s_delta = s_st
    return (y_prompt, y_sample, p_conv_a, p_conv_b, p_lru, p_conv_c, p_conv_d, p_delta,
            s_conv_a, s_conv_b, s_lru, s_conv_c, s_conv_d, s_delta)
```

```python
import collections
import contextlib
import numpy as np
import concourse.bass as bass
import concourse.mybir as mybir
from concourse.bass_utils import run_bass_kernel_spmd

F32 = mybir.dt.float32
F32R = mybir.dt.float32r
ALU = mybir.AluOpType
AF = mybir.ActivationFunctionType

D = 1024
NL = 4
KC = 8
SEQ = 4096
TP = 512
NPT = SEQ // TP
NSS = 4
LS = 32
W3 = 3072
OFF_A, OFF_B, OFF_C, OFF_D = 0, 4096, 6144, 9216
OFF_DZ = 12288
OFF_DA = 13312
OFF_G = 13328
N_IN = 17424
ALPHA = (2 * NL) ** 0.25
LN_EPS = 1e-5
RMS_EPS = 1e-6
L2_EPS = 1e-6
LRU_C = 8.0
NPAR = 63
PR_BG, PR_ACW, PR_BCW, PR_BCB, PR_BBX, PR_BBA, PR_LAM, PR_CCW, PR_CCB, PR_CLG, PR_CLB, PR_DCW, PR_LNG, PR_LNB = \
    0, 4, 7, 11, 12, 13, 14, 15, 46, 47, 48, 49, 61, 62

ENGS = ('pe', 'act', 'dve', 'pool', 'sp')
SAME_ENG_DIST = 4


class Buf:
    __slots__ = ('name', 'excl', 'writes', 'reads')

    def __init__(self, name, excl=False):
        self.name = name
        self.excl = excl
        self.writes = []
        self.reads = []


class V:
    __slots__ = ('bufs', 'ap', 'rr')

    def __init__(self, bufs, ap, rr=False):
        self.bufs = tuple(bufs)
        self.ap = ap
        self.rr = rr

    def __getitem__(self, idx):
        return V(self.bufs, self.ap[idx], self.rr)

    def r(self):
        return V(self.bufs, self.ap.bitcast(F32R), self.rr)

    def rearrange(self, pattern, **kw):
        return V(self.bufs, self.ap.rearrange(pattern, **kw), self.rr)

    @property
    def o(self):
        return self.ap.bitcast(F32R) if self.rr else self.ap


class Op:
    __slots__ = ('eng', 'fn', 'waits', 'signal', 'idx', 'sig_val', 'dsem', 'dval')

    def __init__(self, eng, fn):
        self.eng = eng
        self.fn = fn
        self.waits = []
        self.signal = False
        self.idx = -1
        self.sig_val = None
        self.dsem = None
        self.dval = None


class Sched:
    def __init__(self, nc):
        self.nc = nc
        self.ops = {e: [] for e in ENGS}
        self.waited = {e: collections.defaultdict(int) for e in ENGS}
        self.dma_counts = collections.defaultdict(int)

    def _dep(self, op, prod, same_eng_raw=False):
        if prod is op:
            return
        if prod.dsem is not None:
            key = ('d', prod.dsem)
            if self.waited[op.eng][key] >= prod.dval:
                return
            self.waited[op.eng][key] = prod.dval
            op.waits.append((key, prod.dval))
            return
        if prod.eng == op.eng:
            if not same_eng_raw:
                return
            if op.idx - prod.idx >= SAME_ENG_DIST:
                return
        key = ('e', prod.eng)
        if self.waited[op.eng][key] >= prod.idx + 1:
            return
        self.waited[op.eng][key] = prod.idx + 1
        prod.signal = True
        op.waits.append((key, prod))

    @staticmethod
    def _bufs(vs):
        out = []
        for v in vs:
            if v is None:
                continue
            for b in v.bufs:
                if b not in out:
                    out.append(b)
        return out

    def add(self, eng, fn, reads=(), writes=(), dsem=None, multi=False):
        op = Op(eng, fn)
        op.idx = len(self.ops[eng])
        rb = self._bufs(reads)
        wb = self._bufs(writes)
        if dsem is not None:
            self.dma_counts[dsem] += 1
            op.dsem = dsem
            op.dval = 16 * self.dma_counts[dsem]
        saved = {}
        if multi:
            for b in wb:
                if not b.reads:
                    saved[b] = list(b.writes)
                    b.writes = []
        for b in rb:
            if b in wb:
                continue
            for p in b.writes:
                self._dep(op, p, same_eng_raw=True)
            if b.excl:
                for p in b.reads:
                    self._dep(op, p)
        for b in wb:
            for p in b.writes:
                self._dep(op, p, same_eng_raw=(b in rb))
            for p in b.reads:
                self._dep(op, p)
        for b in rb:
            if b not in wb:
                b.reads.append(op)
        for b in wb:
            b.writes = saved.get(b, []) + [op]
            b.reads = []
        self.ops[eng].append(op)
        return op

    def emit(self):
        nc = self.nc
        for e in ENGS:
            n = 0
            for op in self.ops[e]:
                if op.signal:
                    n += 1
                    op.sig_val = n
        dkeys = sorted(self.dma_counts.keys(), key=str)
        with contextlib.ExitStack() as st:
            esem = {e: st.enter_context(nc.semaphore('sem_' + e)) for e in ENGS}
            dsem = {k: st.enter_context(nc.semaphore('dsem_%d' % i)) for i, k in enumerate(dkeys)}
            block = st.enter_context(nc.Block())

            def body(e):
                def f(eng):
                    for op in self.ops[e]:
                        for key, val in op.waits:
                            if key[0] == 'd':
                                eng.wait_ge(dsem[key[1]], val)
                            else:
                                eng.wait_ge(esem[key[1]], val.sig_val)
                        ins = op.fn(eng)
                        if op.dsem is not None:
                            ins.then_inc(dsem[op.dsem], 16)
                        elif op.signal:
                            ins.then_inc(esem[e], 1)
                    if e == 'sp':
                        for k in dkeys:
                            eng.wait_ge(dsem[k], 16 * self.dma_counts[k])
                return f

            block.tensor(body('pe'))
            block.scalar(body('act'))
            block.vector(body('dve'))
            block.gpsimd(body('pool'))
            block.sync(body('sp'))

    def mm(self, out, lhsT, rhs, start=True, stop=True):
        def fn(eng):
            return eng.matmul(out.ap, lhsT.ap, rhs.ap, start=start, stop=stop)
        return self.add('pe', fn, reads=[lhsT, rhs], writes=[out], multi=not start)

    def transpose(self, out, in_, ident, multi=False):
        def fn(eng):
            return eng.transpose(out.ap, in_.ap, ident.ap)
        return self.add('pe', fn, reads=[in_, ident], writes=[out], multi=multi)

    def act(self, out, in_, func, bias=None, scale=None, accum=None, multi=False):
        def fn(eng):
            kw = {}
            if bias is not None:
                kw['bias'] = bias.ap if isinstance(bias, V) else bias
            if scale is not None:
                kw['scale'] = scale.ap if isinstance(scale, V) else scale
            if accum is not None:
                kw['accum_out'] = accum.ap
            return eng.activation(out.o, in_.ap, func, **kw)
        reads = [in_] + [a for a in (bias, scale) if isinstance(a, V)]
        writes = [out] + ([accum] if accum is not None else [])
        return self.add('act', fn, reads=reads, writes=writes, multi=multi)

    def tt(self, e, out, in0, in1, op, multi=False):
        def fn(eng):
            return eng.tensor_tensor(out.o, in0.ap, in1.ap, op)
        return self.add(e, fn, reads=[in0, in1], writes=[out], multi=multi)

    def ts(self, e, out, in0, s1, s2=None, op0=ALU.mult, op1=None, multi=False):
        def fn(eng):
            a1 = s1.ap if isinstance(s1, V) else s1
            a2 = s2.ap if isinstance(s2, V) else s2
            if op1 is None:
                return eng.tensor_scalar(out.o, in0.ap, a1, None, op0)
            return eng.tensor_scalar(out.o, in0.ap, a1, a2, op0, op1)
        reads = [in0] + [a for a in (s1, s2) if isinstance(a, V)]
        return self.add(e, fn, reads=reads, writes=[out], multi=multi)

    def stt(self, out, in0, scalar, in1, op0, op1, multi=False, accum=None):
        def fn(eng):
            sc = scalar.ap if isinstance(scalar, V) else scalar
            if accum is not None:
                return eng.scalar_tensor_tensor(out.o, in0.ap, sc, in1.ap, op0, op1, accum_out=accum.ap)
            return eng.scalar_tensor_tensor(out.o, in0.ap, sc, in1.ap, op0, op1)
        reads = [in0, in1] + ([scalar] if isinstance(scalar, V) else [])
        writes = [out] + ([accum] if accum is not None else [])
        return self.add('dve', fn, reads=reads, writes=writes, multi=multi)

    def scan(self, out, d0, d1, initial, multi=False):
        def fn(eng):
            ini = initial.ap if isinstance(initial, V) else initial
            return eng.tensor_tensor_scan(out.o, d0.ap, d1.ap, ini, ALU.mult, ALU.add)
        reads = [d0, d1] + ([initial] if isinstance(initial, V) else [])
        return self.add('dve', fn, reads=reads, writes=[out], multi=multi)

    def copy(self, e, out, in_, multi=False):
        if e == 'act':
            def fn(eng):
                return eng.copy(out.o, in_.ap)
        else:
            def fn(eng):
                return eng.tensor_copy(out.o, in_.ap)
        return self.add(e, fn, reads=[in_], writes=[out], multi=multi)

    def memset(self, e, out, val, multi=False):
        def fn(eng):
            return eng.memset(out.ap, val)
        return self.add(e, fn, reads=[], writes=[out], multi=multi)

    def recip(self, out, in_, multi=False):
        def fn(eng):
            return eng.reciprocal(out.o, in_.ap)
        return self.add('dve', fn, reads=[in_], writes=[out], multi=multi)

    def bn_stats(self, out, in_, multi=False):
        def fn(eng):
            return eng.bn_stats(out.ap, in_.ap)
        return self.add('dve', fn, reads=[in_], writes=[out], multi=multi)

    def bn_aggr(self, out, in_):
        def fn(eng):
            return eng.bn_aggr(out.ap, in_.ap)
        return self.add('dve', fn, reads=[in_], writes=[out])

    def dma(self, out, in_, dsem, q='sp', multi=False):
        oap = out.o if isinstance(out, V) else out
        iap = in_.ap if isinstance(in_, V) else in_

        def fn(eng):
            return eng.dma_start(out=oap, in_=iap)
        rd = [in_] if isinstance(in_, V) else []
        wr = [out] if isinstance(out, V) else []
        return self.add(q, fn, reads=rd, writes=wr, dsem=dsem, multi=multi)


class Tile:
    def __init__(self, tensor, name, nsub=0, excl=False, rr=False):
        self.t = tensor
        self.name = name
        self.rr = rr
        if nsub:
            self.bufs = [Buf('%s[%d]' % (name, i), excl) for i in range(nsub)]
        else:
            self.bufs = [Buf(name, excl)]

    def v(self):
        return V(self.bufs, self.t[:], self.rr)

    def sub(self, i):
        return V([self.bufs[i]], self.t[:, i], self.rr)

    def __getitem__(self, idx):
        return V(self.bufs, self.t[idx], self.rr)


class TPool:
    def __init__(self, tiles):
        self.free = collections.deque(tiles)

    def alloc(self):
        assert self.free, 'pool exhausted'
        return self.free.popleft()

    def release(self, *ts):
        for t in ts:
            self.free.append(t)


class TileInfo:
    def __init__(self, kind, idx, n_ptiles=NPT):
        self.kind = kind
        self.idx = idx
        if kind == 'p':
            self.T, self.NS, self.L = TP, 1, TP
            self.first = idx == 0
            self.last = idx == n_ptiles - 1
            self.NB, self.BL = 2, 64
        else:
            self.T, self.NS, self.L = NSS * LS, NSS, LS
            self.first = True
            self.last = True
            self.NB, self.BL = 4, 32
        self.G = self.T // 128


def build_program(n_layers=NL, n_ptiles=NPT, do_sample=True):
    nc = bass.Bass('TRN2', target_bir_lowering=False)

    def din(name, shape):
        return nc.dram_tensor(name, list(shape), F32, kind='ExternalInput').ap()

    def dout(name, shape):
        return nc.dram_tensor(name, list(shape), F32, kind='ExternalOutput').ap()

    x_p = din('x_p', [SEQ, D])
    x_s = din('x_s', [NSS * LS, D])
    st_a = din('st_a', [NL, NSS, 2, D])
    st_b = din('st_b', [NL, NSS, 3, D])
    st_lru = din('st_lru', [NL, NSS, D])
    st_c = din('st_c', [NL, NSS, 30, D])
    st_d = din('st_d', [NL, NSS, 3, W3])
    st_delta = din('st_delta', [NL, NSS, 8, 128, 128])
    ln_in = din('ln_in', [2, D])
    w_in = din('w_in', [NL, D, N_IN])
    par_rows = din('par_rows', [NL, NPAR, D])
    b_wx = din('b_wx', [NL, 8, 128, 128])
    b_wa = din('b_wa', [NL, 8, 128, 128])
    d_alog = din('d_alog', [NL, 8])
    d_dtb = din('d_dtb', [NL, 8])
    d_ng = din('d_ng', [NL, 128])
    w_br = din('w_br', [NL, 4, D, D])
    w_out = din('w_out', [NL, D, D])
    consts = din('consts', [10, 128, 128])

    y_p = dout('y_p', [SEQ, D])
    y_s = dout('y_s', [NSS * LS, D])
    o_a = {'p': dout('p_a', [NL, 1, 2, D]), 's': dout('s_a', [NL, NSS, 2, D])}
    o_b = {'p': dout('p_b', [NL, 1, 3, D]), 's': dout('s_b', [NL, NSS, 3, D])}
    o_lru = {'p': dout('p_lru', [NL, 1, 1, D]), 's': dout('s_lru', [NL, NSS, 1, D])}
    o_c = {'p': dout('p_c', [NL, 1, 30, D]), 's': dout('s_c', [NL, NSS, 30, D])}
    o_d = {'p': dout('p_d', [NL, 1, 3, W3]), 's': dout('s_d', [NL, NSS, 3, W3])}
    o_delta = {'p': dout('p_delta', [NL, 1, 8, 128, 128]), 's': dout('s_delta', [NL, NSS, 8, 128, 128])}

    st = contextlib.ExitStack()
    with st:
        S = Sched(nc)

        def sb(name, shape, nsub=0, rr=False):
            return Tile(st.enter_context(nc.sbuf_tensor(name, list(shape), F32)), name, nsub=nsub, rr=rr)

        X = sb('X', [128, KC, TP], nsub=KC, rr=True)
        MIX = sb('MIX', [128, KC, TP], nsub=KC, rr=True)
        BA = sb('BA', [128, KC, TP + 32], nsub=KC)
        BB = sb('BB', [128, KC, TP], nsub=KC, rr=True)
        SCR = sb('SCR', [128, 12, TP], nsub=12, rr=True)
        tmp_pool = TPool([sb('TMP%d' % i, [128, TP]) for i in range(6)])
        w_pool = TPool([sb('WP%d' % i, [128, KC, 512], rr=True) for i in range(2)])
        SST = sb('SST', [128, 4, 8, 128], nsub=4)
        CONST = sb('CONST', [128, 10, 128])
        PAR = sb('PAR', [128, NL, KC, 64])
        PLN = sb('PLN', [128, KC, 2])
        DNG = sb('DNG', [128, NL])
        DTB = sb('DTB', [128, NL, 8])
        NEGA = sb('NEGA', [128, NL, 8])
        CLAM = sb('CLAM', [128, NL, KC])
        CC = sb('CC', [128, 8])
        HA = sb('HA', [128, NL, KC, 2])
        HB = sb('HB', [128, NL, KC, 3])
        HC = sb('HC', [128, NL, KC, 30])
        HD = sb('HD', [128, NL, 24, 3])
        HL = sb('HL', [128, NL, KC])
        STT = sb('STT', [128, NSS, KC, 36])
        STD = sb('STD', [128, NSS, 24, 3])
        gr_pool = TPool([sb('GR%d' % i, [128, 128]) for i in range(4)])
        CT = [sb('CT%d' % i, [128, TP + 8], rr=True) for i in range(2)]
        GB = sb('GB', [128, 4, 16])
        GG = sb('GG', [128, 4, 8])
        BETA = sb('BETA', [128, 4, 8])
        GC = sb('GC', [128, 4, 8])
        S1 = sb('S1', [128, 4, 8])
        S2 = sb('S2', [128, 4, 8])
        S2M = sb('S2M', [128, 4, 4, 8])
        SMALL = TPool([sb('SM%d' % i, [128, 16]) for i in range(6)])
        EX = sb('EX', [128, 4, 128])
        DXL = sb('DXL', [128, 4, 128])
        DXU = sb('DXU', [128, 4, 128])
        EGR = sb('EGR', [128, 4, 128])
        GD = EGR
        ut_pool = TPool([sb('UT%d' % i, [128, 4, 128]) for i in range(4)])
        utr_pool = TPool([sb('UR%d' % i, [128, 2, 128]) for i in range(10)])
        XST = sb('XST', [128, 2, 6])
        XMV = sb('XMV', [128, 4])

        ps_pool = TPool([Tile(st.enter_context(nc.psum_tensor('PS%d' % i, [128, 512], F32)), 'PS%d' % i, excl=True)
                         for i in range(8)])

        IDENT = CONST[:, 0, :]
        ONES = CONST[:, 1, :]
        ONESM = CONST[:, 2, :]
        CMASK = {'p': dict(MUI=CONST[:, 3, :], MLS=CONST[:, 4, :], BLK=CONST[:, 5, :], BM=CONST[:, 9, 0:2]),
                 's': dict(MUI=CONST[:, 6, :], MLS=CONST[:, 7, :], BLK=CONST[:, 8, :], BM=CONST[:, 9, 2:6])}

        def par(l, c, idx):
            return PAR[:, l, c, idx:idx + 1]

        S.dma(CONST.v(), consts.rearrange('k p n -> p k n'), 'const')
        for i, val in enumerate([LN_EPS, RMS_EPS, L2_EPS, 1.0, 0.0]):
            S.memset('dve', CC[:, i:i + 1], val, multi=(i > 0))
        C_LN, C_RMS, C_L2, C_ONE, C_ZERO = (CC[:, i:i + 1] for i in range(5))

        BAF = BA.t[:, :, :].rearrange('p a b -> p (a b)')
        PSTG = V(BA.bufs, BAF[0:64, 0:1024])
        for l in range(n_layers):
            S.dma(PSTG[0:NPAR, :], par_rows[l], 'pstg')
            ps = ps_pool.alloc()
            for c in range(KC):
                S.transpose(ps[:, c * 64:c * 64 + NPAR], PSTG[0:NPAR, c * 128:(c + 1) * 128], IDENT[0:NPAR, 0:NPAR],
                            multi=(c > 0))
            S.copy('act', PAR[:, l, :, 0:NPAR], ps[:, :].rearrange('p (c k) -> p c k', c=KC)[:, :, 0:NPAR],
                   multi=(l > 0))
            ps_pool.release(ps)
        S.dma(PSTG[0:2, :], ln_in, 'pstg')
        ps = ps_pool.alloc()
        for c in range(KC):
            S.transpose(ps[:, c * 2:c * 2 + 2], PSTG[0:2, c * 128:(c + 1) * 128], IDENT[0:2, 0:2], multi=(c > 0))
        S.copy('act', PLN.v(), ps[:, 0:16].rearrange('p (c k) -> p c k', c=KC))
        ps_pool.release(ps)
        S.dma(PSTG[0:NL, 0:128], d_ng, 'pstg')
        ps = ps_pool.alloc()
        S.transpose(ps[:, 0:NL], PSTG[0:NL, 0:128], IDENT[0:NL, 0:NL])
        S.copy('act', DNG.v(), ps[:, 0:NL])
        ps_pool.release(ps)
        for l in range(NL):
            S.dma(DTB[:, l, :], d_dtb[l].partition_broadcast(128), 'dtb', multi=(l > 0))
            S.dma(NEGA[:, l, :], d_alog[l].partition_broadcast(128), 'alog', multi=(l > 0))
        S.act(NEGA.v(), NEGA.v(), AF.Exp)
        S.ts('dve', NEGA.v(), NEGA.v(), -1.0)
        for l in range(n_layers):
            S.act(CLAM[:, l, :], PAR[:, l, :, PR_LAM], AF.Exp, scale=-1.0, multi=(l > 0))
        S.act(CLAM.v(), CLAM.v(), AF.Ln, bias=C_ONE)
        S.ts('dve', CLAM.v(), CLAM.v(), -LRU_C)

        def layer_wdescs(l):
            d = []

            def win(c0, n=512):
                d.append(('m', w_in[l], c0, n))
            for br in range(4):
                if br == 0:
                    for sec in (1, 2, 3, 0):
                        win(OFF_A + sec * D), win(OFF_A + sec * D + 512)
                elif br == 1:
                    win(OFF_B), win(OFF_B + 512)
                    d.append(('g', b_wx[l]))
                    d.append(('g', b_wa[l]))
                    win(OFF_B + D), win(OFF_B + D + 512)
                elif br == 2:
                    for sec in range(3):
                        win(OFF_C + sec * D), win(OFF_C + sec * D + 512)
                else:
                    win(OFF_DA, 16)
                    for half in range(2):
                        for sec in range(3):
                            win(OFF_D + sec * D + half * 512)
                    win(OFF_DZ), win(OFF_DZ + 512)
                for blk in range(2):
                    win(OFF_G + br * D + blk * 512)
                    d.append(('m', w_br[l, br], blk * 512, 512))
            d.append(('m', w_out[l], 0, 512))
            d.append(('m', w_out[l], 512, 512))
            return d

        tiles = [TileInfo('p', i, n_ptiles) for i in range(n_ptiles)] + ([TileInfo('s', 0)] if do_sample else [])
        wsched = collections.deque()
        for tl in tiles:
            for l in range(n_layers):
                wsched.extend(layer_wdescs(l))
        winflight = collections.deque()
        wslot = {id(t): i for i, t in enumerate(w_pool.free)}

        def wissue():
            while wsched and w_pool.free:
                desc = wsched.popleft()
                t = w_pool.alloc()
                if desc[0] == 'm':
                    _, mat, c0, n = desc
                    S.dma(t[:, :, 0:n].r(), mat[:, c0:c0 + n].rearrange('(kc p) n -> p kc n', p=128),
                          ('w', wslot[id(t)]), q='pool')
                else:
                    S.dma(t[:, :, 0:128].r(), desc[1].rearrange('h i j -> i h j'), ('w', wslot[id(t)]), q='pool')
                winflight.append((desc, t))

        def wnext(kind, c0=None):
            if not winflight:
                wissue()
            desc, t = winflight.popleft()
            assert desc[0] == kind and (c0 is None or desc[2] == c0), (desc[0], kind, c0, desc[2:])
            wissue()
            return t

        def wdone(t):
            w_pool.release(t)

        def mm8(ps_v, wt, cc, rhs_tile, T, ncols=128):
            for kc in range(KC):
                S.mm(ps_v, wt[:, kc, cc * 128:cc * 128 + ncols].r(), rhs_tile.sub(kc)[:, 0:T].r(),
                     start=(kc == 0), stop=(kc == KC - 1))

        def hview(tile_v, NS, H, L):
            return tile_v[:, 0:NS * (H + L)].rearrange('p (s l) -> p s l', s=NS)

        def ps3(ps_v, NS, L):
            return ps_v[:, 0:NS * L].rearrange('p (s l) -> p s l', s=NS)

        row_ctr = [0]

        tmp_idx = {id(t): i for i, t in enumerate(tmp_pool.free)}

        def rows_finish(ps, R, n, dst_ap):
            stg = tmp_pool.alloc()
            S.copy('act', stg[0:R, 0:n], ps[0:R, 0:n])
            ps_pool.release(ps)
            S.dma(dst_ap.rearrange('s h n -> (s h) n'), stg[0:R, 0:n], ('tmp', tmp_idx[id(stg)]))
            tmp_pool.release(stg)

        def rows_src(v3, NS_, H_):
            if NS_ == 1:
                return v3[:, 0, :], None
            gt = gr_pool.alloc()
            S.copy('dve', gt[:, 0:NS_ * H_].rearrange('p (s h) -> p s h', s=NS_), v3)
            return gt[:, 0:NS_ * H_], gt

        def emit_rows(src_fn, R, chunks, dst_ap, NS_=1):
            ps = ps_pool.alloc()
            for i, c in enumerate(chunks):
                src, gt = rows_src(src_fn(c), NS_, R // NS_)
                S.transpose(ps[0:R, i * 128:(i + 1) * 128], src, IDENT, multi=(i > 0))
                if gt is not None:
                    gr_pool.release(gt)
            rows_finish(ps, R, len(chunks) * 128, dst_ap)

        def ln_feature(src, dst_fn, T, gcol, bcol, func):
            pm = ps_pool.alloc()
            pq = ps_pool.alloc()
            for c in range(KC):
                S.mm(pm[:, 0:T], ONESM, src.sub(c)[:, 0:T], start=(c == 0), stop=(c == KC - 1))
            for c in range(KC):
                sq = tmp_pool.alloc()
                S.act(sq[:, 0:T], src.sub(c)[:, 0:T], AF.Square)
                S.mm(pq[:, 0:T], ONESM, sq[:, 0:T], start=(c == 0), stop=(c == KC - 1))
                tmp_pool.release(sq)
            mean = tmp_pool.alloc()
            rstd = tmp_pool.alloc()
            S.copy('act', mean[:, 0:T], pm[:, 0:T])
            ps_pool.release(pm)
            S.tt('dve', rstd[:, 0:T], mean[:, 0:T], mean[:, 0:T], ALU.mult)
            S.tt('dve', rstd[:, 0:T], pq[:, 0:T], rstd[:, 0:T], ALU.subtract)
            ps_pool.release(pq)
            S.act(rstd[:, 0:T], rstd[:, 0:T], AF.Ln, bias=C_LN)
            S.act(rstd[:, 0:T], rstd[:, 0:T], AF.Exp, scale=-0.5)
            for c in range(KC):
                t = tmp_pool.alloc()
                S.tt('dve', t[:, 0:T], src.sub(c)[:, 0:T], mean[:, 0:T], ALU.subtract)
                S.tt('dve', t[:, 0:T], t[:, 0:T], rstd[:, 0:T], ALU.mult)
                S.act(dst_fn(c), t[:, 0:T], func, scale=gcol(c), bias=bcol(c))
                tmp_pool.release(t)
            tmp_pool.release(mean, rstd)

        def load_hist(tl, l, H, hist_tile, stt_off, bf=None):
            bf = bf or BA.sub
            NS, L = tl.NS, tl.L
            for c in range(KC):
                hv = hview(bf(c), NS, H, L)[:, :, 0:H]
                if tl.kind == 'p':
                    if tl.first:
                        S.memset('dve', hv, 0.0)
                    else:
                        S.copy('dve', hv, hist_tile[:, l, c, :].rearrange('p (s h) -> p s h', s=1))
                else:
                    S.copy('dve', hv, STT[:, :, c, stt_off:stt_off + H])

        def save_hist(tl, l, H, hist_tile, out_ap, c_list=range(KC), bf=None):
            bf = bf or BA.sub
            NS, L = tl.NS, tl.L
            if tl.kind == 'p' and not tl.last:
                for c in c_list:
                    S.copy('dve', hist_tile[:, l, c, :], hview(bf(c), NS, H, L)[:, 0, L:L + H])
            else:
                for blk in range(2):
                    chunks = list(range(blk * 4, blk * 4 + 4))
                    emit_rows(lambda c: hview(bf(c), NS, H, L)[:, :, L:L + H], NS * H, chunks,
                              out_ap[tl.kind][l, :, :, blk * 512:(blk + 1) * 512], NS_=NS)

        def merge(tl, l, br):
            T = tl.T
            for blk in range(2):
                wg = wnext('m', OFF_G + br * D + blk * 512)
                gs = []
                for cc in range(4):
                    m = blk * 4 + cc
                    ps = ps_pool.alloc()
                    mm8(ps[:, 0:T], wg, cc, X, T)
                    g = tmp_pool.alloc()
                    S.act(g[:, 0:T], ps[:, 0:T], AF.Sigmoid, bias=par(l, m, PR_BG + br))
                    ps_pool.release(ps)
                    gs.append(g)
                wdone(wg)
                wb = wnext('m', blk * 512)
                for cc in range(4):
                    m = blk * 4 + cc
                    ps = ps_pool.alloc()
                    mm8(ps[:, 0:T], wb, cc, BB, T)
                    if br == 0:
                        S.tt('dve', MIX.sub(m)[:, 0:T], gs[cc][:, 0:T], ps[:, 0:T], ALU.mult)
                    else:
                        S.tt('dve', gs[cc][:, 0:T], gs[cc][:, 0:T], ps[:, 0:T], ALU.mult)
                        o = MIX.sub(m)[:, 0:T]
                        S.tt('dve', o.r() if br == 3 else o, MIX.sub(m)[:, 0:T], gs[cc][:, 0:T], ALU.add)
                    ps_pool.release(ps)
                    tmp_pool.release(gs[cc])
                wdone(wb)

        def branch_a(tl, l):
            T, NS, L = tl.T, tl.NS, tl.L
            H = 2
            load_hist(tl, l, H, HA, 0)
            for blk in range(2):
                wt = wnext('m', OFF_A + 1 * D + blk * 512)
                for cc in range(4):
                    c = blk * 4 + cc
                    ps = ps_pool.alloc()
                    mm8(ps[:, 0:T], wt, cc, X, T)
                    S.copy('act', hview(BA.sub(c), NS, H, L)[:, :, H:H + L], ps3(ps, NS, L))
                    ps_pool.release(ps)
                wdone(wt)
            for blk in range(2):
                wt = wnext('m', OFF_A + 2 * D + blk * 512)
                for cc in range(4):
                    c = blk * 4 + cc
                    ps = ps_pool.alloc()
                    mm8(ps[:, 0:T], wt, cc, X, T)
                    hv = hview(BA.sub(c), NS, H, L)
                    S.tt('dve', hv[:, :, H:H + L], hv[:, :, H:H + L], ps3(ps, NS, L), ALU.mult)
                    ps_pool.release(ps)
                    ov = ps3(BB.sub(c), NS, L)
                    S.ts('dve', ov, hv[:, :, 0:L], par(l, c, PR_ACW + 0))
                    for k in (1, 2):
                        S.stt(ov, hv[:, :, k:k + L], par(l, c, PR_ACW + k), ov, ALU.mult, ALU.add)
                wdone(wt)
            save_hist(tl, l, H, HA, o_a)
            for blk in range(2):
                wt = wnext('m', OFF_A + 3 * D + blk * 512)
                for cc in range(4):
                    c = blk * 4 + cc
                    ps = ps_pool.alloc()
                    mm8(ps[:, 0:T], wt, cc, X, T)
                    t = tmp_pool.alloc()
                    S.act(t[:, 0:T], ps[:, 0:T], AF.Silu)
                    ps_pool.release(ps)
                    S.tt('dve', BB.sub(c)[:, 0:T], BB.sub(c)[:, 0:T], t[:, 0:T], ALU.mult)
                    tmp_pool.release(t)
                wdone(wt)
            for blk in range(2):
                wt = wnext('m', OFF_A + 0 * D + blk * 512)
                for cc in range(4):
                    c = blk * 4 + cc
                    ps = ps_pool.alloc()
                    mm8(ps[:, 0:T], wt, cc, X, T)
                    S.tt('dve', BB.sub(c)[:, 0:T].r(), BB.sub(c)[:, 0:T], ps[:, 0:T], ALU.mult)
                    ps_pool.release(ps)
                wdone(wt)
            merge(tl, l, 0)

        def branch_b(tl, l):
            T, NS, L = tl.T, tl.NS, tl.L
            H = 3
            load_hist(tl, l, H, HB, 2)
            for blk in range(2):
                wt = wnext('m', OFF_B + blk * 512)
                for cc in range(4):
                    c = blk * 4 + cc
                    ps = ps_pool.alloc()
                    mm8(ps[:, 0:T], wt, cc, X, T)
                    hv = hview(BA.sub(c), NS, H, L)
                    S.copy('act', hv[:, :, H:H + L], ps3(ps, NS, L))
                    ps_pool.release(ps)
                    ov = ps3(BB.sub(c), NS, L)
                    S.ts('dve', ov, hv[:, :, 0:L], par(l, c, PR_BCW), par(l, c, PR_BCB), ALU.mult, ALU.add)
                    for k in (1, 2, 3):
                        o2 = ov.r() if k == 3 else ov
                        S.stt(o2, hv[:, :, k:k + L], par(l, c, PR_BCW + k), ov, ALU.mult, ALU.add)
                wdone(wt)
            save_hist(tl, l, H, HB, o_b)
            wx = wnext('g')
            wa = wnext('g')
            for c0 in (0, 4):
                gas = [tmp_pool.alloc() for _ in range(4)]
                for i in range(4):
                    c = c0 + i
                    xb = BB.sub(c)[:, 0:T]
                    p1 = ps_pool.alloc()
                    p2 = ps_pool.alloc()
                    S.mm(p1[:, 0:T], wx[:, c, 0:128].r(), xb.r())
                    S.mm(p2[:, 0:T], wa[:, c, 0:128].r(), xb.r())
                    S.act(BA.sub(i)[:, 0:T], p1[:, 0:T], AF.Sigmoid, bias=par(l, c, PR_BBX))
                    S.act(gas[i][:, 0:T], p2[:, 0:T], AF.Sigmoid, bias=par(l, c, PR_BBA))
                    ps_pool.release(p1, p2)
                for i in range(4):
                    c = c0 + i
                    S.act(gas[i][:, 0:T], gas[i][:, 0:T], AF.Exp, scale=CLAM[:, l, c:c + 1])
                for i in range(4):
                    t3 = BA.sub(4 + i)[:, 0:T]
                    S.tt('dve', t3, gas[i][:, 0:T], gas[i][:, 0:T], ALU.mult)
                    S.ts('dve', t3, t3, -1.0, 1.0, ALU.mult, ALU.add)
                    S.ts('dve', t3, t3, 1e-30, None, ALU.max)
                for i in range(4):
                    t3 = BA.sub(4 + i)[:, 0:T]
                    S.act(t3, t3, AF.Ln)
                for i in range(4):
                    t3 = BA.sub(4 + i)[:, 0:T]
                    S.act(t3, t3, AF.Exp, scale=0.5)
                for i in range(4):
                    c = c0 + i
                    gx = BA.sub(i)[:, 0:T]
                    t3 = BA.sub(4 + i)[:, 0:T]
                    xb = BB.sub(c)[:, 0:T]
                    S.tt('dve', gx, gx, t3, ALU.mult)
                    S.tt('dve', gx, gx, xb, ALU.mult)
                    for s_ in range(NS):
                        if tl.kind == 'p':
                            ini = 0.0 if tl.first else HL[:, l, c:c + 1]
                        else:
                            ini = STT[:, s_, c, 5:6]
                        S.scan(BB.sub(c)[:, s_ * L:(s_ + 1) * L], gas[i][:, s_ * L:(s_ + 1) * L],
                               BA.sub(i)[:, s_ * L:(s_ + 1) * L], ini, multi=(s_ > 0))
                    if tl.kind == 'p' and not tl.last:
                        S.copy('dve', HL[:, l, c:c + 1], BB.sub(c)[:, L - 1:L])
                tmp_pool.release(*gas)
            wdone(wx)
            wdone(wa)
            if tl.last:
                for blk in range(2):
                    chunks = list(range(blk * 4, blk * 4 + 4))
                    emit_rows(lambda c: ps3(BB.sub(c), NS, L)[:, :, L - 1:L], NS, chunks,
                              o_lru[tl.kind][l, :, :, blk * 512:(blk + 1) * 512], NS_=NS)
            for blk in range(2):
                wt = wnext('m', OFF_B + D + blk * 512)
                for cc in range(4):
                    c = blk * 4 + cc
                    ps = ps_pool.alloc()
                    mm8(ps[:, 0:T], wt, cc, X, T)
                    t = tmp_pool.alloc()
                    S.act(t[:, 0:T], ps[:, 0:T], AF.Silu)
                    ps_pool.release(ps)
                    S.tt('dve', BB.sub(c)[:, 0:T].r(), BB.sub(c)[:, 0:T], t[:, 0:T], ALU.mult)
                    tmp_pool.release(t)
                wdone(wt)
            merge(tl, l, 1)

        SCRF = SCR.t[:, :, :].rearrange('p a b -> p (a b)')

        def cb(c):
            return V(SCR.bufs, SCRF[:, c * 544:(c + 1) * 544], True)

        def branch_c(tl, l):
            T, NS, L = tl.T, tl.NS, tl.L
            H = 30
            load_hist(tl, l, H, HC, 6, bf=cb)
            for blk in range(2):
                wt = wnext('m', OFF_C + blk * 512)
                for cc in range(4):
                    c = blk * 4 + cc
                    ps = ps_pool.alloc()
                    mm8(ps[:, 0:T], wt, cc, X, T)
                    S.copy('act', hview(cb(c), NS, H, L)[:, :, H:H + L], ps3(ps, NS, L))
                    ps_pool.release(ps)
                wdone(wt)
            for blk in range(2):
                wt = wnext('m', OFF_C + D + blk * 512)
                for cc in range(4):
                    c = blk * 4 + cc
                    ps = ps_pool.alloc()
                    mm8(ps[:, 0:T], wt, cc, X, T)
                    t = tmp_pool.alloc()
                    S.act(t[:, 0:T], ps[:, 0:T], AF.Sigmoid)
                    ps_pool.release(ps)
                    hv = hview(cb(c), NS, H, L)
                    S.tt('dve', hv[:, :, H:H + L], hv[:, :, H:H + L], ps3(t, NS, L), ALU.mult)
                    tmp_pool.release(t)
                wdone(wt)
            NSLOT = 12
            slots = [Buf('dg_s%d' % i) for i in range(NSLOT)]
            n = 0
            for c in range(KC):
                hv = hview(cb(c), NS, H, L)
                ps = ps_pool.alloc()
                for k in range(31):
                    i = n % NSLOT
                    first = n < NSLOT
                    n += 1
                    ap = SCRF[:, 4352 + i * 128:4352 + (i + 1) * 128]
                    wv = V([slots[i]] + (list(SCR.bufs) if first else []), ap, True)
                    rv = V([slots[i]], ap, True)
                    S.ts('dve', wv, IDENT, par(l, c, PR_CCW + k))
                    S.mm(ps3(ps, NS, L), rv.r(), hv[:, :, k:k + L].r(), start=(k == 0), stop=(k == 30))
                S.act(BB.sub(c)[:, 0:T], ps[:, 0:T], AF.Identity, bias=par(l, c, PR_CCB))
                ps_pool.release(ps)
            S.ts('dve', V(slots + list(SCR.bufs), SCRF[:, 6143:6144], True), IDENT[:, 0:1], 0.0)
            save_hist(tl, l, H, HC, o_c, bf=cb)
            ln_feature(BB, lambda c: BB.sub(c)[:, 0:T], T, lambda c: par(l, c, PR_CLG), lambda c: par(l, c, PR_CLB),
                       AF.Silu)
            for blk in range(2):
                wt = wnext('m', OFF_C + 2 * D + blk * 512)
                for cc in range(4):
                    c = blk * 4 + cc
                    ps = ps_pool.alloc()
                    mm8(ps[:, 0:T], wt, cc, X, T)
                    t = tmp_pool.alloc()
                    S.act(t[:, 0:T], ps[:, 0:T], AF.Silu)
                    ps_pool.release(ps)
                    S.tt('dve', BB.sub(c)[:, 0:T].r(), BB.sub(c)[:, 0:T], t[:, 0:T], ALU.mult)
                    tmp_pool.release(t)
                wdone(wt)
            merge(tl, l, 2)

        def gdn_pair(tl, l, g, half, jp, msk, ds):
            NB, BL = tl.NB, tl.BL
            DXL, DXU, EGR = ds['DXL'], ds['DXU'], ds['EGR']
            j0 = 2 * jp
            hh0 = half * 4 + j0
            tg = slice(g * 128, (g + 1) * 128)
            ua, ur = ut_pool.alloc, utr_pool.alloc

            def qT(j):
                return SCR.sub(j0 + j)[:, tg]

            def kT(j):
                return SCR.sub(4 + j0 + j)[:, tg]

            def vT(j):
                return SCR.sub(8 + j0 + j)[:, tg]

            def hs(j):
                return slice(j * 128, (j + 1) * 128)

            def flat(t):
                return t.v().rearrange('p a b -> p (a b)')
            IDENT4 = V(IDENT.bufs, IDENT.ap.unsqueeze(1).broadcast_to([128, 4, 128]))
            PQ = ua()
            NM = ua()

            def Pj(j):
                return PQ[:, j, :]

            def Qj(j):
                return PQ[:, 2 + j, :]

            def Nj(j):
                return NM[:, j, :]

            def Mj(j):
                return NM[:, 2 + j, :]
            psG = ps_pool.alloc()
            for j in range(2):
                S.mm(psG[:, hs(j)], kT(j), kT(j))
            for j in range(2):
                S.stt(Qj(j), psG[:, hs(j)], BETA[:, g, hh0 + j:hh0 + j + 1], DXL[:, j0 + j, :], ALU.mult, ALU.mult,
                      multi=True)
            ps_pool.release(psG)
            yield
            psB = ps_pool.alloc()
            for j in range(2):
                S.transpose(psB[:, hs(j)], Qj(j), IDENT, multi=(j > 0))
            S.copy('act', PQ[:, 0:2, :].rearrange('p a b -> p (a b)'), psB[:, 0:256], multi=True)
            ps_pool.release(psB)
            S.tt('dve', NM.v(), IDENT4, PQ.v(), ALU.subtract)
            yield
            psk = ps_pool.alloc()
            psv = ps_pool.alloc()
            for j in range(2):
                S.transpose(psk[:, hs(j)], kT(j), IDENT, multi=(j > 0))
            for j in range(2):
                S.transpose(psv[:, hs(j)], vT(j), IDENT, multi=(j > 0))
            KBG = ur()
            VB = ur()
            KTMt = ur()
            KTM = KTMt.v()
            for j in range(2):
                S.act(KBG[:, j, :], psk[:, hs(j)], AF.Identity, scale=S1[:, g, hh0 + j:hh0 + j + 1], multi=(j > 0))
            S.copy('dve', KTM.rearrange('p a b -> p (a b)'), psk[:, 0:256], multi=True)
            ps_pool.release(psk)
            for j in range(2):
                S.act(VB[:, j, :], psv[:, hs(j)], AF.Identity, scale=BETA[:, g, hh0 + j:hh0 + j + 1], multi=(j > 0))
            ps_pool.release(psv)
            yield
            psQK = ps_pool.alloc()
            for j in range(2):
                S.mm(psQK[:, hs(j)], kT(j), qT(j))
            PT = ur()
            S.tt('dve', flat(PT), psQK[:, 0:256], DXU[:, j0:j0 + 2, :].rearrange('p a b -> p (a b)'), ALU.mult)
            ps_pool.release(psQK)
            QD = ur()
            for j in range(2):
                S.tt('dve', QD[:, j, :], qT(j), EGR[:, j0 + j, :], ALU.mult, multi=(j > 0))
            yield
            NF = None
            for k in range(1, 7):
                ps1 = ps_pool.alloc() if k <= 5 else None
                ps2 = ps_pool.alloc() if k >= 2 else None
                if k <= 5:
                    for j in range(2):
                        S.mm(ps1[:, hs(j)], Qj(j), Pj(j))
                if k <= 4:
                    for j in range(2):
                        S.mm(ps1[:, hs(2 + j)], Pj(j), Qj(j))
                if k >= 2:
                    for j in range(2):
                        S.mm(ps2[:, hs(j)], Mj(j), Pj(j))
                if 2 <= k <= 5:
                    for j in range(2):
                        S.mm(ps2[:, hs(2 + j)], Pj(j), Mj(j))
                if k <= 4:
                    S.copy('act', flat(PQ), ps1[:, 0:512])
                elif k == 5:
                    S.copy('act', PQ[:, 0:2, :].rearrange('p a b -> p (a b)'), ps1[:, 0:256])
                if ps1 is not None:
                    ps_pool.release(ps1)
                if 2 <= k <= 5:
                    S.tt('dve', flat(NM), ps2[:, 0:512], flat(NM), ALU.add)
                elif k == 6:
                    NF = ur()
                    S.tt('dve', flat(NF), ps2[:, 0:256], NM[:, 0:2, :].rearrange('p a b -> p (a b)'), ALU.add)
                if ps2 is not None:
                    ps_pool.release(ps2)
                yield
            ut_pool.release(PQ, NM)
            T1 = ua()
            UB = V(T1.bufs, T1.t[:, 2:4, :])
            psW = ps_pool.alloc()
            for j in range(2):
                S.mm(psW[:, hs(j)], KBG[:, j, :], NF[:, j, :])
            WT = ur()
            S.copy('act', flat(WT), psW[:, 0:256])
            ps_pool.release(psW)
            psU = ps_pool.alloc()
            for j in range(2):
                S.mm(psU[:, hs(j)], NF[:, j, :], VB[:, j, :])
            S.copy('act', UB.rearrange('p a b -> p (a b)'), psU[:, 0:256], multi=True)
            ps_pool.release(psU)
            utr_pool.release(KBG, VB, NF)
            yield
            T2 = ua()
            O = V(T2.bufs, T2.t[:, 0:2, :])
            sq = V(T2.bufs, T2.t[:, 2:4, :])
            Of = O.rearrange('p a b -> p (a b)')
            for b in range(NB):
                si = l if tl.kind == 'p' else b

                def Sv(j):
                    return V([SST.bufs[si]], SST.t[:, si, hh0 + j, :])
                psWS = ps_pool.alloc()
                for j in range(2):
                    S.mm(psWS[:, hs(j)], WT[:, j, :], Sv(j))
                U = ur()
                S.tt('dve', flat(U), UB.rearrange('p a b -> p (a b)'), psWS[:, 0:256], ALU.subtract)
                ps_pool.release(psWS)
                KE = ur()
                for j in range(2):
                    S.ts('dve', KE[:, j, :], KTM[:, j, :], S2M[:, g, b, hh0 + j:hh0 + j + 1], multi=(j > 0))
                yield
                psO = ps_pool.alloc()
                for j in range(2):
                    S.mm(psO[:, hs(j)], QD[:, j, :], Sv(j), start=True, stop=False)
                    S.mm(psO[:, hs(j)], PT[:, j, :], U[:, j, :], start=False, stop=True)
                psS = ps_pool.alloc()
                for j in range(2):
                    S.mm(psS[:, hs(j)], KE[:, j, :], U[:, j, :])
                if b == 0:
                    S.ts('dve', Of, psO[:, 0:256], msk['BM'][:, b:b + 1])
                else:
                    S.stt(Of, psO[:, 0:256], msk['BM'][:, b:b + 1], Of, ALU.mult, ALU.add)
                ps_pool.release(psO)
                for j in range(2):
                    gend = EGR[:, j0 + j, (b + 1) * BL - 1:(b + 1) * BL]
                    S.stt(Sv(j), Sv(j), gend, psS[:, hs(j)], ALU.mult, ALU.add, multi=(j > 0))
                ps_pool.release(psS)
                utr_pool.release(U, KE)
                yield
            utr_pool.release(WT, PT, QD, KTMt)
            ut_pool.release(T1)
            sm = SMALL.alloc()
            for j in range(2):
                S.stt(sq[:, j, :], O[:, j, :], 1.0, O[:, j, :], ALU.mult, ALU.mult, accum=sm[:, j:j + 1], multi=(j > 0))
            S.act(sm[:, 0:2], sm[:, 0:2], AF.Ln, scale=1.0 / 128.0, bias=C_RMS)
            S.act(sm[:, 0:2], sm[:, 0:2], AF.Exp, scale=-0.5)
            for j in range(2):
                S.ts('dve', sq[:, j, :], O[:, j, :], sm[:, j:j + 1], multi=(j > 0))
            SMALL.release(sm)
            psT = ps_pool.alloc()
            for j in range(2):
                S.transpose(psT[:, hs(j)], sq[:, j, :], IDENT, multi=(j > 0))
            ov = V([BB.bufs[hh0], BB.bufs[hh0 + 1]], BB.t[:, hh0:hh0 + 2, tg], True)
            S.act(ov, psT[:, 0:256].rearrange('p (a b) -> p a b', a=2), AF.Identity, scale=DNG[:, l:l + 1])
            ps_pool.release(psT)
            ut_pool.release(T2)

        def run_interleaved(gens):
            gens = list(gens)
            while gens:
                for gq in list(gens):
                    try:
                        next(gq)
                    except StopIteration:
                        gens.remove(gq)

        def branch_d(tl, l):
            T, NS, L, G = tl.T, tl.NS, tl.L, tl.G
            NB, BL = tl.NB, tl.BL
            msk = CMASK[tl.kind]
            H = 3
            wt = wnext('m', OFF_DA)
            for g in range(G):
                ps = ps_pool.alloc()
                for kc in range(KC):
                    S.mm(ps[:, 0:16], X.sub(kc)[:, g * 128:(g + 1) * 128], wt[:, kc, 0:16],
                         start=(kc == 0), stop=(kc == KC - 1))
                t1 = SMALL.alloc()
                S.tt('dve', t1[:, 0:8], ps[:, 0:8], DTB[:, l, :], ALU.add)
                S.act(t1[:, 0:8], t1[:, 0:8], AF.Exp)
                S.act(t1[:, 0:8], t1[:, 0:8], AF.Ln, bias=C_ONE)
                S.tt('dve', GG[:, g, :], t1[:, 0:8], NEGA[:, l, :], ALU.mult, multi=(g > 0))
                S.act(BETA[:, g, :], ps[:, 8:16], AF.Sigmoid, multi=(g > 0))
                ps_pool.release(ps)
                SMALL.release(t1)
                p2 = ps_pool.alloc()
                S.mm(p2[:, 0:8], msk['MUI'], GG[:, g, :])
                S.mm(p2[:, 8:16], msk['BLK'], GG[:, g, :], start=True, stop=True)
                S.copy('act', GC[:, g, :], p2[:, 0:8], multi=(g > 0))
                eg = SMALL.alloc()
                S.act(eg[:, 0:8], p2[:, 0:8], AF.Exp)
                S.tt('dve', S1[:, g, :], eg[:, 0:8], BETA[:, g, :], ALU.mult, multi=(g > 0))
                S.tt('dve', eg[:, 0:8], p2[:, 8:16], GC[:, g, :], ALU.subtract)
                ps_pool.release(p2)
                S.act(S2[:, g, :], eg[:, 0:8], AF.Exp, multi=(g > 0))
                SMALL.release(eg)
                for b in range(NB):
                    S.ts('dve', S2M[:, g, b, :], S2[:, g, :], msk['BM'][:, b:b + 1], multi=(g > 0 or b > 0))
            wdone(wt)
            if tl.kind == 's':
                for s in range(NSS):
                    S.dma(SST.sub(s), st_delta[l, s].rearrange('h k v -> k h v'), ('sst', s))
            elif tl.first:
                for h in range(8):
                    S.ts('dve', V([SST.bufs[l]], SST.t[:, l, h, :]), IDENT, 0.0, multi=(h > 0))
            for half in range(2):
                dslots = [Buf('dq_s%d' % i) for i in range(4)]
                ndg = 0
                for sec in range(3):
                    wt = wnext('m', OFF_D + sec * D + half * 512)
                    emit_d = not (tl.kind == 'p' and not tl.last)
                    ps_rows = ps_pool.alloc() if emit_d else None
                    for cc in range(4):
                        qi = sec * 4 + cc
                        hc = sec * 8 + half * 4 + cc
                        ct = CT[qi % 2]
                        hv = hview(ct.v(), NS, H, L)
                        if tl.kind == 'p':
                            if tl.first:
                                S.ts('dve', hv[:, :, 0:H], IDENT[:, 0:NS * H].rearrange('p (s h) -> p s h', s=NS), 0.0)
                            else:
                                S.copy('dve', hv[:, :, 0:H], HD[:, l, hc, :].rearrange('p (s h) -> p s h', s=1))
                        else:
                            S.copy('dve', hv[:, :, 0:H], STD[:, :, hc, :])
                        ps = ps_pool.alloc()
                        mm8(ps[:, 0:T], wt, cc, X, T)
                        S.copy('act', hv[:, :, H:H + L], ps3(ps, NS, L), multi=True)
                        ps_pool.release(ps)
                        ps2 = ps_pool.alloc()
                        for k in range(4):
                            i = ndg % 4
                            first = ndg < 4
                            ndg += 1
                            ap = BB.t[:, 7, i * 128:(i + 1) * 128]
                            wv = V([dslots[i]] + ([BB.bufs[7]] if first else []), ap, True)
                            rv = V([dslots[i]], ap, True)
                            S.ts('dve', wv, IDENT, par(l, (hc % 8), PR_DCW + k * 3 + sec))
                            S.mm(ps3(ps2, NS, L), rv.r(), hv[:, :, k:k + L].r(), start=(k == 0), stop=(k == 3))
                        S.act(SCR.sub(qi)[:, 0:T], ps2[:, 0:T], AF.Silu)
                        ps_pool.release(ps2)
                        if tl.kind == 'p' and not tl.last:
                            S.copy('dve', HD[:, l, hc, :], hv[:, 0, L:L + H])
                        else:
                            src, gt = rows_src(hv[:, :, L:L + H], NS, H)
                            S.transpose(ps_rows[0:NS * H, cc * 128:(cc + 1) * 128], src, IDENT, multi=(cc > 0))
                            if gt is not None:
                                gr_pool.release(gt)
                    if emit_d:
                        c0 = sec * D + half * 512
                        rows_finish(ps_rows, NS * H, 512, o_d[tl.kind][l, :, :, c0:c0 + 512])
                    wdone(wt)
                S.ts('dve', V(dslots + [BB.bufs[7]], BB.t[:, 7, 511:512], True), IDENT[:, 0:1], 0.0)
                for q0 in (0, 4):
                    ts_ = [tmp_pool.alloc() for _ in range(4)]
                    pss = [ps_pool.alloc() for _ in range(4)]
                    for i in range(4):
                        S.act(ts_[i][:, 0:T], SCR.sub(q0 + i)[:, 0:T], AF.Square)
                    for i in range(4):
                        S.mm(pss[i][:, 0:T], ONES, ts_[i][:, 0:T])
                    for i in range(4):
                        S.act(ts_[i][:, 0:T], pss[i][:, 0:T], AF.Ln, bias=C_L2)
                        ps_pool.release(pss[i])
                    for i in range(4):
                        S.act(ts_[i][:, 0:T], ts_[i][:, 0:T], AF.Exp, scale=-0.5)
                    for i in range(4):
                        src = SCR.sub(q0 + i)[:, 0:T]
                        if q0 == 0:
                            S.stt(src, src, 128.0 ** -0.5, ts_[i][:, 0:T], ALU.mult, ALU.mult)
                        else:
                            S.tt('dve', src, src, ts_[i][:, 0:T], ALU.mult)
                        tmp_pool.release(ts_[i])
                tts = [tmp_pool.alloc() for _ in range(6)]

                def as4(t):
                    return Tile(t.t[:, :].rearrange('p (a b) -> p a b', a=4), t.name + '_v')

                def mk(t):
                    v = as4(t)
                    v.bufs = t.bufs
                    return v
                dsets = [dict(EX=EX, DXL=DXL, DXU=DXU, EGR=EGR),
                         dict(EX=mk(tts[0]), DXL=mk(tts[1]), DXU=mk(tts[2]), EGR=mk(tts[3]))]
                extra = []
                for t in tts[4:6]:
                    for hf in range(2):
                        e = Tile(t.t[:, hf * 256:(hf + 1) * 256].rearrange('p (a b) -> p a b', a=2), t.name + '_e%d' % hf)
                        e.bufs = t.bufs
                        extra.append(e)
                for e in extra:
                    utr_pool.free.append(e)

                def prep_gen(g, ds):
                    EXs, DXLs, DXUs, EGRs = ds['EX'], ds['DXL'], ds['DXU'], ds['EGR']
                    GDs = EGRs
                    for j in range(4):
                        hh = half * 4 + j
                        S.ts('dve', GDs[:, j, :], msk['MUI'], GG[:, g, hh:hh + 1], multi=(j > 0))
                    yield
                    ps = ps_pool.alloc()
                    S.mm(ps[:, :], ONES, GDs.v().rearrange('p a b -> p (a b)'))
                    for j in range(4):
                        hh = half * 4 + j
                        S.ts('dve', EXs[:, j, :], ps[:, j * 128:(j + 1) * 128], GC[:, g, hh:hh + 1], None,
                             ALU.subtract, multi=(j > 0))
                    S.act(EGRs.v().rearrange('p a b -> p (a b)'), ps[:, :], AF.Exp)
                    ps_pool.release(ps)
                    yield
                    exf = EXs.v().rearrange('p a b -> p (a b)')
                    S.stt(exf, exf, -1.0, exf, ALU.mult, ALU.max)
                    S.act(EXs.v(), EXs.v(), AF.Exp, scale=-1.0)
                    yield
                    for j in range(4):
                        S.tt('dve', DXLs[:, j, :], EXs[:, j, :], msk['MLS'], ALU.mult, multi=(j > 0))
                    yield
                    for j in range(4):
                        S.tt('dve', DXUs[:, j, :], EXs[:, j, :], msk['MUI'], ALU.mult, multi=(j > 0))
                    yield

                run_interleaved([prep_gen(0, dsets[0])])
                for g in range(G):
                    gens = [gdn_pair(tl, l, g, half, jp, msk, dsets[g % 2]) for jp in range(2)]
                    if g + 1 < G:
                        gens.append(prep_gen(g + 1, dsets[(g + 1) % 2]))
                    run_interleaved(gens)
                for e in extra:
                    utr_pool.free.remove(e)
                tmp_pool.release(*tts)
            if tl.last:
                if tl.kind == 'p':
                    S.dma(o_delta['p'][l, 0].rearrange('h k v -> k h v'), SST.sub(l), ('sst', l))
                else:
                    for s in range(NSS):
                        S.dma(o_delta['s'][l, s].rearrange('h k v -> k h v'), SST.sub(s), ('sst', s))
            for blk in range(2):
                wt = wnext('m', OFF_DZ + blk * 512)
                for cc in range(4):
                    c = blk * 4 + cc
                    ps = ps_pool.alloc()
                    mm8(ps[:, 0:T], wt, cc, X, T)
                    t = tmp_pool.alloc()
                    S.act(t[:, 0:T], ps[:, 0:T], AF.Silu)
                    ps_pool.release(ps)
                    S.tt('dve', BB.sub(c)[:, 0:T].r(), BB.sub(c)[:, 0:T], t[:, 0:T], ALU.mult)
                    tmp_pool.release(t)
                wdone(wt)
            merge(tl, l, 3)

        def out_proj(tl, l):
            T = tl.T
            for blk in range(2):
                wt = wnext('m', blk * 512)
                for cc in range(4):
                    c = blk * 4 + cc
                    ps = ps_pool.alloc()
                    mm8(ps[:, 0:T], wt, cc, MIX, T)
                    S.stt(BB.sub(c)[:, 0:T], X.sub(c)[:, 0:T], ALPHA, ps[:, 0:T], ALU.mult, ALU.add)
                    ps_pool.release(ps)
                wdone(wt)
            ln_feature(BB, lambda c: X.sub(c)[:, 0:T].r(), T, lambda c: par(l, c, PR_LNG), lambda c: par(l, c, PR_LNB),
                       AF.Identity)

        def load_sample_states(l):
            for s in range(NSS):
                for hf in range(2):
                    cs = slice(hf * 512, (hf + 1) * 512)
                    stg = tmp_pool.alloc()
                    key = ('tmp', tmp_idx[id(stg)])
                    S.dma(stg[0:2, :], st_a[l, s][:, cs], key)
                    S.dma(stg[2:5, :], st_b[l, s][:, cs], key, multi=True)
                    S.dma(stg[5:6, :], st_lru[l, s:s + 1][:, cs], key, multi=True)
                    S.dma(stg[6:36, :], st_c[l, s][:, cs], key, multi=True)
                    ps = ps_pool.alloc()
                    for cc in range(4):
                        S.transpose(ps[:, cc * 36:(cc + 1) * 36], stg[0:36, cc * 128:(cc + 1) * 128], IDENT[0:36, 0:36],
                                    multi=(cc > 0))
                    S.copy('act', STT[:, s, hf * 4:(hf + 1) * 4, :], ps[:, 0:4 * 36].rearrange('p (c k) -> p c k', c=4),
                           multi=(s > 0 or hf > 0))
                    ps_pool.release(ps)
                    tmp_pool.release(stg)
                for sec in range(3):
                    for hf in range(2):
                        stg = tmp_pool.alloc()
                        key = ('tmp', tmp_idx[id(stg)])
                        S.dma(stg[0:3, :], st_d[l, s][:, sec * D + hf * 512:sec * D + (hf + 1) * 512], key)
                        ps = ps_pool.alloc()
                        for cc in range(4):
                            S.transpose(ps[:, cc * 3:(cc + 1) * 3], stg[0:3, cc * 128:(cc + 1) * 128], IDENT[0:3, 0:3],
                                        multi=(cc > 0))
                        h0 = sec * 8 + hf * 4
                        S.copy('act', STD[:, s, h0:h0 + 4, :], ps[:, 0:12].rearrange('p (c k) -> p c k', c=4),
                               multi=(s > 0 or sec > 0 or hf > 0))
                        ps_pool.release(ps)
                        tmp_pool.release(stg)

        XS = V(BA.bufs, BAF[:, 0:4096].rearrange('p (g d) -> p g d', g=4))

        def load_tile(tl):
            G = tl.G
            if tl.kind == 'p':
                src = x_p[tl.idx * TP:(tl.idx + 1) * TP, :].rearrange('(g p) d -> p g d', p=128)
                S.dma(XS, src, 'xs')
            else:
                S.dma(XS[:, 0, :], x_s, 'xs')
            for g in range(G):
                for hf in range(2):
                    S.bn_stats(XST[:, hf, :], XS[:, g, hf * 512:(hf + 1) * 512], multi=(hf > 0))
                S.bn_aggr(XMV[:, 0:2], XST.v().rearrange('p a b -> p (a b)'))
                S.act(XMV[:, 2:3], XMV[:, 1:2], AF.Ln, bias=C_LN)
                S.act(XMV[:, 2:3], XMV[:, 2:3], AF.Exp, scale=-0.5)
                S.ts('dve', XS[:, g, :], XS[:, g, :], XMV[:, 0:1], XMV[:, 2:3], ALU.subtract, ALU.mult)
                for blk in range(2):
                    ps = ps_pool.alloc()
                    for cc in range(4):
                        c = blk * 4 + cc
                        S.transpose(ps[:, cc * 128:(cc + 1) * 128], XS[:, g, c * 128:(c + 1) * 128], IDENT,
                                    multi=(cc > 0))
                    for cc in range(4):
                        c = blk * 4 + cc
                        S.act(X.sub(c)[:, g * 128:(g + 1) * 128].r(), ps[:, cc * 128:(cc + 1) * 128], AF.Identity,
                              scale=PLN[:, c, 0:1], bias=PLN[:, c, 1:2], multi=(g > 0))
                    ps_pool.release(ps)

        def store_tile(tl):
            G = tl.G
            for g in range(G):
                for blk in range(2):
                    ps = ps_pool.alloc()
                    for cc in range(4):
                        c = blk * 4 + cc
                        S.transpose(ps[:, cc * 128:(cc + 1) * 128], X.sub(c)[:, g * 128:(g + 1) * 128], IDENT,
                                    multi=(cc > 0))
                    S.copy('act', XS[:, g, blk * 512:(blk + 1) * 512], ps[:, :], multi=(g > 0 or blk > 0))
                    ps_pool.release(ps)
            if tl.kind == 'p':
                dst = y_p[tl.idx * TP:(tl.idx + 1) * TP, :].rearrange('(g p) d -> p g d', p=128)
                S.dma(dst, XS, 'xs')
            else:
                S.dma(y_s, XS[:, 0, :], 'xs')

        for tl in tiles:
            load_tile(tl)
            for l in range(n_layers):
                if tl.kind == 's':
                    load_sample_states(l)
                branch_a(tl, l)
                branch_b(tl, l)
                branch_c(tl, l)
                branch_d(tl, l)
                out_proj(tl, l)
            store_tile(tl)
        assert not wsched and not winflight
        S.emit()
    return nc


def _consts():
    c = np.zeros((10, 128, 128), np.float32)
    idx = np.arange(128)
    c[0] = np.eye(128)
    c[1] = 1.0
    c[2] = 1.0 / D
    for base, bl in ((3, 64), (6, 32)):
        same = (idx[:, None] // bl) == (idx[None, :] // bl)
        c[base + 0] = same & (idx[None, :] >= idx[:, None])
        c[base + 1] = same & (idx[None, :] < idx[:, None])
        c[base + 2] = same
    for b in range(2):
        c[9, :, b] = (idx // 64) == b
    for b in range(4):
        c[9, :, 2 + b] = (idx // 32) == b
    return c


_NC_CACHE = {}


def kernel(x_prompt, x_sample, state_conv_a, state_conv_b, state_lru, state_conv_c, state_conv_d, state_delta,
           ln_in_g, ln_in_b, w_in, b_gate, a_conv_w, b_conv_w, b_conv_b, b_wx, b_bx, b_wa, b_ba, b_lambda,
           c_conv_w, c_conv_b, c_ln_g, c_ln_b, d_conv_w, d_a_log, d_dt_bias, d_norm_g, w_branch, w_out, ln_g, ln_b):
    f = lambda a: np.ascontiguousarray(np.asarray(a, dtype=np.float32))
    n = 8
    if 'nc' not in _NC_CACHE:
        _NC_CACHE['nc'] = build_program()
    nc = _NC_CACHE['nc']
    rows = np.concatenate([
        f(b_gate), f(a_conv_w), f(b_conv_w), f(b_conv_b)[:, None], f(b_bx).reshape(NL, 1, D),
        f(b_ba).reshape(NL, 1, D), f(b_lambda)[:, None], f(c_conv_w), f(c_conv_b)[:, None], f(c_ln_g)[:, None],
        f(c_ln_b)[:, None], f(d_conv_w).reshape(NL, 4 * 3, D), f(ln_g)[:, None], f(ln_b)[:, None]], axis=1)
    assert rows.shape == (NL, NPAR, D)
    shared = dict(
        ln_in=np.stack([f(ln_in_g), f(ln_in_b)]), w_in=f(w_in), par_rows=np.ascontiguousarray(rows),
        b_wx=f(b_wx), b_wa=f(b_wa), d_alog=f(d_a_log), d_dtb=f(d_dt_bias), d_ng=f(d_norm_g),
        w_br=f(w_branch), w_out=f(w_out), consts=_consts())
    xp, xs = f(x_prompt), f(x_sample)
    sa, sbb, sl, sc, sd, sdl = (f(a) for a in (state_conv_a, state_conv_b, state_lru, state_conv_c, state_conv_d,
                                               state_delta))
    in_maps = []
    for c in range(n):
        q = slice(c * NSS, (c + 1) * NSS)
        m = dict(shared)
        m.update(x_p=xp[c], x_s=np.ascontiguousarray(xs[q].reshape(NSS * LS, D)),
                 st_a=np.ascontiguousarray(sa[:, q]), st_b=np.ascontiguousarray(sbb[:, q]),
                 st_lru=np.ascontiguousarray(sl[:, q]), st_c=np.ascontiguousarray(sc[:, q]),
                 st_d=np.ascontiguousarray(sd[:, q]), st_delta=np.ascontiguousarray(sdl[:, q]))
        in_maps.append(m)
    res = run_bass_kernel_spmd(nc, in_maps, core_ids=list(range(n)))
    R = res.results
    cat = lambda k, ax: np.concatenate([np.asarray(r[k]) for r in R], axis=ax)
    y_prompt = np.stack([np.asarray(r['y_p']) for r in R]).astype(np.float32)
    y_sample = cat('y_s', 0).reshape(n * NSS, LS, D).astype(np.float32)
    outs = [y_prompt, y_sample]
    for pre in ('p', 's'):
        outs.append(cat(pre + '_a', 1))
        outs.append(cat(pre + '_b', 1))
        outs.append(cat(pre + '_lru', 1)[:, :, 0, :])
        outs.append(cat(pre + '_c', 1))
        outs.append(cat(pre + '_d', 1))
        outs.append(cat(pre + '_delta', 1))
    return tuple(np.ascontiguousarray(o, dtype=np.float32) for o in outs)
```

```python
import collections
import contextlib
import numpy as np
import concourse.bass as bass
import concourse.mybir as mybir
from concourse.bass_utils import run_bass_kernel_spmd

F32 = mybir.dt.float32
F32R = mybir.dt.float32r
ALU = mybir.AluOpType
AF = mybir.ActivationFunctionType

D = 1024
NL = 4
KC = 8
SEQ = 4096
TP = 512
NPT = SEQ // TP
NSS = 4
LS = 32
W3 = 3072
OFF_A, OFF_B, OFF_C, OFF_D = 0, 4096, 6144, 9216
OFF_DZ = 12288
OFF_DA = 13312
OFF_G = 13328
N_IN = 17424
ALPHA = (2 * NL) ** 0.25
LN_EPS = 1e-5
RMS_EPS = 1e-6
L2_EPS = 1e-6
LRU_C = 8.0
NPAR = 63
PR_BG, PR_ACW, PR_BCW, PR_BCB, PR_BBX, PR_BBA, PR_LAM, PR_CCW, PR_CCB, PR_CLG, PR_CLB, PR_DCW, PR_LNG, PR_LNB = \
    0, 4, 7, 11, 12, 13, 14, 15, 46, 47, 48, 49, 61, 62

ENGS = ('pe', 'act', 'dve', 'pool', 'sp')
SAME_ENG_DIST = 4


class Buf:
    __slots__ = ('name', 'excl', 'writes', 'reads')

    def __init__(self, name, excl=False):
        self.name = name
        self.excl = excl
        self.writes = []
        self.reads = []


class V:
    __slots__ = ('bufs', 'ap', 'rr')

    def __init__(self, bufs, ap, rr=False):
        self.bufs = tuple(bufs)
        self.ap = ap
        self.rr = rr

    def __getitem__(self, idx):
        return V(self.bufs, self.ap[idx], self.rr)

    def r(self):
        return V(self.bufs, self.ap.bitcast(F32R), self.rr)

    def rearrange(self, pattern, **kw):
        return V(self.bufs, self.ap.rearrange(pattern, **kw), self.rr)

    @property
    def o(self):
        return self.ap.bitcast(F32R) if self.rr else self.ap


class Op:
    __slots__ = ('eng', 'fn', 'waits', 'signal', 'idx', 'sig_val', 'dsem', 'dval')

    def __init__(self, eng, fn):
        self.eng = eng
        self.fn = fn
        self.waits = []
        self.signal = False
        self.idx = -1
        self.sig_val = None
        self.dsem = None
        self.dval = None


class Sched:
    def __init__(self, nc):
        self.nc = nc
        self.ops = {e: [] for e in ENGS}
        self.waited = {e: collections.defaultdict(int) for e in ENGS}
        self.dma_counts = collections.defaultdict(int)

    def _dep(self, op, prod, same_eng_raw=False):
        if prod is op:
            return
        if prod.dsem is not None:
            key = ('d', prod.dsem)
            if self.waited[op.eng][key] >= prod.dval:
                return
            self.waited[op.eng][key] = prod.dval
            op.waits.append((key, prod.dval))
            return
        if prod.eng == op.eng:
            if not same_eng_raw:
                return
            if op.idx - prod.idx >= SAME_ENG_DIST:
                return
        key = ('e', prod.eng)
        if self.waited[op.eng][key] >= prod.idx + 1:
            return
        self.waited[op.eng][key] = prod.idx + 1
        prod.signal = True
        op.waits.append((key, prod))

    @staticmethod
    def _bufs(vs):
        out = []
        for v in vs:
            if v is None:
                continue
            for b in v.bufs:
                if b not in out:
                    out.append(b)
        return out

    def add(self, eng, fn, reads=(), writes=(), dsem=None, multi=False):
        op = Op(eng, fn)
        op.idx = len(self.ops[eng])
        rb = self._bufs(reads)
        wb = self._bufs(writes)
        if dsem is not None:
            self.dma_counts[dsem] += 1
            op.dsem = dsem
            op.dval = 16 * self.dma_counts[dsem]
        saved = {}
        if multi:
            for b in wb:
                if not b.reads:
                    saved[b] = list(b.writes)
                    b.writes = []
        for b in rb:
            if b in wb:
                continue
            for p in b.writes:
                self._dep(op, p, same_eng_raw=True)
            if b.excl:
                for p in b.reads:
                    self._dep(op, p)
        for b in wb:
            for p in b.writes:
                self._dep(op, p, same_eng_raw=(b in rb))
            for p in b.reads:
                self._dep(op, p)
        for b in rb:
            if b not in wb:
                b.reads.append(op)
        for b in wb:
            b.writes = saved.get(b, []) + [op]
            b.reads = []
        self.ops[eng].append(op)
        return op

    def emit(self):
        nc = self.nc
        for e in ENGS:
            n = 0
            for op in self.ops[e]:
                if op.signal:
                    n += 1
                    op.sig_val = n
        dkeys = sorted(self.dma_counts.keys(), key=str)
        with contextlib.ExitStack() as st:
            esem = {e: st.enter_context(nc.semaphore('sem_' + e)) for e in ENGS}
            dsem = {k: st.enter_context(nc.semaphore('dsem_%d' % i)) for i, k in enumerate(dkeys)}
            block = st.enter_context(nc.Block())

            def body(e):
                def f(eng):
                    for op in self.ops[e]:
                        for key, val in op.waits:
                            if key[0] == 'd':
                                eng.wait_ge(dsem[key[1]], val)
                            else:
                                eng.wait_ge(esem[key[1]], val.sig_val)
                        ins = op.fn(eng)
                        if op.dsem is not None:
                            ins.then_inc(dsem[op.dsem], 16)
                        elif op.signal:
                            ins.then_inc(esem[e], 1)
                    if e == 'sp':
                        for k in dkeys:
                            eng.wait_ge(dsem[k], 16 * self.dma_counts[k])
                return f

            block.tensor(body('pe'))
            block.scalar(body('act'))
            block.vector(body('dve'))
            block.gpsimd(body('pool'))
            block.sync(body('sp'))

    def mm(self, out, lhsT, rhs, start=True, stop=True):
        def fn(eng):
            return eng.matmul(out.ap, lhsT.ap, rhs.ap, start=start, stop=stop)
        return self.add('pe', fn, reads=[lhsT, rhs], writes=[out], multi=not start)

    def transpose(self, out, in_, ident, multi=False):
        def fn(eng):
            return eng.transpose(out.ap, in_.ap, ident.ap)
        return self.add('pe', fn, reads=[in_, ident], writes=[out], multi=multi)

    def act(self, out, in_, func, bias=None, scale=None, accum=None, multi=False):
        def fn(eng):
            kw = {}
            if bias is not None:
                kw['bias'] = bias.ap if isinstance(bias, V) else bias
            if scale is not None:
                kw['scale'] = scale.ap if isinstance(scale, V) else scale
            if accum is not None:
                kw['accum_out'] = accum.ap
            return eng.activation(out.o, in_.ap, func, **kw)
        reads = [in_] + [a for a in (bias, scale) if isinstance(a, V)]
        writes = [out] + ([accum] if accum is not None else [])
        return self.add('act', fn, reads=reads, writes=writes, multi=multi)

    def tt(self, e, out, in0, in1, op, multi=False):
        def fn(eng):
            return eng.tensor_tensor(out.o, in0.ap, in1.ap, op)
        return self.add(e, fn, reads=[in0, in1], writes=[out], multi=multi)

    def ts(self, e, out, in0, s1, s2=None, op0=ALU.mult, op1=None, multi=False):
        def fn(eng):
            a1 = s1.ap if isinstance(s1, V) else s1
            a2 = s2.ap if isinstance(s2, V) else s2
            if op1 is None:
                return eng.tensor_scalar(out.o, in0.ap, a1, None, op0)
            return eng.tensor_scalar(out.o, in0.ap, a1, a2, op0, op1)
        reads = [in0] + [a for a in (s1, s2) if isinstance(a, V)]
        return self.add(e, fn, reads=reads, writes=[out], multi=multi)

    def stt(self, out, in0, scalar, in1, op0, op1, multi=False, accum=None):
        def fn(eng):
            sc = scalar.ap if isinstance(scalar, V) else scalar
            if accum is not None:
                return eng.scalar_tensor_tensor(out.o, in0.ap, sc, in1.ap, op0, op1, accum_out=accum.ap)
            return eng.scalar_tensor_tensor(out.o, in0.ap, sc, in1.ap, op0, op1)
        reads = [in0, in1] + ([scalar] if isinstance(scalar, V) else [])
        writes = [out] + ([accum] if accum is not None else [])
        return self.add('dve', fn, reads=reads, writes=writes, multi=multi)

    def scan(self, out, d0, d1, initial, multi=False):
        def fn(eng):
            ini = initial.ap if isinstance(initial, V) else initial
            return eng.tensor_tensor_scan(out.o, d0.ap, d1.ap, ini, ALU.mult, ALU.add)
        reads = [d0, d1] + ([initial] if isinstance(initial, V) else [])
        return self.add('dve', fn, reads=reads, writes=[out], multi=multi)

    def copy(self, e, out, in_, multi=False):
        if e == 'act':
            def fn(eng):
                return eng.copy(out.o, in_.ap)
        else:
            def fn(eng):
                return eng.tensor_copy(out.o, in_.ap)
        return self.add(e, fn, reads=[in_], writes=[out], multi=multi)

    def memset(self, e, out, val, multi=False):
        def fn(eng):
            return eng.memset(out.ap, val)
        return self.add(e, fn, reads=[], writes=[out], multi=multi)

    def recip(self, out, in_, multi=False):
        def fn(eng):
            return eng.reciprocal(out.o, in_.ap)
        return self.add('dve', fn, reads=[in_], writes=[out], multi=multi)

    def bn_stats(self, out, in_, multi=False):
        def fn(eng):
            return eng.bn_stats(out.ap, in_.ap)
        return self.add('dve', fn, reads=[in_], writes=[out], multi=multi)

    def bn_aggr(self, out, in_):
        def fn(eng):
            return eng.bn_aggr(out.ap, in_.ap)
        return self.add('dve', fn, reads=[in_], writes=[out])

    def dma(self, out, in_, dsem, q='sp', multi=False):
        oap = out.o if isinstance(out, V) else out
        iap = in_.ap if isinstance(in_, V) else in_

        def fn(eng):
            return eng.dma_start(out=oap, in_=iap)
        rd = [in_] if isinstance(in_, V) else []
        wr = [out] if isinstance(out, V) else []
        return self.add(q, fn, reads=rd, writes=wr, dsem=dsem, multi=multi)


class Tile:
    def __init__(self, tensor, name, nsub=0, excl=False, rr=False):
        self.t = tensor
        self.name = name
        self.rr = rr
        if nsub:
            self.bufs = [Buf('%s[%d]' % (name, i), excl) for i in range(nsub)]
        else:
            self.bufs = [Buf(name, excl)]

    def v(self):
        return V(self.bufs, self.t[:], self.rr)

    def sub(self, i):
        return V([self.bufs[i]], self.t[:, i], self.rr)

    def __getitem__(self, idx):
        return V(self.bufs, self.t[idx], self.rr)


class TPool:
    def __init__(self, tiles):
        self.free = collections.deque(tiles)

    def alloc(self):
        assert self.free, 'pool exhausted'
        return self.free.popleft()

    def release(self, *ts):
        for t in ts:
            self.free.append(t)


class TileInfo:
    def __init__(self, kind, idx, n_ptiles=NPT):
        self.kind = kind
        self.idx = idx
        if kind == 'p':
            self.T, self.NS, self.L = TP, 1, TP
            self.first = idx == 0
            self.last = idx == n_ptiles - 1
            self.NB, self.BL = 2, 64
        else:
            self.T, self.NS, self.L = NSS * LS, NSS, LS
            self.first = True
            self.last = True
            self.NB, self.BL = 4, 32
        self.G = self.T // 128


def build_program(n_layers=NL, n_ptiles=NPT, do_sample=True):
    nc = bass.Bass('TRN2', target_bir_lowering=False)

    def din(name, shape):
        return nc.dram_tensor(name, list(shape), F32, kind='ExternalInput').ap()

    def dout(name, shape):
        return nc.dram_tensor(name, list(shape), F32, kind='ExternalOutput').ap()

    x_p = din('x_p', [SEQ, D])
    x_s = din('x_s', [NSS * LS, D])
    st_a = din('st_a', [NL, NSS, 2, D])
    st_b = din('st_b', [NL, NSS, 3, D])
    st_lru = din('st_lru', [NL, NSS, D])
    st_c = din('st_c', [NL, NSS, 30, D])
    st_d = din('st_d', [NL, NSS, 3, W3])
    st_delta = din('st_delta', [NL, NSS, 8, 128, 128])
    ln_in = din('ln_in', [2, D])
    w_in = din('w_in', [NL, D, N_IN])
    par_rows = din('par_rows', [NL, NPAR, D])
    b_wx = din('b_wx', [NL, 8, 128, 128])
    b_wa = din('b_wa', [NL, 8, 128, 128])
    d_alog = din('d_alog', [NL, 8])
    d_dtb = din('d_dtb', [NL, 8])
    d_ng = din('d_ng', [NL, 128])
    w_br = din('w_br', [NL, 4, D, D])
    w_out = din('w_out', [NL, D, D])
    consts = din('consts', [10, 128, 128])

    y_p = dout('y_p', [SEQ, D])
    y_s = dout('y_s', [NSS * LS, D])
    o_a = {'p': dout('p_a', [NL, 1, 2, D]), 's': dout('s_a', [NL, NSS, 2, D])}
    o_b = {'p': dout('p_b', [NL, 1, 3, D]), 's': dout('s_b', [NL, NSS, 3, D])}
    o_lru = {'p': dout('p_lru', [NL, 1, 1, D]), 's': dout('s_lru', [NL, NSS, 1, D])}
    o_c = {'p': dout('p_c', [NL, 1, 30, D]), 's': dout('s_c', [NL, NSS, 30, D])}
    o_d = {'p': dout('p_d', [NL, 1, 3, W3]), 's': dout('s_d', [NL, NSS, 3, W3])}
    o_delta = {'p': dout('p_delta', [NL, 1, 8, 128, 128]), 's': dout('s_delta', [NL, NSS, 8, 128, 128])}

    st = contextlib.ExitStack()
    with st:
        S = Sched(nc)

        def sb(name, shape, nsub=0, rr=False):
            return Tile(st.enter_context(nc.sbuf_tensor(name, list(shape), F32)), name, nsub=nsub, rr=rr)

        X = sb('X', [128, KC, TP], nsub=KC, rr=True)
        MIX = sb('MIX', [128, KC, TP], nsub=KC, rr=True)
        BA = sb('BA', [128, KC, TP + 32], nsub=KC)
        BB = sb('BB', [128, KC, TP], nsub=KC, rr=True)
        SCR = sb('SCR', [128, 12, TP], nsub=12, rr=True)
        tmp_pool = TPool([sb('TMP%d' % i, [128, TP]) for i in range(6)])
        w_pool = TPool([sb('WP%d' % i, [128, KC, 256], rr=True) for i in range(4)])
        SST = sb('SST', [128, 4, 8, 128], nsub=4)
        CONST = sb('CONST', [128, 10, 128])
        PAR = sb('PAR', [128, NL, KC, 64])
        PLN = sb('PLN', [128, KC, 2])
        DNG = sb('DNG', [128, NL])
        DTB = sb('DTB', [128, NL, 8])
        NEGA = sb('NEGA', [128, NL, 8])
        CLAM = sb('CLAM', [128, NL, KC])
        CC = sb('CC', [128, 8])
        HA = sb('HA', [128, NL, KC, 2])
        HB = sb('HB', [128, NL, KC, 3])
        HC = sb('HC', [128, NL, KC, 30])
        HD = sb('HD', [128, NL, 24, 3])
        HL = sb('HL', [128, NL, KC])
        STT = sb('STT', [128, NSS, KC, 36])
        STD = sb('STD', [128, NSS, 24, 3])
        gr_pool = TPool([sb('GR%d' % i, [128, 128]) for i in range(4)])
        CT = [sb('CT%d' % i, [128, TP + 8], rr=True) for i in range(2)]
        GB = sb('GB', [128, 4, 16])
        GG = sb('GG', [128, 4, 8])
        BETA = sb('BETA', [128, 4, 8])
        GC = sb('GC', [128, 4, 8])
        S1 = sb('S1', [128, 4, 8])
        S2 = sb('S2', [128, 4, 8])
        S2M = sb('S2M', [128, 4, 4, 8])
        SMALL = TPool([sb('SM%d' % i, [128, 16]) for i in range(6)])
        EX = sb('EX', [128, 4, 128])
        DXL = sb('DXL', [128, 4, 128])
        DXU = sb('DXU', [128, 4, 128])
        EGR = sb('EGR', [128, 4, 128])
        GD = EGR
        ut_pool = TPool([sb('UT%d' % i, [128, 4, 128]) for i in range(4)])
        utr_pool = TPool([sb('UR%d' % i, [128, 2, 128]) for i in range(10)])
        XST = sb('XST', [128, 2, 6])
        XMV = sb('XMV', [128, 4])

        ps_pool = TPool([Tile(st.enter_context(nc.psum_tensor('PS%d' % i, [128, 512], F32)), 'PS%d' % i, excl=True)
                         for i in range(8)])

        IDENT = CONST[:, 0, :]
        ONES = CONST[:, 1, :]
        ONESM = CONST[:, 2, :]
        CMASK = {'p': dict(MUI=CONST[:, 3, :], MLS=CONST[:, 4, :], BLK=CONST[:, 5, :], BM=CONST[:, 9, 0:2]),
                 's': dict(MUI=CONST[:, 6, :], MLS=CONST[:, 7, :], BLK=CONST[:, 8, :], BM=CONST[:, 9, 2:6])}

        def par(l, c, idx):
            return PAR[:, l, c, idx:idx + 1]

        S.dma(CONST.v(), consts.rearrange('k p n -> p k n'), 'const')
        for i, val in enumerate([LN_EPS, RMS_EPS, L2_EPS, 1.0, 0.0]):
            S.memset('dve', CC[:, i:i + 1], val, multi=(i > 0))
        C_LN, C_RMS, C_L2, C_ONE, C_ZERO = (CC[:, i:i + 1] for i in range(5))

        BAF = BA.t[:, :, :].rearrange('p a b -> p (a b)')
        PSTG = V(BA.bufs, BAF[0:64, 0:1024])
        for l in range(n_layers):
            S.dma(PSTG[0:NPAR, :], par_rows[l], 'pstg')
            ps = ps_pool.alloc()
            for c in range(KC):
                S.transpose(ps[:, c * 64:c * 64 + NPAR], PSTG[0:NPAR, c * 128:(c + 1) * 128], IDENT[0:NPAR, 0:NPAR],
                            multi=(c > 0))
            S.copy('act', PAR[:, l, :, 0:NPAR], ps[:, :].rearrange('p (c k) -> p c k', c=KC)[:, :, 0:NPAR],
                   multi=(l > 0))
            ps_pool.release(ps)
        S.dma(PSTG[0:2, :], ln_in, 'pstg')
        ps = ps_pool.alloc()
        for c in range(KC):
            S.transpose(ps[:, c * 2:c * 2 + 2], PSTG[0:2, c * 128:(c + 1) * 128], IDENT[0:2, 0:2], multi=(c > 0))
        S.copy('act', PLN.v(), ps[:, 0:16].rearrange('p (c k) -> p c k', c=KC))
        ps_pool.release(ps)
        S.dma(PSTG[0:NL, 0:128], d_ng, 'pstg')
        ps = ps_pool.alloc()
        S.transpose(ps[:, 0:NL], PSTG[0:NL, 0:128], IDENT[0:NL, 0:NL])
        S.copy('act', DNG.v(), ps[:, 0:NL])
        ps_pool.release(ps)
        for l in range(NL):
            S.dma(DTB[:, l, :], d_dtb[l].partition_broadcast(128), 'dtb', multi=(l > 0))
            S.dma(NEGA[:, l, :], d_alog[l].partition_broadcast(128), 'alog', multi=(l > 0))
        S.act(NEGA.v(), NEGA.v(), AF.Exp)
        S.ts('dve', NEGA.v(), NEGA.v(), -1.0)
        for l in range(n_layers):
            S.act(CLAM[:, l, :], PAR[:, l, :, PR_LAM], AF.Exp, scale=-1.0, multi=(l > 0))
        S.act(CLAM.v(), CLAM.v(), AF.Ln, bias=C_ONE)
        S.ts('dve', CLAM.v(), CLAM.v(), -LRU_C)

        def layer_wdescs(l):
            d = []

            def win(c0, n=256):
                d.append(('m', w_in[l], c0, n))

            def win4(c0):
                for q in range(4):
                    win(c0 + q * 256)
            for br in range(4):
                if br == 0:
                    for sec in (1, 2, 3, 0):
                        win4(OFF_A + sec * D)
                elif br == 1:
                    win4(OFF_B)
                    d.append(('g', b_wx[l]))
                    d.append(('g', b_wa[l]))
                    win4(OFF_B + D)
                elif br == 2:
                    for sec in range(3):
                        win4(OFF_C + sec * D)
                else:
                    win(OFF_DA, 16)
                    for half in range(2):
                        for sec in range(3):
                            win(OFF_D + sec * D + half * 512)
                            win(OFF_D + sec * D + half * 512 + 256)
                    win4(OFF_DZ)
                for blk in range(4):
                    win(OFF_G + br * D + blk * 256)
                    d.append(('m', w_br[l, br], blk * 256, 256))
            for blk in range(4):
                d.append(('m', w_out[l], blk * 256, 256))
            return d

        tiles = [TileInfo('p', i, n_ptiles) for i in range(n_ptiles)] + ([TileInfo('s', 0)] if do_sample else [])
        wsched = collections.deque()
        for tl in tiles:
            for l in range(n_layers):
                wsched.extend(layer_wdescs(l))
        winflight = collections.deque()
        wslot = {id(t): i for i, t in enumerate(w_pool.free)}

        def wissue():
            while wsched and w_pool.free:
                desc = wsched.popleft()
                t = w_pool.alloc()
                if desc[0] == 'm':
                    _, mat, c0, n = desc
                    S.dma(t[:, :, 0:n].r(), mat[:, c0:c0 + n].rearrange('(kc p) n -> p kc n', p=128),
                          ('w', wslot[id(t)]), q='pool')
                else:
                    S.dma(t[:, :, 0:128].r(), desc[1].rearrange('h i j -> i h j'), ('w', wslot[id(t)]), q='pool')
                winflight.append((desc, t))

        def wnext(kind, c0=None):
            if not winflight:
                wissue()
            desc, t = winflight.popleft()
            assert desc[0] == kind and (c0 is None or desc[2] == c0), (desc[0], kind, c0, desc[2:])
            wissue()
            return t

        def wdone(t):
            w_pool.release(t)

        def mm8(ps_v, wt, cc, rhs_tile, T, ncols=128):
            for kc in range(KC):
                S.mm(ps_v, wt[:, kc, cc * 128:cc * 128 + ncols].r(), rhs_tile.sub(kc)[:, 0:T].r(),
                     start=(kc == 0), stop=(kc == KC - 1))

        def hview(tile_v, NS, H, L):
            return tile_v[:, 0:NS * (H + L)].rearrange('p (s l) -> p s l', s=NS)

        def ps3(ps_v, NS, L):
            return ps_v[:, 0:NS * L].rearrange('p (s l) -> p s l', s=NS)

        row_ctr = [0]

        tmp_idx = {id(t): i for i, t in enumerate(tmp_pool.free)}

        def rows_finish(ps, R, n, dst_ap):
            stg = tmp_pool.alloc()
            S.copy('act', stg[0:R, 0:n], ps[0:R, 0:n])
            ps_pool.release(ps)
            S.dma(dst_ap.rearrange('s h n -> (s h) n'), stg[0:R, 0:n], ('tmp', tmp_idx[id(stg)]))
            tmp_pool.release(stg)

        def rows_src(v3, NS_, H_):
            if NS_ == 1:
                return v3[:, 0, :], None
            gt = gr_pool.alloc()
            S.copy('dve', gt[:, 0:NS_ * H_].rearrange('p (s h) -> p s h', s=NS_), v3)
            return gt[:, 0:NS_ * H_], gt

        def emit_rows(src_fn, R, chunks, dst_ap, NS_=1):
            ps = ps_pool.alloc()
            for i, c in enumerate(chunks):
                src, gt = rows_src(src_fn(c), NS_, R // NS_)
                S.transpose(ps[0:R, i * 128:(i + 1) * 128], src, IDENT, multi=(i > 0))
                if gt is not None:
                    gr_pool.release(gt)
            rows_finish(ps, R, len(chunks) * 128, dst_ap)

        def ln_feature(src, dst_fn, T, gcol, bcol, func):
            pm = ps_pool.alloc()
            pq = ps_pool.alloc()
            for c in range(KC):
                S.mm(pm[:, 0:T], ONESM, src.sub(c)[:, 0:T], start=(c == 0), stop=(c == KC - 1))
            for c in range(KC):
                sq = tmp_pool.alloc()
                S.act(sq[:, 0:T], src.sub(c)[:, 0:T], AF.Square)
                S.mm(pq[:, 0:T], ONESM, sq[:, 0:T], start=(c == 0), stop=(c == KC - 1))
                tmp_pool.release(sq)
            mean = tmp_pool.alloc()
            rstd = tmp_pool.alloc()
            S.copy('act', mean[:, 0:T], pm[:, 0:T])
            ps_pool.release(pm)
            S.tt('dve', rstd[:, 0:T], mean[:, 0:T], mean[:, 0:T], ALU.mult)
            S.tt('dve', rstd[:, 0:T], pq[:, 0:T], rstd[:, 0:T], ALU.subtract)
            ps_pool.release(pq)
            S.act(rstd[:, 0:T], rstd[:, 0:T], AF.Ln, bias=C_LN)
            S.act(rstd[:, 0:T], rstd[:, 0:T], AF.Exp, scale=-0.5)
            for c in range(KC):
                t = tmp_pool.alloc()
                S.tt('dve', t[:, 0:T], src.sub(c)[:, 0:T], mean[:, 0:T], ALU.subtract)
                S.tt('dve', t[:, 0:T], t[:, 0:T], rstd[:, 0:T], ALU.mult)
                S.act(dst_fn(c), t[:, 0:T], func, scale=gcol(c), bias=bcol(c))
                tmp_pool.release(t)
            tmp_pool.release(mean, rstd)

        def load_hist(tl, l, H, hist_tile, stt_off, bf=None):
            bf = bf or BA.sub
            NS, L = tl.NS, tl.L
            for c in range(KC):
                hv = hview(bf(c), NS, H, L)[:, :, 0:H]
                if tl.kind == 'p':
                    if tl.first:
                        S.memset('dve', hv, 0.0)
                    else:
                        S.copy('dve', hv, hist_tile[:, l, c, :].rearrange('p (s h) -> p s h', s=1))
                else:
                    S.copy('dve', hv, STT[:, :, c, stt_off:stt_off + H])

        def save_hist(tl, l, H, hist_tile, out_ap, c_list=range(KC), bf=None):
            bf = bf or BA.sub
            NS, L = tl.NS, tl.L
            if tl.kind == 'p' and not tl.last:
                for c in c_list:
                    S.copy('dve', hist_tile[:, l, c, :], hview(bf(c), NS, H, L)[:, 0, L:L + H])
            else:
                for blk in range(2):
                    chunks = list(range(blk * 4, blk * 4 + 4))
                    emit_rows(lambda c: hview(bf(c), NS, H, L)[:, :, L:L + H], NS * H, chunks,
                              out_ap[tl.kind][l, :, :, blk * 512:(blk + 1) * 512], NS_=NS)

        def merge(tl, l, br):
            T = tl.T
            for blk in range(4):
                wg = wnext('m', OFF_G + br * D + blk * 256)
                gs = []
                for cc in range(2):
                    m = blk * 2 + cc
                    ps = ps_pool.alloc()
                    mm8(ps[:, 0:T], wg, cc, X, T)
                    g = tmp_pool.alloc()
                    S.act(g[:, 0:T], ps[:, 0:T], AF.Sigmoid, bias=par(l, m, PR_BG + br))
                    ps_pool.release(ps)
                    gs.append(g)
                wdone(wg)
                wb = wnext('m', blk * 256)
                for cc in range(2):
                    m = blk * 2 + cc
                    ps = ps_pool.alloc()
                    mm8(ps[:, 0:T], wb, cc, BB, T)
                    if br == 0:
                        S.tt('dve', MIX.sub(m)[:, 0:T], gs[cc][:, 0:T], ps[:, 0:T], ALU.mult)
                    else:
                        S.tt('dve', gs[cc][:, 0:T], gs[cc][:, 0:T], ps[:, 0:T], ALU.mult)
                        o = MIX.sub(m)[:, 0:T]
                        S.tt('dve', o.r() if br == 3 else o, MIX.sub(m)[:, 0:T], gs[cc][:, 0:T], ALU.add)
                    ps_pool.release(ps)
                    tmp_pool.release(gs[cc])
                wdone(wb)

        def branch_a(tl, l):
            T, NS, L = tl.T, tl.NS, tl.L
            H = 2
            load_hist(tl, l, H, HA, 0)
            for blk in range(4):
                wt = wnext('m', OFF_A + 1 * D + blk * 256)
                for cc in range(2):
                    c = blk * 2 + cc
                    ps = ps_pool.alloc()
                    mm8(ps[:, 0:T], wt, cc, X, T)
                    S.copy('act', hview(BA.sub(c), NS, H, L)[:, :, H:H + L], ps3(ps, NS, L))
                    ps_pool.release(ps)
                wdone(wt)
            for blk in range(4):
                wt = wnext('m', OFF_A + 2 * D + blk * 256)
                for cc in range(2):
                    c = blk * 2 + cc
                    ps = ps_pool.alloc()
                    mm8(ps[:, 0:T], wt, cc, X, T)
                    hv = hview(BA.sub(c), NS, H, L)
                    S.tt('dve', hv[:, :, H:H + L], hv[:, :, H:H + L], ps3(ps, NS, L), ALU.mult)
                    ps_pool.release(ps)
                    ov = ps3(BB.sub(c), NS, L)
                    S.ts('dve', ov, hv[:, :, 0:L], par(l, c, PR_ACW + 0))
                    for k in (1, 2):
                        S.stt(ov, hv[:, :, k:k + L], par(l, c, PR_ACW + k), ov, ALU.mult, ALU.add)
                wdone(wt)
            save_hist(tl, l, H, HA, o_a)
            for blk in range(4):
                wt = wnext('m', OFF_A + 3 * D + blk * 256)
                for cc in range(2):
                    c = blk * 2 + cc
                    ps = ps_pool.alloc()
                    mm8(ps[:, 0:T], wt, cc, X, T)
                    t = tmp_pool.alloc()
                    S.act(t[:, 0:T], ps[:, 0:T], AF.Silu)
                    ps_pool.release(ps)
                    S.tt('dve', BB.sub(c)[:, 0:T], BB.sub(c)[:, 0:T], t[:, 0:T], ALU.mult)
                    tmp_pool.release(t)
                wdone(wt)
            for blk in range(4):
                wt = wnext('m', OFF_A + 0 * D + blk * 256)
                for cc in range(2):
                    c = blk * 2 + cc
                    ps = ps_pool.alloc()
                    mm8(ps[:, 0:T], wt, cc, X, T)
                    S.tt('dve', BB.sub(c)[:, 0:T].r(), BB.sub(c)[:, 0:T], ps[:, 0:T], ALU.mult)
                    ps_pool.release(ps)
                wdone(wt)
            merge(tl, l, 0)

        def branch_b(tl, l):
            T, NS, L = tl.T, tl.NS, tl.L
            H = 3
            load_hist(tl, l, H, HB, 2)
            for blk in range(4):
                wt = wnext('m', OFF_B + blk * 256)
                for cc in range(2):
                    c = blk * 2 + cc
                    ps = ps_pool.alloc()
                    mm8(ps[:, 0:T], wt, cc, X, T)
                    hv = hview(BA.sub(c), NS, H, L)
                    S.copy('act', hv[:, :, H:H + L], ps3(ps, NS, L))
                    ps_pool.release(ps)
                    ov = ps3(BB.sub(c), NS, L)
                    S.ts('dve', ov, hv[:, :, 0:L], par(l, c, PR_BCW), par(l, c, PR_BCB), ALU.mult, ALU.add)
                    for k in (1, 2, 3):
                        o2 = ov.r() if k == 3 else ov
                        S.stt(o2, hv[:, :, k:k + L], par(l, c, PR_BCW + k), ov, ALU.mult, ALU.add)
                wdone(wt)
            save_hist(tl, l, H, HB, o_b)
            wx = wnext('g')
            wa = wnext('g')
            for c0 in (0, 4):
                gas = [tmp_pool.alloc() for _ in range(4)]
                for i in range(4):
                    c = c0 + i
                    xb = BB.sub(c)[:, 0:T]
                    p1 = ps_pool.alloc()
                    p2 = ps_pool.alloc()
                    S.mm(p1[:, 0:T], wx[:, c, 0:128].r(), xb.r())
                    S.mm(p2[:, 0:T], wa[:, c, 0:128].r(), xb.r())
                    S.act(BA.sub(i)[:, 0:T], p1[:, 0:T], AF.Sigmoid, bias=par(l, c, PR_BBX))
                    S.act(gas[i][:, 0:T], p2[:, 0:T], AF.Sigmoid, bias=par(l, c, PR_BBA))
                    ps_pool.release(p1, p2)
                for i in range(4):
                    c = c0 + i
                    S.act(gas[i][:, 0:T], gas[i][:, 0:T], AF.Exp, scale=CLAM[:, l, c:c + 1])
                for i in range(4):
                    t3 = BA.sub(4 + i)[:, 0:T]
                    S.tt('dve', t3, gas[i][:, 0:T], gas[i][:, 0:T], ALU.mult)
                    S.ts('dve', t3, t3, -1.0, 1.0, ALU.mult, ALU.add)
                    S.ts('dve', t3, t3, 1e-18, None, ALU.max)
                for i in range(4):
                    t3 = BA.sub(4 + i)[:, 0:T]
                    S.act(t3, t3, AF.Ln)
                for i in range(4):
                    t3 = BA.sub(4 + i)[:, 0:T]
                    S.act(t3, t3, AF.Exp, scale=0.5)
                for i in range(4):
                    c = c0 + i
                    gx = BA.sub(i)[:, 0:T]
                    t3 = BA.sub(4 + i)[:, 0:T]
                    xb = BB.sub(c)[:, 0:T]
                    S.tt('dve', gx, gx, t3, ALU.mult)
                    S.tt('dve', gx, gx, xb, ALU.mult)
                    for s_ in range(NS):
                        if tl.kind == 'p':
                            ini = 0.0 if tl.first else HL[:, l, c:c + 1]
                        else:
                            ini = STT[:, s_, c, 5:6]
                        S.scan(BB.sub(c)[:, s_ * L:(s_ + 1) * L], gas[i][:, s_ * L:(s_ + 1) * L],
                               BA.sub(i)[:, s_ * L:(s_ + 1) * L], ini, multi=(s_ > 0))
                    if tl.kind == 'p' and not tl.last:
                        S.copy('dve', HL[:, l, c:c + 1], BB.sub(c)[:, L - 1:L])
                tmp_pool.release(*gas)
            wdone(wx)
            wdone(wa)
            if tl.last:
                for blk in range(2):
                    chunks = list(range(blk * 4, blk * 4 + 4))
                    emit_rows(lambda c: ps3(BB.sub(c), NS, L)[:, :, L - 1:L], NS, chunks,
                              o_lru[tl.kind][l, :, :, blk * 512:(blk + 1) * 512], NS_=NS)
            for blk in range(4):
                wt = wnext('m', OFF_B + D + blk * 256)
                for cc in range(2):
                    c = blk * 2 + cc
                    ps = ps_pool.alloc()
                    mm8(ps[:, 0:T], wt, cc, X, T)
                    t = tmp_pool.alloc()
                    S.act(t[:, 0:T], ps[:, 0:T], AF.Silu)
                    ps_pool.release(ps)
                    S.tt('dve', BB.sub(c)[:, 0:T].r(), BB.sub(c)[:, 0:T], t[:, 0:T], ALU.mult)
                    tmp_pool.release(t)
                wdone(wt)
            merge(tl, l, 1)

        SCRF = SCR.t[:, :, :].rearrange('p a b -> p (a b)')

        def cb(c):
            return V(SCR.bufs, SCRF[:, c * 544:(c + 1) * 544], True)

        def branch_c(tl, l):
            T, NS, L = tl.T, tl.NS, tl.L
            H = 30
            load_hist(tl, l, H, HC, 6, bf=cb)
            for blk in range(4):
                wt = wnext('m', OFF_C + blk * 256)
                for cc in range(2):
                    c = blk * 2 + cc
                    ps = ps_pool.alloc()
                    mm8(ps[:, 0:T], wt, cc, X, T)
                    S.copy('act', hview(cb(c), NS, H, L)[:, :, H:H + L], ps3(ps, NS, L))
                    ps_pool.release(ps)
                wdone(wt)
            for blk in range(4):
                wt = wnext('m', OFF_C + D + blk * 256)
                for cc in range(2):
                    c = blk * 2 + cc
                    ps = ps_pool.alloc()
                    mm8(ps[:, 0:T], wt, cc, X, T)
                    t = tmp_pool.alloc()
                    S.act(t[:, 0:T], ps[:, 0:T], AF.Sigmoid)
                    ps_pool.release(ps)
                    hv = hview(cb(c), NS, H, L)
                    S.tt('dve', hv[:, :, H:H + L], hv[:, :, H:H + L], ps3(t, NS, L), ALU.mult)
                    tmp_pool.release(t)
                wdone(wt)
            NSLOT = 12
            slots = [Buf('dg_s%d' % i) for i in range(NSLOT)]
            n = 0
            for c in range(KC):
                hv = hview(cb(c), NS, H, L)
                ps = ps_pool.alloc()
                for k in range(31):
                    i = n % NSLOT
                    first = n < NSLOT
                    n += 1
                    ap = SCRF[:, 4352 + i * 128:4352 + (i + 1) * 128]
                    wv = V([slots[i]] + (list(SCR.bufs) if first else []), ap, True)
                    rv = V([slots[i]], ap, True)
                    S.ts('dve', wv, IDENT, par(l, c, PR_CCW + k))
                    S.mm(ps3(ps, NS, L), rv.r(), hv[:, :, k:k + L].r(), start=(k == 0), stop=(k == 30))
                S.act(BB.sub(c)[:, 0:T], ps[:, 0:T], AF.Identity, bias=par(l, c, PR_CCB))
                ps_pool.release(ps)
            S.ts('dve', V(slots + list(SCR.bufs), SCRF[:, 6143:6144], True), IDENT[:, 0:1], 0.0)
            save_hist(tl, l, H, HC, o_c, bf=cb)
            ln_feature(BB, lambda c: BB.sub(c)[:, 0:T], T, lambda c: par(l, c, PR_CLG), lambda c: par(l, c, PR_CLB),
                       AF.Silu)
            for blk in range(4):
                wt = wnext('m', OFF_C + 2 * D + blk * 256)
                for cc in range(2):
                    c = blk * 2 + cc
                    ps = ps_pool.alloc()
                    mm8(ps[:, 0:T], wt, cc, X, T)
                    t = tmp_pool.alloc()
                    S.act(t[:, 0:T], ps[:, 0:T], AF.Silu)
                    ps_pool.release(ps)
                    S.tt('dve', BB.sub(c)[:, 0:T].r(), BB.sub(c)[:, 0:T], t[:, 0:T], ALU.mult)
                    tmp_pool.release(t)
                wdone(wt)
            merge(tl, l, 2)

        def gdn_pair(tl, l, g, half, jp, msk, ds):
            NB, BL = tl.NB, tl.BL
            DXL, DXU, EGR = ds['DXL'], ds['DXU'], ds['EGR']
            j0 = 2 * jp
            hh0 = half * 4 + j0
            tg = slice(g * 128, (g + 1) * 128)
            ua, ur = ut_pool.alloc, utr_pool.alloc

            def qT(j):
                return SCR.sub(j0 + j)[:, tg]

            def kT(j):
                return SCR.sub(4 + j0 + j)[:, tg]

            def vT(j):
                return SCR.sub(8 + j0 + j)[:, tg]

            def hs(j):
                return slice(j * 128, (j + 1) * 128)

            def flat(t):
                return t.v().rearrange('p a b -> p (a b)')
            IDENT4 = V(IDENT.bufs, IDENT.ap.unsqueeze(1).broadcast_to([128, 4, 128]))
            PQ = ua()
            NM = ua()

            def Pj(j):
                return PQ[:, j, :]

            def Qj(j):
                return PQ[:, 2 + j, :]

            def Nj(j):
                return NM[:, j, :]

            def Mj(j):
                return NM[:, 2 + j, :]
            psG = ps_pool.alloc()
            for j in range(2):
                S.mm(psG[:, hs(j)], kT(j), kT(j))
            for j in range(2):
                S.stt(Qj(j), psG[:, hs(j)], BETA[:, g, hh0 + j:hh0 + j + 1], DXL[:, j0 + j, :], ALU.mult, ALU.mult,
                      multi=True)
            ps_pool.release(psG)
            yield
            psB = ps_pool.alloc()
            for j in range(2):
                S.transpose(psB[:, hs(j)], Qj(j), IDENT, multi=(j > 0))
            S.copy('act', PQ[:, 0:2, :].rearrange('p a b -> p (a b)'), psB[:, 0:256], multi=True)
            ps_pool.release(psB)
            S.tt('dve', NM.v(), IDENT4, PQ.v(), ALU.subtract)
            yield
            psk = ps_pool.alloc()
            psv = ps_pool.alloc()
            for j in range(2):
                S.transpose(psk[:, hs(j)], kT(j), IDENT, multi=(j > 0))
            for j in range(2):
                S.transpose(psv[:, hs(j)], vT(j), IDENT, multi=(j > 0))
            KBG = ur()
            VB = ur()
            KTMt = ur()
            KTM = KTMt.v()
            for j in range(2):
                S.act(KBG[:, j, :], psk[:, hs(j)], AF.Identity, scale=S1[:, g, hh0 + j:hh0 + j + 1], multi=(j > 0))
            S.copy('dve', KTM.rearrange('p a b -> p (a b)'), psk[:, 0:256], multi=True)
            ps_pool.release(psk)
            for j in range(2):
                S.act(VB[:, j, :], psv[:, hs(j)], AF.Identity, scale=BETA[:, g, hh0 + j:hh0 + j + 1], multi=(j > 0))
            ps_pool.release(psv)
            yield
            psQK = ps_pool.alloc()
            for j in range(2):
                S.mm(psQK[:, hs(j)], kT(j), qT(j))
            PT = ur()
            S.tt('dve', flat(PT), psQK[:, 0:256], DXU[:, j0:j0 + 2, :].rearrange('p a b -> p (a b)'), ALU.mult)
            ps_pool.release(psQK)
            QD = ur()
            for j in range(2):
                S.tt('dve', QD[:, j, :], qT(j), EGR[:, j0 + j, :], ALU.mult, multi=(j > 0))
            yield
            NF = None
            for k in range(1, 7):
                ps1 = ps_pool.alloc() if k <= 5 else None
                ps2 = ps_pool.alloc() if k >= 2 else None
                if k <= 5:
                    for j in range(2):
                        S.mm(ps1[:, hs(j)], Qj(j), Pj(j))
                if k <= 4:
                    for j in range(2):
                        S.mm(ps1[:, hs(2 + j)], Pj(j), Qj(j))
                if k >= 2:
                    for j in range(2):
                        S.mm(ps2[:, hs(j)], Mj(j), Pj(j))
                if 2 <= k <= 5:
                    for j in range(2):
                        S.mm(ps2[:, hs(2 + j)], Pj(j), Mj(j))
                if k <= 4:
                    S.copy('act', flat(PQ), ps1[:, 0:512])
                elif k == 5:
                    S.copy('act', PQ[:, 0:2, :].rearrange('p a b -> p (a b)'), ps1[:, 0:256])
                if ps1 is not None:
                    ps_pool.release(ps1)
                if 2 <= k <= 5:
                    S.tt('dve', flat(NM), ps2[:, 0:512], flat(NM), ALU.add)
                elif k == 6:
                    NF = ur()
                    S.tt('dve', flat(NF), ps2[:, 0:256], NM[:, 0:2, :].rearrange('p a b -> p (a b)'), ALU.add)
                if ps2 is not None:
                    ps_pool.release(ps2)
                yield
            ut_pool.release(PQ, NM)
            T1 = ua()
            UB = V(T1.bufs, T1.t[:, 2:4, :])
            psW = ps_pool.alloc()
            for j in range(2):
                S.mm(psW[:, hs(j)], KBG[:, j, :], NF[:, j, :])
            WT = ur()
            S.copy('act', flat(WT), psW[:, 0:256])
            ps_pool.release(psW)
            psU = ps_pool.alloc()
            for j in range(2):
                S.mm(psU[:, hs(j)], NF[:, j, :], VB[:, j, :])
            S.copy('act', UB.rearrange('p a b -> p (a b)'), psU[:, 0:256], multi=True)
            ps_pool.release(psU)
            utr_pool.release(KBG, VB, NF)
            yield
            T2 = ua()
            O = V(T2.bufs, T2.t[:, 0:2, :])
            sq = V(T2.bufs, T2.t[:, 2:4, :])
            Of = O.rearrange('p a b -> p (a b)')
            for b in range(NB):
                si = l if tl.kind == 'p' else b

                def Sv(j):
                    return V([SST.bufs[si]], SST.t[:, si, hh0 + j, :])
                psWS = ps_pool.alloc()
                for j in range(2):
                    S.mm(psWS[:, hs(j)], WT[:, j, :], Sv(j))
                U = ur()
                S.tt('dve', flat(U), UB.rearrange('p a b -> p (a b)'), psWS[:, 0:256], ALU.subtract)
                ps_pool.release(psWS)
                KE = ur()
                for j in range(2):
                    S.ts('dve', KE[:, j, :], KTM[:, j, :], S2M[:, g, b, hh0 + j:hh0 + j + 1], multi=(j > 0))
                yield
                psO = ps_pool.alloc()
                for j in range(2):
                    S.mm(psO[:, hs(j)], QD[:, j, :], Sv(j), start=True, stop=False)
                    S.mm(psO[:, hs(j)], PT[:, j, :], U[:, j, :], start=False, stop=True)
                psS = ps_pool.alloc()
                for j in range(2):
                    S.mm(psS[:, hs(j)], KE[:, j, :], U[:, j, :])
                if b == 0:
                    S.ts('dve', Of, psO[:, 0:256], msk['BM'][:, b:b + 1])
                else:
                    S.stt(Of, psO[:, 0:256], msk['BM'][:, b:b + 1], Of, ALU.mult, ALU.add)
                ps_pool.release(psO)
                for j in range(2):
                    gend = EGR[:, j0 + j, (b + 1) * BL - 1:(b + 1) * BL]
                    S.stt(Sv(j), Sv(j), gend, psS[:, hs(j)], ALU.mult, ALU.add, multi=(j > 0))
                ps_pool.release(psS)
                utr_pool.release(U, KE)
                yield
            utr_pool.release(WT, PT, QD, KTMt)
            ut_pool.release(T1)
            sm = SMALL.alloc()
            for j in range(2):
                S.stt(sq[:, j, :], O[:, j, :], 1.0, O[:, j, :], ALU.mult, ALU.mult, accum=sm[:, j:j + 1], multi=(j > 0))
            S.act(sm[:, 0:2], sm[:, 0:2], AF.Ln, scale=1.0 / 128.0, bias=C_RMS)
            S.act(sm[:, 0:2], sm[:, 0:2], AF.Exp, scale=-0.5)
            for j in range(2):
                S.ts('dve', sq[:, j, :], O[:, j, :], sm[:, j:j + 1], multi=(j > 0))
            SMALL.release(sm)
            psT = ps_pool.alloc()
            for j in range(2):
                S.transpose(psT[:, hs(j)], sq[:, j, :], IDENT, multi=(j > 0))
            ov = V([BB.bufs[hh0], BB.bufs[hh0 + 1]], BB.t[:, hh0:hh0 + 2, tg], True)
            S.act(ov, psT[:, 0:256].rearrange('p (a b) -> p a b', a=2), AF.Identity, scale=DNG[:, l:l + 1])
            ps_pool.release(psT)
            ut_pool.release(T2)

        def run_interleaved(gens):
            gens = list(gens)
            while gens:
                for gq in list(gens):
                    try:
                        next(gq)
                    except StopIteration:
                        gens.remove(gq)

        def branch_d(tl, l):
            T, NS, L, G = tl.T, tl.NS, tl.L, tl.G
            NB, BL = tl.NB, tl.BL
            msk = CMASK[tl.kind]
            H = 3
            wt = wnext('m', OFF_DA)
            for g in range(G):
                ps = ps_pool.alloc()
                for kc in range(KC):
                    S.mm(ps[:, 0:16], X.sub(kc)[:, g * 128:(g + 1) * 128], wt[:, kc, 0:16],
                         start=(kc == 0), stop=(kc == KC - 1))
                t1 = SMALL.alloc()
                S.tt('dve', t1[:, 0:8], ps[:, 0:8], DTB[:, l, :], ALU.add)
                S.act(t1[:, 0:8], t1[:, 0:8], AF.Exp)
                S.act(t1[:, 0:8], t1[:, 0:8], AF.Ln, bias=C_ONE)
                S.tt('dve', GG[:, g, :], t1[:, 0:8], NEGA[:, l, :], ALU.mult, multi=(g > 0))
                S.act(BETA[:, g, :], ps[:, 8:16], AF.Sigmoid, multi=(g > 0))
                ps_pool.release(ps)
                SMALL.release(t1)
                p2 = ps_pool.alloc()
                S.mm(p2[:, 0:8], msk['MUI'], GG[:, g, :])
                S.mm(p2[:, 8:16], msk['BLK'], GG[:, g, :], start=True, stop=True)
                S.copy('act', GC[:, g, :], p2[:, 0:8], multi=(g > 0))
                eg = SMALL.alloc()
                S.act(eg[:, 0:8], p2[:, 0:8], AF.Exp)
                S.tt('dve', S1[:, g, :], eg[:, 0:8], BETA[:, g, :], ALU.mult, multi=(g > 0))
                S.tt('dve', eg[:, 0:8], p2[:, 8:16], GC[:, g, :], ALU.subtract)
                ps_pool.release(p2)
                S.act(S2[:, g, :], eg[:, 0:8], AF.Exp, multi=(g > 0))
                SMALL.release(eg)
                for b in range(NB):
                    S.ts('dve', S2M[:, g, b, :], S2[:, g, :], msk['BM'][:, b:b + 1], multi=(g > 0 or b > 0))
            wdone(wt)
            if tl.kind == 's':
                for s in range(NSS):
                    S.dma(SST.sub(s), st_delta[l, s].rearrange('h k v -> k h v'), ('sst', s))
            elif tl.first:
                for h in range(8):
                    S.ts('dve', V([SST.bufs[l]], SST.t[:, l, h, :]), IDENT, 0.0, multi=(h > 0))
            for half in range(2):
                dslots = [Buf('dq_s%d' % i) for i in range(4)]
                ndg = 0
                for sec in range(3):
                    emit_d = not (tl.kind == 'p' and not tl.last)
                    ps_rows = ps_pool.alloc() if emit_d else None
                    wt = None
                    for cc in range(4):
                        if cc % 2 == 0:
                            if wt is not None:
                                wdone(wt)
                            wt = wnext('m', OFF_D + sec * D + half * 512 + (cc // 2) * 256)
                        qi = sec * 4 + cc
                        hc = sec * 8 + half * 4 + cc
                        ct = CT[qi % 2]
                        hv = hview(ct.v(), NS, H, L)
                        if tl.kind == 'p':
                            if tl.first:
                                S.ts('dve', hv[:, :, 0:H], IDENT[:, 0:NS * H].rearrange('p (s h) -> p s h', s=NS), 0.0)
                            else:
                                S.copy('dve', hv[:, :, 0:H], HD[:, l, hc, :].rearrange('p (s h) -> p s h', s=1))
                        else:
                            S.copy('dve', hv[:, :, 0:H], STD[:, :, hc, :])
                        ps = ps_pool.alloc()
                        mm8(ps[:, 0:T], wt, cc % 2, X, T)
                        S.copy('act', hv[:, :, H:H + L], ps3(ps, NS, L), multi=True)
                        ps_pool.release(ps)
                        ps2 = ps_pool.alloc()
                        for k in range(4):
                            i = ndg % 4
                            first = ndg < 4
                            ndg += 1
                            ap = BB.t[:, 7, i * 128:(i + 1) * 128]
                            wv = V([dslots[i]] + ([BB.bufs[7]] if first else []), ap, True)
                            rv = V([dslots[i]], ap, True)
                            S.ts('dve', wv, IDENT, par(l, (hc % 8), PR_DCW + k * 3 + sec))
                            S.mm(ps3(ps2, NS, L), rv.r(), hv[:, :, k:k + L].r(), start=(k == 0), stop=(k == 3))
                        S.act(SCR.sub(qi)[:, 0:T], ps2[:, 0:T], AF.Silu)
                        ps_pool.release(ps2)
                        if tl.kind == 'p' and not tl.last:
                            S.copy('dve', HD[:, l, hc, :], hv[:, 0, L:L + H])
                        else:
                            src, gt = rows_src(hv[:, :, L:L + H], NS, H)
                            S.transpose(ps_rows[0:NS * H, cc * 128:(cc + 1) * 128], src, IDENT, multi=(cc > 0))
                            if gt is not None:
                                gr_pool.release(gt)
                    if emit_d:
                        c0 = sec * D + half * 512
                        rows_finish(ps_rows, NS * H, 512, o_d[tl.kind][l, :, :, c0:c0 + 512])
                    wdone(wt)
                S.ts('dve', V(dslots + [BB.bufs[7]], BB.t[:, 7, 511:512], True), IDENT[:, 0:1], 0.0)
                for q0 in (0, 4):
                    ts_ = [tmp_pool.alloc() for _ in range(4)]
                    pss = [ps_pool.alloc() for _ in range(4)]
                    for i in range(4):
                        S.act(ts_[i][:, 0:T], SCR.sub(q0 + i)[:, 0:T], AF.Square)
                    for i in range(4):
                        S.mm(pss[i][:, 0:T], ONES, ts_[i][:, 0:T])
                    for i in range(4):
                        S.act(ts_[i][:, 0:T], pss[i][:, 0:T], AF.Ln, bias=C_L2)
                        ps_pool.release(pss[i])
                    for i in range(4):
                        S.act(ts_[i][:, 0:T], ts_[i][:, 0:T], AF.Exp, scale=-0.5)
                    for i in range(4):
                        src = SCR.sub(q0 + i)[:, 0:T]
                        if q0 == 0:
                            S.stt(src, src, 128.0 ** -0.5, ts_[i][:, 0:T], ALU.mult, ALU.mult)
                        else:
                            S.tt('dve', src, src, ts_[i][:, 0:T], ALU.mult)
                        tmp_pool.release(ts_[i])
                tts = [tmp_pool.alloc() for _ in range(6)]

                def as4(t):
                    return Tile(t.t[:, :].rearrange('p (a b) -> p a b', a=4), t.name + '_v')

                def mk(t):
                    v = as4(t)
                    v.bufs = t.bufs
                    return v
                dsets = [dict(EX=EX, DXL=DXL, DXU=DXU, EGR=EGR),
                         dict(EX=mk(tts[0]), DXL=mk(tts[1]), DXU=mk(tts[2]), EGR=mk(tts[3]))]
                extra = []
                for t in tts[4:6]:
                    for hf in range(2):
                        e = Tile(t.t[:, hf * 256:(hf + 1) * 256].rearrange('p (a b) -> p a b', a=2), t.name + '_e%d' % hf)
                        e.bufs = t.bufs
                        extra.append(e)
                for e in extra:
                    utr_pool.free.append(e)

                def prep_gen(g, ds):
                    EXs, DXLs, DXUs, EGRs = ds['EX'], ds['DXL'], ds['DXU'], ds['EGR']
                    GDs = EGRs
                    for j in range(4):
                        hh = half * 4 + j
                        S.ts('dve', GDs[:, j, :], msk['MUI'], GG[:, g, hh:hh + 1], multi=(j > 0))
                    yield
                    ps = ps_pool.alloc()
                    S.mm(ps[:, :], ONES, GDs.v().rearrange('p a b -> p (a b)'))
                    for j in range(4):
                        hh = half * 4 + j
                        S.ts('dve', EXs[:, j, :], ps[:, j * 128:(j + 1) * 128], GC[:, g, hh:hh + 1], None,
                             ALU.subtract, multi=(j > 0))
                    S.act(EGRs.v().rearrange('p a b -> p (a b)'), ps[:, :], AF.Exp)
                    ps_pool.release(ps)
                    yield
                    exf = EXs.v().rearrange('p a b -> p (a b)')
                    S.stt(exf, exf, -1.0, exf, ALU.mult, ALU.max)
                    S.act(EXs.v(), EXs.v(), AF.Exp, scale=-1.0)
                    yield
                    for j in range(4):
                        S.tt('dve', DXLs[:, j, :], EXs[:, j, :], msk['MLS'], ALU.mult, multi=(j > 0))
                    yield
                    for j in range(4):
                        S.tt('dve', DXUs[:, j, :], EXs[:, j, :], msk['MUI'], ALU.mult, multi=(j > 0))
                    yield

                run_interleaved([prep_gen(0, dsets[0])])
                for g in range(G):
                    gens = [gdn_pair(tl, l, g, half, jp, msk, dsets[g % 2]) for jp in range(2)]
                    if g + 1 < G:
                        gens.append(prep_gen(g + 1, dsets[(g + 1) % 2]))
                    run_interleaved(gens)
                for e in extra:
                    utr_pool.free.remove(e)
                tmp_pool.release(*tts)
            if tl.last:
                if tl.kind == 'p':
                    S.dma(o_delta['p'][l, 0].rearrange('h k v -> k h v'), SST.sub(l), ('sst', l))
                else:
                    for s in range(NSS):
                        S.dma(o_delta['s'][l, s].rearrange('h k v -> k h v'), SST.sub(s), ('sst', s))
            for blk in range(4):
                wt = wnext('m', OFF_DZ + blk * 256)
                for cc in range(2):
                    c = blk * 2 + cc
                    ps = ps_pool.alloc()
                    mm8(ps[:, 0:T], wt, cc, X, T)
                    t = tmp_pool.alloc()
                    S.act(t[:, 0:T], ps[:, 0:T], AF.Silu)
                    ps_pool.release(ps)
                    S.tt('dve', BB.sub(c)[:, 0:T].r(), BB.sub(c)[:, 0:T], t[:, 0:T], ALU.mult)
                    tmp_pool.release(t)
                wdone(wt)
            merge(tl, l, 3)

        def out_proj(tl, l):
            T = tl.T
            for blk in range(4):
                wt = wnext('m', blk * 256)
                for cc in range(2):
                    c = blk * 2 + cc
                    ps = ps_pool.alloc()
                    mm8(ps[:, 0:T], wt, cc, MIX, T)
                    S.stt(BB.sub(c)[:, 0:T], X.sub(c)[:, 0:T], ALPHA, ps[:, 0:T], ALU.mult, ALU.add)
                    ps_pool.release(ps)
                wdone(wt)
            ln_feature(BB, lambda c: X.sub(c)[:, 0:T].r(), T, lambda c: par(l, c, PR_LNG), lambda c: par(l, c, PR_LNB),
                       AF.Identity)

        def load_sample_states(l):
            for s in range(NSS):
                for hf in range(2):
                    cs = slice(hf * 512, (hf + 1) * 512)
                    stg = tmp_pool.alloc()
                    key = ('tmp', tmp_idx[id(stg)])
                    S.dma(stg[0:2, :], st_a[l, s][:, cs], key)
                    S.dma(stg[2:5, :], st_b[l, s][:, cs], key, multi=True)
                    S.dma(stg[5:6, :], st_lru[l, s:s + 1][:, cs], key, multi=True)
                    S.dma(stg[6:36, :], st_c[l, s][:, cs], key, multi=True)
                    ps = ps_pool.alloc()
                    for cc in range(4):
                        S.transpose(ps[:, cc * 36:(cc + 1) * 36], stg[0:36, cc * 128:(cc + 1) * 128], IDENT[0:36, 0:36],
                                    multi=(cc > 0))
                    S.copy('act', STT[:, s, hf * 4:(hf + 1) * 4, :], ps[:, 0:4 * 36].rearrange('p (c k) -> p c k', c=4),
                           multi=(s > 0 or hf > 0))
                    ps_pool.release(ps)
                    tmp_pool.release(stg)
                for sec in range(3):
                    for hf in range(2):
                        stg = tmp_pool.alloc()
                        key = ('tmp', tmp_idx[id(stg)])
                        S.dma(stg[0:3, :], st_d[l, s][:, sec * D + hf * 512:sec * D + (hf + 1) * 512], key)
                        ps = ps_pool.alloc()
                        for cc in range(4):
                            S.transpose(ps[:, cc * 3:(cc + 1) * 3], stg[0:3, cc * 128:(cc + 1) * 128], IDENT[0:3, 0:3],
                                        multi=(cc > 0))
                        h0 = sec * 8 + hf * 4
                        S.copy('act', STD[:, s, h0:h0 + 4, :], ps[:, 0:12].rearrange('p (c k) -> p c k', c=4),
                               multi=(s > 0 or sec > 0 or hf > 0))
                        ps_pool.release(ps)
                        tmp_pool.release(stg)

        XS = V(BA.bufs, BAF[:, 0:4096].rearrange('p (g d) -> p g d', g=4))

        def load_tile(tl):
            G = tl.G
            if tl.kind == 'p':
                src = x_p[tl.idx * TP:(tl.idx + 1) * TP, :].rearrange('(g p) d -> p g d', p=128)
                S.dma(XS, src, 'xs')
            else:
                S.dma(XS[:, 0, :], x_s, 'xs')
            for g in range(G):
                for hf in range(2):
                    S.bn_stats(XST[:, hf, :], XS[:, g, hf * 512:(hf + 1) * 512], multi=(hf > 0))
                S.bn_aggr(XMV[:, 0:2], XST.v().rearrange('p a b -> p (a b)'))
                S.act(XMV[:, 2:3], XMV[:, 1:2], AF.Ln, bias=C_LN)
                S.act(XMV[:, 2:3], XMV[:, 2:3], AF.Exp, scale=-0.5)
                S.ts('dve', XS[:, g, :], XS[:, g, :], XMV[:, 0:1], XMV[:, 2:3], ALU.subtract, ALU.mult)
                for blk in range(2):
                    ps = ps_pool.alloc()
                    for cc in range(4):
                        c = blk * 4 + cc
                        S.transpose(ps[:, cc * 128:(cc + 1) * 128], XS[:, g, c * 128:(c + 1) * 128], IDENT,
                                    multi=(cc > 0))
                    for cc in range(4):
                        c = blk * 4 + cc
                        S.act(X.sub(c)[:, g * 128:(g + 1) * 128].r(), ps[:, cc * 128:(cc + 1) * 128], AF.Identity,
                              scale=PLN[:, c, 0:1], bias=PLN[:, c, 1:2], multi=(g > 0))
                    ps_pool.release(ps)

        def store_tile(tl):
            G = tl.G
            for g in range(G):
                for blk in range(2):
                    ps = ps_pool.alloc()
                    for cc in range(4):
                        c = blk * 4 + cc
                        S.transpose(ps[:, cc * 128:(cc + 1) * 128], X.sub(c)[:, g * 128:(g + 1) * 128], IDENT,
                                    multi=(cc > 0))
                    S.copy('act', XS[:, g, blk * 512:(blk + 1) * 512], ps[:, :], multi=(g > 0 or blk > 0))
                    ps_pool.release(ps)
            if tl.kind == 'p':
                dst = y_p[tl.idx * TP:(tl.idx + 1) * TP, :].rearrange('(g p) d -> p g d', p=128)
                S.dma(dst, XS, 'xs')
            else:
                S.dma(y_s, XS[:, 0, :], 'xs')

        for tl in tiles:
            load_tile(tl)
            for l in range(n_layers):
                if tl.kind == 's':
                    load_sample_states(l)
                branch_a(tl, l)
                branch_b(tl, l)
                branch_c(tl, l)
                branch_d(tl, l)
                out_proj(tl, l)
            store_tile(tl)
        assert not wsched and not winflight
        S.emit()
    return nc


def _consts():
    c = np.zeros((10, 128, 128), np.float32)
    idx = np.arange(128)
    c[0] = np.eye(128)
    c[1] = 1.0
    c[2] = 1.0 / D
    for base, bl in ((3, 64), (6, 32)):
        same = (idx[:, None] // bl) == (idx[None, :] // bl)
        c[base + 0] = same & (idx[None, :] >= idx[:, None])
        c[base + 1] = same & (idx[None, :] < idx[:, None])
        c[base + 2] = same
    for b in range(2):
        c[9, :, b] = (idx // 64) == b
    for b in range(4):
        c[9, :, 2 + b] = (idx // 32) == b
    return c


_NC_CACHE = {}


def kernel(x_prompt, x_sample, state_conv_a, state_conv_b, state_lru, state_conv_c, state_conv_d, state_delta,
           ln_in_g, ln_in_b, w_in, b_gate, a_conv_w, b_conv_w, b_conv_b, b_wx, b_bx, b_wa, b_ba, b_lambda,
           c_conv_w, c_conv_b, c_ln_g, c_ln_b, d_conv_w, d_a_log, d_dt_bias, d_norm_g, w_branch, w_out, ln_g, ln_b):
    f = lambda a: np.ascontiguousarray(np.asarray(a, dtype=np.float32))
    n = 8
    if 'nc' not in _NC_CACHE:
        _NC_CACHE['nc'] = build_program()
    nc = _NC_CACHE['nc']
    rows = np.concatenate([
        f(b_gate), f(a_conv_w), f(b_conv_w), f(b_conv_b)[:, None], f(b_bx).reshape(NL, 1, D),
        f(b_ba).reshape(NL, 1, D), f(b_lambda)[:, None], f(c_conv_w), f(c_conv_b)[:, None], f(c_ln_g)[:, None],
        f(c_ln_b)[:, None], f(d_conv_w).reshape(NL, 4 * 3, D), f(ln_g)[:, None], f(ln_b)[:, None]], axis=1)
    assert rows.shape == (NL, NPAR, D)
    shared = dict(
        ln_in=np.stack([f(ln_in_g), f(ln_in_b)]), w_in=f(w_in), par_rows=np.ascontiguousarray(rows),
        b_wx=f(b_wx), b_wa=f(b_wa), d_alog=f(d_a_log), d_dtb=f(d_dt_bias), d_ng=f(d_norm_g),
        w_br=f(w_branch), w_out=f(w_out), consts=_consts())
    xp, xs = f(x_prompt), f(x_sample)
    sa, sbb, sl, sc, sd, sdl = (f(a) for a in (state_conv_a, state_conv_b, state_lru, state_conv_c, state_conv_d,
                                               state_delta))
    in_maps = []
    for c in range(n):
        q = slice(c * NSS, (c + 1) * NSS)
        m = dict(shared)
        m.update(x_p=xp[c], x_s=np.ascontiguousarray(xs[q].reshape(NSS * LS, D)),
                 st_a=np.ascontiguousarray(sa[:, q]), st_b=np.ascontiguousarray(sbb[:, q]),
                 st_lru=np.ascontiguousarray(sl[:, q]), st_c=np.ascontiguousarray(sc[:, q]),
                 st_d=np.ascontiguousarray(sd[:, q]), st_delta=np.ascontiguousarray(sdl[:, q]))
        in_maps.append(m)
    res = run_bass_kernel_spmd(nc, in_maps, core_ids=list(range(n)))
    R = res.results
    cat = lambda k, ax: np.concatenate([np.asarray(r[k]) for r in R], axis=ax)
    y_prompt = np.stack([np.asarray(r['y_p']) for r in R]).astype(np.float32)
    y_sample = cat('y_s', 0).reshape(n * NSS, LS, D).astype(np.float32)
    outs = [y_prompt, y_sample]
    for pre in ('p', 's'):
        outs.append(cat(pre + '_a', 1))
        outs.append(cat(pre + '_b', 1))
        outs.append(cat(pre + '_lru', 1)[:, :, 0, :])
        outs.append(cat(pre + '_c', 1))
        outs.append(cat(pre + '_d', 1))
        outs.append(cat(pre + '_delta', 1))
    return tuple(np.ascontiguousarray(o, dtype=np.float32) for o in outs)
```

```python
import collections
import contextlib
import numpy as np
import concourse.bass as bass
import concourse.mybir as mybir
from concourse.bass_utils import run_bass_kernel_spmd

F32 = mybir.dt.float32
F32R = mybir.dt.float32r
BF16 = mybir.dt.bfloat16
ALU = mybir.AluOpType
AF = mybir.ActivationFunctionType

D = 1024
NL = 4
KC = 8
SEQ = 4096
TP = 512
NPT = SEQ // TP
NSS = 4
LS = 32
W3 = 3072
OFF_A, OFF_B, OFF_C, OFF_D = 0, 4096, 6144, 9216
OFF_DZ = 12288
OFF_DA = 13312
OFF_G = 13328
N_IN = 17424
ALPHA = (2 * NL) ** 0.25
LN_EPS = 1e-5
RMS_EPS = 1e-6
L2_EPS = 1e-6
LRU_C = 8.0
NPAR = 63
PR_BG, PR_ACW, PR_BCW, PR_BCB, PR_BBX, PR_BBA, PR_LAM, PR_CCW, PR_CCB, PR_CLG, PR_CLB, PR_DCW, PR_LNG, PR_LNB = \
    0, 4, 7, 11, 12, 13, 14, 15, 46, 47, 48, 49, 61, 62

ENGS = ('pe', 'act', 'dve', 'pool', 'sp')
SAME_ENG_DIST = 4


class Buf:
    __slots__ = ('name', 'excl', 'writes', 'reads')

    def __init__(self, name, excl=False):
        self.name = name
        self.excl = excl
        self.writes = []
        self.reads = []


class V:
    __slots__ = ('bufs', 'ap', 'rr')

    def __init__(self, bufs, ap, rr=False):
        self.bufs = tuple(bufs)
        self.ap = ap
        self.rr = rr

    def __getitem__(self, idx):
        return V(self.bufs, self.ap[idx], self.rr)

    def r(self):
        return V(self.bufs, self.ap.bitcast(F32R), self.rr)

    def rearrange(self, pattern, **kw):
        return V(self.bufs, self.ap.rearrange(pattern, **kw), self.rr)

    @property
    def o(self):
        return self.ap.bitcast(F32R) if self.rr else self.ap


class Op:
    __slots__ = ('eng', 'fn', 'waits', 'signal', 'idx', 'sig_val', 'dsem', 'dval')

    def __init__(self, eng, fn):
        self.eng = eng
        self.fn = fn
        self.waits = []
        self.signal = False
        self.idx = -1
        self.sig_val = None
        self.dsem = None
        self.dval = None


class Sched:
    def __init__(self, nc):
        self.nc = nc
        self.ops = {e: [] for e in ENGS}
        self.waited = {e: collections.defaultdict(int) for e in ENGS}
        self.dma_counts = collections.defaultdict(int)

    def _dep(self, op, prod, same_eng_raw=False):
        if prod is op:
            return
        if prod.dsem is not None:
            key = ('d', prod.dsem)
            if self.waited[op.eng][key] >= prod.dval:
                return
            self.waited[op.eng][key] = prod.dval
            op.waits.append((key, prod.dval))
            return
        if prod.eng == op.eng:
            if not same_eng_raw:
                return
            if op.idx - prod.idx >= SAME_ENG_DIST:
                return
        key = ('e', prod.eng)
        if self.waited[op.eng][key] >= prod.idx + 1:
            return
        self.waited[op.eng][key] = prod.idx + 1
        prod.signal = True
        op.waits.append((key, prod))

    @staticmethod
    def _bufs(vs):
        out = []
        for v in vs:
            if v is None:
                continue
            for b in v.bufs:
                if b not in out:
                    out.append(b)
        return out

    def add(self, eng, fn, reads=(), writes=(), dsem=None, multi=False):
        op = Op(eng, fn)
        op.idx = len(self.ops[eng])
        rb = self._bufs(reads)
        wb = self._bufs(writes)
        if dsem is not None:
            self.dma_counts[dsem] += 1
            op.dsem = dsem
            op.dval = 16 * self.dma_counts[dsem]
        saved = {}
        if multi:
            for b in wb:
                if not b.reads:
                    saved[b] = list(b.writes)
                    b.writes = []
        for b in rb:
            if b in wb:
                continue
            for p in b.writes:
                self._dep(op, p, same_eng_raw=True)
            if b.excl:
                for p in b.reads:
                    self._dep(op, p)
        for b in wb:
            for p in b.writes:
                self._dep(op, p, same_eng_raw=(b in rb))
            for p in b.reads:
                self._dep(op, p)
        for b in rb:
            if b not in wb:
                b.reads.append(op)
        for b in wb:
            b.writes = saved.get(b, []) + [op]
            b.reads = []
        self.ops[eng].append(op)
        return op

    def emit(self):
        nc = self.nc
        for e in ENGS:
            n = 0
            for op in self.ops[e]:
                if op.signal:
                    n += 1
                    op.sig_val = n
        dkeys = sorted(self.dma_counts.keys(), key=str)
        with contextlib.ExitStack() as st:
            esem = {e: st.enter_context(nc.semaphore('sem_' + e)) for e in ENGS}
            dsem = {k: st.enter_context(nc.semaphore('dsem_%d' % i)) for i, k in enumerate(dkeys)}
            block = st.enter_context(nc.Block())

            def body(e):
                def f(eng):
                    for op in self.ops[e]:
                        for key, val in op.waits:
                            if key[0] == 'd':
                                eng.wait_ge(dsem[key[1]], val)
                            else:
                                eng.wait_ge(esem[key[1]], val.sig_val)
                        ins = op.fn(eng)
                        if op.dsem is not None:
                            ins.then_inc(dsem[op.dsem], 16)
                        elif op.signal:
                            ins.then_inc(esem[e], 1)
                    if e == 'sp':
                        for k in dkeys:
                            eng.wait_ge(dsem[k], 16 * self.dma_counts[k])
                return f

            block.tensor(body('pe'))
            block.scalar(body('act'))
            block.vector(body('dve'))
            block.gpsimd(body('pool'))
            block.sync(body('sp'))

    def mm(self, out, lhsT, rhs, start=True, stop=True):
        def fn(eng):
            return eng.matmul(out.ap, lhsT.ap, rhs.ap, start=start, stop=stop)
        return self.add('pe', fn, reads=[lhsT, rhs], writes=[out], multi=not start)

    def transpose(self, out, in_, ident, multi=False):
        def fn(eng):
            return eng.transpose(out.ap, in_.ap, ident.ap)
        return self.add('pe', fn, reads=[in_, ident], writes=[out], multi=multi)

    def act(self, out, in_, func, bias=None, scale=None, accum=None, multi=False):
        def fn(eng):
            kw = {}
            if bias is not None:
                kw['bias'] = bias.ap if isinstance(bias, V) else bias
            if scale is not None:
                kw['scale'] = scale.ap if isinstance(scale, V) else scale
            if accum is not None:
                kw['accum_out'] = accum.ap
            return eng.activation(out.o, in_.ap, func, **kw)
        reads = [in_] + [a for a in (bias, scale) if isinstance(a, V)]
        writes = [out] + ([accum] if accum is not None else [])
        return self.add('act', fn, reads=reads, writes=writes, multi=multi)

    def tt(self, e, out, in0, in1, op, multi=False):
        def fn(eng):
            return eng.tensor_tensor(out.o, in0.ap, in1.ap, op)
        return self.add(e, fn, reads=[in0, in1], writes=[out], multi=multi)

    def ts(self, e, out, in0, s1, s2=None, op0=ALU.mult, op1=None, multi=False):
        def fn(eng):
            a1 = s1.ap if isinstance(s1, V) else s1
            a2 = s2.ap if isinstance(s2, V) else s2
            if op1 is None:
                return eng.tensor_scalar(out.o, in0.ap, a1, None, op0)
            return eng.tensor_scalar(out.o, in0.ap, a1, a2, op0, op1)
        reads = [in0] + [a for a in (s1, s2) if isinstance(a, V)]
        return self.add(e, fn, reads=reads, writes=[out], multi=multi)

    def stt(self, out, in0, scalar, in1, op0, op1, multi=False, accum=None):
        def fn(eng):
            sc = scalar.ap if isinstance(scalar, V) else scalar
            if accum is not None:
                return eng.scalar_tensor_tensor(out.o, in0.ap, sc, in1.ap, op0, op1, accum_out=accum.ap)
            return eng.scalar_tensor_tensor(out.o, in0.ap, sc, in1.ap, op0, op1)
        reads = [in0, in1] + ([scalar] if isinstance(scalar, V) else [])
        writes = [out] + ([accum] if accum is not None else [])
        return self.add('dve', fn, reads=reads, writes=writes, multi=multi)

    def scan(self, out, d0, d1, initial, multi=False):
        def fn(eng):
            ini = initial.ap if isinstance(initial, V) else initial
            return eng.tensor_tensor_scan(out.o, d0.ap, d1.ap, ini, ALU.mult, ALU.add)
        reads = [d0, d1] + ([initial] if isinstance(initial, V) else [])
        return self.add('dve', fn, reads=reads, writes=[out], multi=multi)

    def copy(self, e, out, in_, multi=False):
        if e == 'act':
            def fn(eng):
                return eng.copy(out.o, in_.ap)
        else:
            def fn(eng):
                return eng.tensor_copy(out.o, in_.ap)
        return self.add(e, fn, reads=[in_], writes=[out], multi=multi)

    def memset(self, e, out, val, multi=False):
        def fn(eng):
            return eng.memset(out.ap, val)
        return self.add(e, fn, reads=[], writes=[out], multi=multi)

    def recip(self, out, in_, multi=False):
        def fn(eng):
            return eng.reciprocal(out.o, in_.ap)
        return self.add('dve', fn, reads=[in_], writes=[out], multi=multi)

    def bn_stats(self, out, in_, multi=False):
        def fn(eng):
            return eng.bn_stats(out.ap, in_.ap)
        return self.add('dve', fn, reads=[in_], writes=[out], multi=multi)

    def bn_aggr(self, out, in_):
        def fn(eng):
            return eng.bn_aggr(out.ap, in_.ap)
        return self.add('dve', fn, reads=[in_], writes=[out])

    def dma(self, out, in_, dsem, q='sp', multi=False):
        oap = out.o if isinstance(out, V) else out
        iap = in_.ap if isinstance(in_, V) else in_

        def fn(eng):
            return eng.dma_start(out=oap, in_=iap)
        rd = [in_] if isinstance(in_, V) else []
        wr = [out] if isinstance(out, V) else []
        return self.add(q, fn, reads=rd, writes=wr, dsem=dsem, multi=multi)


class Tile:
    def __init__(self, tensor, name, nsub=0, excl=False, rr=False):
        self.t = tensor
        self.name = name
        self.rr = rr
        if nsub:
            self.bufs = [Buf('%s[%d]' % (name, i), excl) for i in range(nsub)]
        else:
            self.bufs = [Buf(name, excl)]

    def v(self):
        return V(self.bufs, self.t[:], self.rr)

    def sub(self, i):
        return V([self.bufs[i]], self.t[:, i], self.rr)

    def __getitem__(self, idx):
        return V(self.bufs, self.t[idx], self.rr)


class TPool:
    def __init__(self, tiles):
        self.free = collections.deque(tiles)

    def alloc(self):
        assert self.free, 'pool exhausted'
        return self.free.popleft()

    def release(self, *ts):
        for t in ts:
            self.free.append(t)


class TileInfo:
    def __init__(self, kind, idx, n_ptiles=NPT):
        self.kind = kind
        self.idx = idx
        if kind == 'p':
            self.T, self.NS, self.L = TP, 1, TP
            self.first = idx == 0
            self.last = idx == n_ptiles - 1
            self.NB, self.BL = 2, 64
        else:
            self.T, self.NS, self.L = NSS * LS, NSS, LS
            self.first = True
            self.last = True
            self.NB, self.BL = 4, 32
        self.G = self.T // 128


def build_program(n_layers=NL, n_ptiles=NPT, do_sample=True):
    nc = bass.Bass('TRN2', target_bir_lowering=False)

    def din(name, shape):
        return nc.dram_tensor(name, list(shape), F32, kind='ExternalInput').ap()

    def dout(name, shape):
        return nc.dram_tensor(name, list(shape), F32, kind='ExternalOutput').ap()

    x_p = din('x_p', [SEQ, D])
    x_s = din('x_s', [NSS * LS, D])
    st_a = din('st_a', [NL, NSS, 2, D])
    st_b = din('st_b', [NL, NSS, 3, D])
    st_lru = din('st_lru', [NL, NSS, D])
    st_c = din('st_c', [NL, NSS, 30, D])
    st_d = din('st_d', [NL, NSS, 3, W3])
    st_delta = din('st_delta', [NL, NSS, 8, 128, 128])
    ln_in = din('ln_in', [2, D])
    w_in = din('w_in', [NL, D, N_IN])
    par_rows = din('par_rows', [NL, NPAR, D])
    b_wx = din('b_wx', [NL, 8, 128, 128])
    b_wa = din('b_wa', [NL, 8, 128, 128])
    d_alog = din('d_alog', [NL, 8])
    d_dtb = din('d_dtb', [NL, 8])
    d_ng = din('d_ng', [NL, 128])
    w_br = din('w_br', [NL, 4, D, D])
    w_out = din('w_out', [NL, D, D])
    consts = din('consts', [10, 128, 128])

    y_p = dout('y_p', [SEQ, D])
    y_s = dout('y_s', [NSS * LS, D])
    o_a = {'p': dout('p_a', [NL, 1, 2, D]), 's': dout('s_a', [NL, NSS, 2, D])}
    o_b = {'p': dout('p_b', [NL, 1, 3, D]), 's': dout('s_b', [NL, NSS, 3, D])}
    o_lru = {'p': dout('p_lru', [NL, 1, 1, D]), 's': dout('s_lru', [NL, NSS, 1, D])}
    o_c = {'p': dout('p_c', [NL, 1, 30, D]), 's': dout('s_c', [NL, NSS, 30, D])}
    o_d = {'p': dout('p_d', [NL, 1, 3, W3]), 's': dout('s_d', [NL, NSS, 3, W3])}
    o_delta = {'p': dout('p_delta', [NL, 1, 8, 128, 128]), 's': dout('s_delta', [NL, NSS, 8, 128, 128])}

    st = contextlib.ExitStack()
    with st:
        S = Sched(nc)

        def sb(name, shape, nsub=0, rr=False, dt=F32):
            return Tile(st.enter_context(nc.sbuf_tensor(name, list(shape), dt)), name, nsub=nsub, rr=rr)

        X = sb('X', [128, KC, TP], nsub=KC, rr=True)
        MIX = sb('MIX', [128, KC, TP], nsub=KC, rr=True)
        BA = sb('BA', [128, KC, TP + 32], nsub=KC)
        BB = sb('BB', [128, KC, TP], nsub=KC, rr=True)
        SCR = sb('SCR', [128, 12, TP], nsub=12, rr=True)
        tmp_pool = TPool([sb('TMP%d' % i, [128, TP]) for i in range(6)])
        w_pool = TPool([sb('WP%d' % i, [128, KC, 256], rr=True) for i in range(4)])
        SST = sb('SST', [128, 4, 8, 128], nsub=4)
        CONST = sb('CONST', [128, 10, 128])
        PAR = sb('PAR', [128, NL, KC, 64])
        PLN = sb('PLN', [128, KC, 2])
        DNG = sb('DNG', [128, NL])
        DTB = sb('DTB', [128, NL, 8])
        NEGA = sb('NEGA', [128, NL, 8])
        CLAM = sb('CLAM', [128, NL, KC])
        CC = sb('CC', [128, 8])
        HA = sb('HA', [128, NL, KC, 2])
        HB = sb('HB', [128, NL, KC, 3])
        HC = sb('HC', [128, NL, KC, 30])
        HD = sb('HD', [128, NL, 24, 3])
        HL = sb('HL', [128, NL, KC])
        STT = sb('STT', [128, NSS, KC, 36])
        STD = sb('STD', [128, NSS, 24, 3])
        gr_pool = TPool([sb('GR%d' % i, [128, 128]) for i in range(4)])
        CT = [sb('CT%d' % i, [128, TP + 8], rr=True) for i in range(2)]
        GB = sb('GB', [128, 4, 16])
        GG = sb('GG', [128, 4, 8])
        BETA = sb('BETA', [128, 4, 8])
        GC = sb('GC', [128, 4, 8])
        S1 = sb('S1', [128, 4, 8])
        S2 = sb('S2', [128, 4, 8])
        S2M = sb('S2M', [128, 4, 4, 8])
        SMALL = TPool([sb('SM%d' % i, [128, 16]) for i in range(6)])
        EX = sb('EX', [128, 4, 128])
        DXL = sb('DXL', [128, 4, 128])
        DXU = sb('DXU', [128, 4, 128])
        EGR = sb('EGR', [128, 4, 128])
        GD = EGR
        ut_pool = TPool([sb('UT%d' % i, [128, 4, 128]) for i in range(4)])
        utr_pool = TPool([sb('UR%d' % i, [128, 2, 128]) for i in range(4)])
        ubf_pool = TPool([sb('UB%d' % i, [128, 2, 128], dt=BF16) for i in range(12)])
        XST = sb('XST', [128, 2, 6])
        XMV = sb('XMV', [128, 4])

        ps_pool = TPool([Tile(st.enter_context(nc.psum_tensor('PS%d' % i, [128, 512], F32)), 'PS%d' % i, excl=True)
                         for i in range(8)])

        IDENT = CONST[:, 0, :]
        ONES = CONST[:, 1, :]
        ONESM = CONST[:, 2, :]
        CMASK = {'p': dict(MUI=CONST[:, 3, :], MLS=CONST[:, 4, :], BLK=CONST[:, 5, :], BM=CONST[:, 9, 0:2]),
                 's': dict(MUI=CONST[:, 6, :], MLS=CONST[:, 7, :], BLK=CONST[:, 8, :], BM=CONST[:, 9, 2:6])}

        def par(l, c, idx):
            return PAR[:, l, c, idx:idx + 1]

        S.dma(CONST.v(), consts.rearrange('k p n -> p k n'), 'const')
        for i, val in enumerate([LN_EPS, RMS_EPS, L2_EPS, 1.0, 0.0]):
            S.memset('dve', CC[:, i:i + 1], val, multi=(i > 0))
        C_LN, C_RMS, C_L2, C_ONE, C_ZERO = (CC[:, i:i + 1] for i in range(5))

        BAF = BA.t[:, :, :].rearrange('p a b -> p (a b)')
        PSTG = V(BA.bufs, BAF[0:64, 0:1024])
        for l in range(n_layers):
            S.dma(PSTG[0:NPAR, :], par_rows[l], 'pstg')
            ps = ps_pool.alloc()
            for c in range(KC):
                S.transpose(ps[:, c * 64:c * 64 + NPAR], PSTG[0:NPAR, c * 128:(c + 1) * 128], IDENT[0:NPAR, 0:NPAR],
                            multi=(c > 0))
            S.copy('act', PAR[:, l, :, 0:NPAR], ps[:, :].rearrange('p (c k) -> p c k', c=KC)[:, :, 0:NPAR],
                   multi=(l > 0))
            ps_pool.release(ps)
        S.dma(PSTG[0:2, :], ln_in, 'pstg')
        ps = ps_pool.alloc()
        for c in range(KC):
            S.transpose(ps[:, c * 2:c * 2 + 2], PSTG[0:2, c * 128:(c + 1) * 128], IDENT[0:2, 0:2], multi=(c > 0))
        S.copy('act', PLN.v(), ps[:, 0:16].rearrange('p (c k) -> p c k', c=KC))
        ps_pool.release(ps)
        S.dma(PSTG[0:NL, 0:128], d_ng, 'pstg')
        ps = ps_pool.alloc()
        S.transpose(ps[:, 0:NL], PSTG[0:NL, 0:128], IDENT[0:NL, 0:NL])
        S.copy('act', DNG.v(), ps[:, 0:NL])
        ps_pool.release(ps)
        for l in range(NL):
            S.dma(DTB[:, l, :], d_dtb[l].partition_broadcast(128), 'dtb', multi=(l > 0))
            S.dma(NEGA[:, l, :], d_alog[l].partition_broadcast(128), 'alog', multi=(l > 0))
        S.act(NEGA.v(), NEGA.v(), AF.Exp)
        S.ts('dve', NEGA.v(), NEGA.v(), -1.0)
        for l in range(n_layers):
            S.act(CLAM[:, l, :], PAR[:, l, :, PR_LAM], AF.Exp, scale=-1.0, multi=(l > 0))
        S.act(CLAM.v(), CLAM.v(), AF.Ln, bias=C_ONE)
        S.ts('dve', CLAM.v(), CLAM.v(), -LRU_C)

        def layer_wdescs(l):
            d = []

            def win(c0, n=256):
                d.append(('m', w_in[l], c0, n))

            def win4(c0):
                for q in range(4):
                    win(c0 + q * 256)
            for br in range(4):
                if br == 0:
                    for sec in (1, 2, 3, 0):
                        win4(OFF_A + sec * D)
                elif br == 1:
                    win4(OFF_B)
                    d.append(('g', b_wx[l]))
                    d.append(('g', b_wa[l]))
                    win4(OFF_B + D)
                elif br == 2:
                    for sec in range(3):
                        win4(OFF_C + sec * D)
                else:
                    win(OFF_DA, 16)
                    for half in range(2):
                        for sec in range(3):
                            win(OFF_D + sec * D + half * 512)
                            win(OFF_D + sec * D + half * 512 + 256)
                    win4(OFF_DZ)
                for blk in range(4):
                    win(OFF_G + br * D + blk * 256)
                    d.append(('m', w_br[l, br], blk * 256, 256))
            for blk in range(4):
                d.append(('m', w_out[l], blk * 256, 256))
            return d

        tiles = [TileInfo('p', i, n_ptiles) for i in range(n_ptiles)] + ([TileInfo('s', 0)] if do_sample else [])
        wsched = collections.deque()
        for tl in tiles:
            for l in range(n_layers):
                wsched.extend(layer_wdescs(l))
        winflight = collections.deque()
        wslot = {id(t): i for i, t in enumerate(w_pool.free)}

        def wissue():
            while wsched and w_pool.free:
                desc = wsched.popleft()
                t = w_pool.alloc()
                if desc[0] == 'm':
                    _, mat, c0, n = desc
                    S.dma(t[:, :, 0:n].r(), mat[:, c0:c0 + n].rearrange('(kc p) n -> p kc n', p=128),
                          ('w', wslot[id(t)]), q='pool')
                else:
                    S.dma(t[:, :, 0:128].r(), desc[1].rearrange('h i j -> i h j'), ('w', wslot[id(t)]), q='pool')
                winflight.append((desc, t))

        def wnext(kind, c0=None):
            if not winflight:
                wissue()
            desc, t = winflight.popleft()
            assert desc[0] == kind and (c0 is None or desc[2] == c0), (desc[0], kind, c0, desc[2:])
            wissue()
            return t

        def wdone(t):
            w_pool.release(t)

        def mm8(ps_v, wt, cc, rhs_tile, T, ncols=128):
            for kc in range(KC):
                S.mm(ps_v, wt[:, kc, cc * 128:cc * 128 + ncols].r(), rhs_tile.sub(kc)[:, 0:T].r(),
                     start=(kc == 0), stop=(kc == KC - 1))

        def hview(tile_v, NS, H, L):
            return tile_v[:, 0:NS * (H + L)].rearrange('p (s l) -> p s l', s=NS)

        def ps3(ps_v, NS, L):
            return ps_v[:, 0:NS * L].rearrange('p (s l) -> p s l', s=NS)

        row_ctr = [0]

        tmp_idx = {id(t): i for i, t in enumerate(tmp_pool.free)}

        def rows_finish(ps, R, n, dst_ap):
            stg = tmp_pool.alloc()
            S.copy('act', stg[0:R, 0:n], ps[0:R, 0:n])
            ps_pool.release(ps)
            S.dma(dst_ap.rearrange('s h n -> (s h) n'), stg[0:R, 0:n], ('tmp', tmp_idx[id(stg)]))
            tmp_pool.release(stg)

        def rows_src(v3, NS_, H_):
            if NS_ == 1:
                return v3[:, 0, :], None
            gt = gr_pool.alloc()
            S.copy('dve', gt[:, 0:NS_ * H_].rearrange('p (s h) -> p s h', s=NS_), v3)
            return gt[:, 0:NS_ * H_], gt

        def emit_rows(src_fn, R, chunks, dst_ap, NS_=1):
            ps = ps_pool.alloc()
            for i, c in enumerate(chunks):
                src, gt = rows_src(src_fn(c), NS_, R // NS_)
                S.transpose(ps[0:R, i * 128:(i + 1) * 128], src, IDENT, multi=(i > 0))
                if gt is not None:
                    gr_pool.release(gt)
            rows_finish(ps, R, len(chunks) * 128, dst_ap)

        def ln_feature(src, dst_fn, T, gcol, bcol, func):
            pm = ps_pool.alloc()
            pq = ps_pool.alloc()
            for c in range(KC):
                S.mm(pm[:, 0:T], ONESM, src.sub(c)[:, 0:T], start=(c == 0), stop=(c == KC - 1))
            for c in range(KC):
                sq = tmp_pool.alloc()
                S.act(sq[:, 0:T], src.sub(c)[:, 0:T], AF.Square)
                S.mm(pq[:, 0:T], ONESM, sq[:, 0:T], start=(c == 0), stop=(c == KC - 1))
                tmp_pool.release(sq)
            mean = tmp_pool.alloc()
            rstd = tmp_pool.alloc()
            S.copy('act', mean[:, 0:T], pm[:, 0:T])
            ps_pool.release(pm)
            S.tt('dve', rstd[:, 0:T], mean[:, 0:T], mean[:, 0:T], ALU.mult)
            S.tt('dve', rstd[:, 0:T], pq[:, 0:T], rstd[:, 0:T], ALU.subtract)
            ps_pool.release(pq)
            S.act(rstd[:, 0:T], rstd[:, 0:T], AF.Ln, bias=C_LN)
            S.act(rstd[:, 0:T], rstd[:, 0:T], AF.Exp, scale=-0.5)
            for c in range(KC):
                t = tmp_pool.alloc()
                S.tt('dve', t[:, 0:T], src.sub(c)[:, 0:T], mean[:, 0:T], ALU.subtract)
                S.tt('dve', t[:, 0:T], t[:, 0:T], rstd[:, 0:T], ALU.mult)
                S.act(dst_fn(c), t[:, 0:T], func, scale=gcol(c), bias=bcol(c))
                tmp_pool.release(t)
            tmp_pool.release(mean, rstd)

        def load_hist(tl, l, H, hist_tile, stt_off, bf=None):
            bf = bf or BA.sub
            NS, L = tl.NS, tl.L
            for c in range(KC):
                hv = hview(bf(c), NS, H, L)[:, :, 0:H]
                if tl.kind == 'p':
                    if tl.first:
                        S.memset('dve', hv, 0.0)
                    else:
                        S.copy('dve', hv, hist_tile[:, l, c, :].rearrange('p (s h) -> p s h', s=1))
                else:
                    S.copy('dve', hv, STT[:, :, c, stt_off:stt_off + H])

        def save_hist(tl, l, H, hist_tile, out_ap, c_list=range(KC), bf=None):
            bf = bf or BA.sub
            NS, L = tl.NS, tl.L
            if tl.kind == 'p' and not tl.last:
                for c in c_list:
                    S.copy('dve', hist_tile[:, l, c, :], hview(bf(c), NS, H, L)[:, 0, L:L + H])
            else:
                for blk in range(2):
                    chunks = list(range(blk * 4, blk * 4 + 4))
                    emit_rows(lambda c: hview(bf(c), NS, H, L)[:, :, L:L + H], NS * H, chunks,
                              out_ap[tl.kind][l, :, :, blk * 512:(blk + 1) * 512], NS_=NS)

        def merge(tl, l, br):
            T = tl.T
            for blk in range(4):
                wg = wnext('m', OFF_G + br * D + blk * 256)
                gs = []
                for cc in range(2):
                    m = blk * 2 + cc
                    ps = ps_pool.alloc()
                    mm8(ps[:, 0:T], wg, cc, X, T)
                    g = tmp_pool.alloc()
                    S.act(g[:, 0:T], ps[:, 0:T], AF.Sigmoid, bias=par(l, m, PR_BG + br))
                    ps_pool.release(ps)
                    gs.append(g)
                wdone(wg)
                wb = wnext('m', blk * 256)
                for cc in range(2):
                    m = blk * 2 + cc
                    ps = ps_pool.alloc()
                    mm8(ps[:, 0:T], wb, cc, BB, T)
                    if br == 0:
                        S.tt('dve', MIX.sub(m)[:, 0:T], gs[cc][:, 0:T], ps[:, 0:T], ALU.mult)
                    else:
                        S.tt('dve', gs[cc][:, 0:T], gs[cc][:, 0:T], ps[:, 0:T], ALU.mult)
                        o = MIX.sub(m)[:, 0:T]
                        S.tt('dve', o.r() if br == 3 else o, MIX.sub(m)[:, 0:T], gs[cc][:, 0:T], ALU.add)
                    ps_pool.release(ps)
                    tmp_pool.release(gs[cc])
                wdone(wb)

        def branch_a(tl, l):
            T, NS, L = tl.T, tl.NS, tl.L
            H = 2
            load_hist(tl, l, H, HA, 0)
            for blk in range(4):
                wt = wnext('m', OFF_A + 1 * D + blk * 256)
                for cc in range(2):
                    c = blk * 2 + cc
                    ps = ps_pool.alloc()
                    mm8(ps[:, 0:T], wt, cc, X, T)
                    S.copy('act', hview(BA.sub(c), NS, H, L)[:, :, H:H + L], ps3(ps, NS, L))
                    ps_pool.release(ps)
                wdone(wt)
            for blk in range(4):
                wt = wnext('m', OFF_A + 2 * D + blk * 256)
                for cc in range(2):
                    c = blk * 2 + cc
                    ps = ps_pool.alloc()
                    mm8(ps[:, 0:T], wt, cc, X, T)
                    hv = hview(BA.sub(c), NS, H, L)
                    S.tt('dve', hv[:, :, H:H + L], hv[:, :, H:H + L], ps3(ps, NS, L), ALU.mult)
                    ps_pool.release(ps)
                    ov = ps3(BB.sub(c), NS, L)
                    S.ts('dve', ov, hv[:, :, 0:L], par(l, c, PR_ACW + 0))
                    for k in (1, 2):
                        S.stt(ov, hv[:, :, k:k + L], par(l, c, PR_ACW + k), ov, ALU.mult, ALU.add)
                wdone(wt)
            save_hist(tl, l, H, HA, o_a)
            for blk in range(4):
                wt = wnext('m', OFF_A + 3 * D + blk * 256)
                for cc in range(2):
                    c = blk * 2 + cc
                    ps = ps_pool.alloc()
                    mm8(ps[:, 0:T], wt, cc, X, T)
                    t = tmp_pool.alloc()
                    S.act(t[:, 0:T], ps[:, 0:T], AF.Silu)
                    ps_pool.release(ps)
                    S.tt('dve', BB.sub(c)[:, 0:T], BB.sub(c)[:, 0:T], t[:, 0:T], ALU.mult)
                    tmp_pool.release(t)
                wdone(wt)
            for blk in range(4):
                wt = wnext('m', OFF_A + 0 * D + blk * 256)
                for cc in range(2):
                    c = blk * 2 + cc
                    ps = ps_pool.alloc()
                    mm8(ps[:, 0:T], wt, cc, X, T)
                    S.tt('dve', BB.sub(c)[:, 0:T].r(), BB.sub(c)[:, 0:T], ps[:, 0:T], ALU.mult)
                    ps_pool.release(ps)
                wdone(wt)
            merge(tl, l, 0)

        def branch_b(tl, l):
            T, NS, L = tl.T, tl.NS, tl.L
            H = 3
            load_hist(tl, l, H, HB, 2)
            for blk in range(4):
                wt = wnext('m', OFF_B + blk * 256)
                for cc in range(2):
                    c = blk * 2 + cc
                    ps = ps_pool.alloc()
                    mm8(ps[:, 0:T], wt, cc, X, T)
                    hv = hview(BA.sub(c), NS, H, L)
                    S.copy('act', hv[:, :, H:H + L], ps3(ps, NS, L))
                    ps_pool.release(ps)
                    ov = ps3(BB.sub(c), NS, L)
                    S.ts('dve', ov, hv[:, :, 0:L], par(l, c, PR_BCW), par(l, c, PR_BCB), ALU.mult, ALU.add)
                    for k in (1, 2, 3):
                        o2 = ov.r() if k == 3 else ov
                        S.stt(o2, hv[:, :, k:k + L], par(l, c, PR_BCW + k), ov, ALU.mult, ALU.add)
                wdone(wt)
            save_hist(tl, l, H, HB, o_b)
            wx = wnext('g')
            wa = wnext('g')
            for c0 in (0, 4):
                gas = [tmp_pool.alloc() for _ in range(4)]
                for i in range(4):
                    c = c0 + i
                    xb = BB.sub(c)[:, 0:T]
                    p1 = ps_pool.alloc()
                    p2 = ps_pool.alloc()
                    S.mm(p1[:, 0:T], wx[:, c, 0:128].r(), xb.r())
                    S.mm(p2[:, 0:T], wa[:, c, 0:128].r(), xb.r())
                    S.act(BA.sub(i)[:, 0:T], p1[:, 0:T], AF.Sigmoid, bias=par(l, c, PR_BBX))
                    S.act(gas[i][:, 0:T], p2[:, 0:T], AF.Sigmoid, bias=par(l, c, PR_BBA))
                    ps_pool.release(p1, p2)
                for i in range(4):
                    c = c0 + i
                    S.act(gas[i][:, 0:T], gas[i][:, 0:T], AF.Exp, scale=CLAM[:, l, c:c + 1])
                for i in range(4):
                    t3 = BA.sub(4 + i)[:, 0:T]
                    S.tt('dve', t3, gas[i][:, 0:T], gas[i][:, 0:T], ALU.mult)
                    S.ts('dve', t3, t3, -1.0, 1.0, ALU.mult, ALU.add)
                    S.ts('dve', t3, t3, 1e-18, None, ALU.max)
                for i in range(4):
                    t3 = BA.sub(4 + i)[:, 0:T]
                    S.act(t3, t3, AF.Ln)
                for i in range(4):
                    t3 = BA.sub(4 + i)[:, 0:T]
                    S.act(t3, t3, AF.Exp, scale=0.5)
                for i in range(4):
                    c = c0 + i
                    gx = BA.sub(i)[:, 0:T]
                    t3 = BA.sub(4 + i)[:, 0:T]
                    xb = BB.sub(c)[:, 0:T]
                    S.tt('dve', gx, gx, t3, ALU.mult)
                    S.tt('dve', gx, gx, xb, ALU.mult)
                    for s_ in range(NS):
                        if tl.kind == 'p':
                            ini = 0.0 if tl.first else HL[:, l, c:c + 1]
                        else:
                            ini = STT[:, s_, c, 5:6]
                        S.scan(BB.sub(c)[:, s_ * L:(s_ + 1) * L], gas[i][:, s_ * L:(s_ + 1) * L],
                               BA.sub(i)[:, s_ * L:(s_ + 1) * L], ini, multi=(s_ > 0))
                    if tl.kind == 'p' and not tl.last:
                        S.copy('dve', HL[:, l, c:c + 1], BB.sub(c)[:, L - 1:L])
                tmp_pool.release(*gas)
            wdone(wx)
            wdone(wa)
            if tl.last:
                for blk in range(2):
                    chunks = list(range(blk * 4, blk * 4 + 4))
                    emit_rows(lambda c: ps3(BB.sub(c), NS, L)[:, :, L - 1:L], NS, chunks,
                              o_lru[tl.kind][l, :, :, blk * 512:(blk + 1) * 512], NS_=NS)
            for blk in range(4):
                wt = wnext('m', OFF_B + D + blk * 256)
                for cc in range(2):
                    c = blk * 2 + cc
                    ps = ps_pool.alloc()
                    mm8(ps[:, 0:T], wt, cc, X, T)
                    t = tmp_pool.alloc()
                    S.act(t[:, 0:T], ps[:, 0:T], AF.Silu)
                    ps_pool.release(ps)
                    S.tt('dve', BB.sub(c)[:, 0:T].r(), BB.sub(c)[:, 0:T], t[:, 0:T], ALU.mult)
                    tmp_pool.release(t)
                wdone(wt)
            merge(tl, l, 1)

        SCRF = SCR.t[:, :, :].rearrange('p a b -> p (a b)')

        def cb(c):
            return V(SCR.bufs, SCRF[:, c * 544:(c + 1) * 544], True)

        def branch_c(tl, l):
            T, NS, L = tl.T, tl.NS, tl.L
            H = 30
            load_hist(tl, l, H, HC, 6, bf=cb)
            for blk in range(4):
                wt = wnext('m', OFF_C + blk * 256)
                for cc in range(2):
                    c = blk * 2 + cc
                    ps = ps_pool.alloc()
                    mm8(ps[:, 0:T], wt, cc, X, T)
                    S.copy('act', hview(cb(c), NS, H, L)[:, :, H:H + L], ps3(ps, NS, L))
                    ps_pool.release(ps)
                wdone(wt)
            for blk in range(4):
                wt = wnext('m', OFF_C + D + blk * 256)
                for cc in range(2):
                    c = blk * 2 + cc
                    ps = ps_pool.alloc()
                    mm8(ps[:, 0:T], wt, cc, X, T)
                    t = tmp_pool.alloc()
                    S.act(t[:, 0:T], ps[:, 0:T], AF.Sigmoid)
                    ps_pool.release(ps)
                    hv = hview(cb(c), NS, H, L)
                    S.tt('dve', hv[:, :, H:H + L], hv[:, :, H:H + L], ps3(t, NS, L), ALU.mult)
                    tmp_pool.release(t)
                wdone(wt)
            NSLOT = 12
            slots = [Buf('dg_s%d' % i) for i in range(NSLOT)]
            n = 0
            for c in range(KC):
                hv = hview(cb(c), NS, H, L)
                ps = ps_pool.alloc()
                for k in range(31):
                    i = n % NSLOT
                    first = n < NSLOT
                    n += 1
                    ap = SCRF[:, 4352 + i * 128:4352 + (i + 1) * 128]
                    wv = V([slots[i]] + (list(SCR.bufs) if first else []), ap, True)
                    rv = V([slots[i]], ap, True)
                    S.ts('dve', wv, IDENT, par(l, c, PR_CCW + k))
                    S.mm(ps3(ps, NS, L), rv.r(), hv[:, :, k:k + L].r(), start=(k == 0), stop=(k == 30))
                S.act(BB.sub(c)[:, 0:T], ps[:, 0:T], AF.Identity, bias=par(l, c, PR_CCB))
                ps_pool.release(ps)
            S.ts('dve', V(slots + list(SCR.bufs), SCRF[:, 6143:6144], True), IDENT[:, 0:1], 0.0)
            save_hist(tl, l, H, HC, o_c, bf=cb)
            ln_feature(BB, lambda c: BB.sub(c)[:, 0:T], T, lambda c: par(l, c, PR_CLG), lambda c: par(l, c, PR_CLB),
                       AF.Silu)
            for blk in range(4):
                wt = wnext('m', OFF_C + 2 * D + blk * 256)
                for cc in range(2):
                    c = blk * 2 + cc
                    ps = ps_pool.alloc()
                    mm8(ps[:, 0:T], wt, cc, X, T)
                    t = tmp_pool.alloc()
                    S.act(t[:, 0:T], ps[:, 0:T], AF.Silu)
                    ps_pool.release(ps)
                    S.tt('dve', BB.sub(c)[:, 0:T].r(), BB.sub(c)[:, 0:T], t[:, 0:T], ALU.mult)
                    tmp_pool.release(t)
                wdone(wt)
            merge(tl, l, 2)

        def gdn_pair(tl, l, g, half, jp, msk, ds):
            NB, BL = tl.NB, tl.BL
            DXL, DXU, EGR = ds['DXL'], ds['DXU'], ds['EGR']
            j0 = 2 * jp
            hh0 = half * 4 + j0
            tg = slice(g * 128, (g + 1) * 128)
            ua, ur, ub = ut_pool.alloc, utr_pool.alloc, ubf_pool.alloc

            def qT(j):
                return SCR.sub(j0 + j)[:, tg]

            def kT(j):
                return SCR.sub(4 + j0 + j)[:, tg]

            def vT(j):
                return SCR.sub(8 + j0 + j)[:, tg]

            def hs(j):
                return slice(j * 128, (j + 1) * 128)

            def flat(t):
                return t.v().rearrange('p a b -> p (a b)')
            IDENT4 = V(IDENT.bufs, IDENT.ap.unsqueeze(1).broadcast_to([128, 4, 128]))
            PQ = ua()
            NM = ua()

            def Pj(j):
                return PQ[:, j, :]

            def Qj(j):
                return PQ[:, 2 + j, :]

            def Nj(j):
                return NM[:, j, :]

            def Mj(j):
                return NM[:, 2 + j, :]
            psG = ps_pool.alloc()
            for j in range(2):
                S.mm(psG[:, hs(j)], kT(j), kT(j))
            for j in range(2):
                S.stt(Qj(j), psG[:, hs(j)], BETA[:, g, hh0 + j:hh0 + j + 1], DXL[:, j0 + j, :], ALU.mult, ALU.mult,
                      multi=True)
            ps_pool.release(psG)
            yield
            psB = ps_pool.alloc()
            for j in range(2):
                S.transpose(psB[:, hs(j)], Qj(j), IDENT, multi=(j > 0))
            S.copy('act', PQ[:, 0:2, :].rearrange('p a b -> p (a b)'), psB[:, 0:256], multi=True)
            ps_pool.release(psB)
            S.tt('dve', NM.v(), IDENT4, PQ.v(), ALU.subtract)
            yield
            psk = ps_pool.alloc()
            psv = ps_pool.alloc()
            for j in range(2):
                S.transpose(psk[:, hs(j)], kT(j), IDENT, multi=(j > 0))
            for j in range(2):
                S.transpose(psv[:, hs(j)], vT(j), IDENT, multi=(j > 0))
            KBG = ur()
            VB = ur()
            KTMt = ur()
            KTM = KTMt.v()
            for j in range(2):
                S.act(KBG[:, j, :], psk[:, hs(j)], AF.Identity, scale=S1[:, g, hh0 + j:hh0 + j + 1], multi=(j > 0))
            S.copy('dve', KTM.rearrange('p a b -> p (a b)'), psk[:, 0:256], multi=True)
            ps_pool.release(psk)
            for j in range(2):
                S.act(VB[:, j, :], psv[:, hs(j)], AF.Identity, scale=BETA[:, g, hh0 + j:hh0 + j + 1], multi=(j > 0))
            ps_pool.release(psv)
            yield
            psQK = ps_pool.alloc()
            for j in range(2):
                S.mm(psQK[:, hs(j)], kT(j), qT(j))
            PT = ub()
            S.tt('dve', flat(PT), psQK[:, 0:256], DXU[:, j0:j0 + 2, :].rearrange('p a b -> p (a b)'), ALU.mult)
            ps_pool.release(psQK)
            QD = ub()
            for j in range(2):
                S.tt('dve', QD[:, j, :], qT(j), EGR[:, j0 + j, :], ALU.mult, multi=(j > 0))
            yield
            NF = None
            for k in range(1, 7):
                ps1 = ps_pool.alloc() if k <= 5 else None
                ps2 = ps_pool.alloc() if k >= 2 else None
                if k <= 5:
                    for j in range(2):
                        S.mm(ps1[:, hs(j)], Qj(j), Pj(j))
                if k <= 4:
                    for j in range(2):
                        S.mm(ps1[:, hs(2 + j)], Pj(j), Qj(j))
                if k >= 2:
                    for j in range(2):
                        S.mm(ps2[:, hs(j)], Mj(j), Pj(j))
                if 2 <= k <= 5:
                    for j in range(2):
                        S.mm(ps2[:, hs(2 + j)], Pj(j), Mj(j))
                if k <= 4:
                    S.copy('act', flat(PQ), ps1[:, 0:512])
                elif k == 5:
                    S.copy('act', PQ[:, 0:2, :].rearrange('p a b -> p (a b)'), ps1[:, 0:256])
                if ps1 is not None:
                    ps_pool.release(ps1)
                if 2 <= k <= 5:
                    S.tt('dve', flat(NM), ps2[:, 0:512], flat(NM), ALU.add)
                elif k == 6:
                    NF = ur()
                    S.tt('dve', flat(NF), ps2[:, 0:256], NM[:, 0:2, :].rearrange('p a b -> p (a b)'), ALU.add)
                if ps2 is not None:
                    ps_pool.release(ps2)
                yield
            ut_pool.release(PQ, NM)
            T1 = ua()
            UB = V(T1.bufs, T1.t[:, 2:4, :])
            psW = ps_pool.alloc()
            for j in range(2):
                S.mm(psW[:, hs(j)], KBG[:, j, :], NF[:, j, :])
            WT = ub()
            S.copy('act', flat(WT), psW[:, 0:256])
            ps_pool.release(psW)
            psU = ps_pool.alloc()
            for j in range(2):
                S.mm(psU[:, hs(j)], NF[:, j, :], VB[:, j, :])
            S.copy('act', UB.rearrange('p a b -> p (a b)'), psU[:, 0:256], multi=True)
            ps_pool.release(psU)
            utr_pool.release(KBG, VB, NF)
            yield
            T2 = ua()
            O = V(T2.bufs, T2.t[:, 0:2, :])
            sq = V(T2.bufs, T2.t[:, 2:4, :])
            Of = O.rearrange('p a b -> p (a b)')
            for b in range(NB):
                si = l if tl.kind == 'p' else b

                def Sv(j):
                    return V([SST.bufs[si]], SST.t[:, si, hh0 + j, :])
                Sb = ub()
                S.copy('act', Sb.v(), V([SST.bufs[si]], SST.t[:, si, hh0:hh0 + 2, :]))
                psWS = ps_pool.alloc()
                for j in range(2):
                    S.mm(psWS[:, hs(j)], WT[:, j, :], Sb[:, j, :])
                U = ub()
                S.tt('dve', flat(U), UB.rearrange('p a b -> p (a b)'), psWS[:, 0:256], ALU.subtract)
                ps_pool.release(psWS)
                KE = ub()
                for j in range(2):
                    S.ts('dve', KE[:, j, :], KTM[:, j, :], S2M[:, g, b, hh0 + j:hh0 + j + 1], multi=(j > 0))
                yield
                psO = ps_pool.alloc()
                for j in range(2):
                    S.mm(psO[:, hs(j)], QD[:, j, :], Sb[:, j, :], start=True, stop=False)
                    S.mm(psO[:, hs(j)], PT[:, j, :], U[:, j, :], start=False, stop=True)
                psS = ps_pool.alloc()
                for j in range(2):
                    S.mm(psS[:, hs(j)], KE[:, j, :], U[:, j, :])
                if b == 0:
                    S.ts('dve', Of, psO[:, 0:256], msk['BM'][:, b:b + 1])
                else:
                    S.stt(Of, psO[:, 0:256], msk['BM'][:, b:b + 1], Of, ALU.mult, ALU.add)
                ps_pool.release(psO)
                for j in range(2):
                    gend = EGR[:, j0 + j, (b + 1) * BL - 1:(b + 1) * BL]
                    S.stt(Sv(j), Sv(j), gend, psS[:, hs(j)], ALU.mult, ALU.add, multi=(j > 0))
                ps_pool.release(psS)
                ubf_pool.release(U, KE, Sb)
                yield
            ubf_pool.release(WT, PT, QD)
            utr_pool.release(KTMt)
            ut_pool.release(T1)
            sm = SMALL.alloc()
            for j in range(2):
                S.stt(sq[:, j, :], O[:, j, :], 1.0, O[:, j, :], ALU.mult, ALU.mult, accum=sm[:, j:j + 1], multi=(j > 0))
            S.act(sm[:, 0:2], sm[:, 0:2], AF.Ln, scale=1.0 / 128.0, bias=C_RMS)
            S.act(sm[:, 0:2], sm[:, 0:2], AF.Exp, scale=-0.5)
            for j in range(2):
                S.ts('dve', sq[:, j, :], O[:, j, :], sm[:, j:j + 1], multi=(j > 0))
            SMALL.release(sm)
            psT = ps_pool.alloc()
            for j in range(2):
                S.transpose(psT[:, hs(j)], sq[:, j, :], IDENT, multi=(j > 0))
            ov = V([BB.bufs[hh0], BB.bufs[hh0 + 1]], BB.t[:, hh0:hh0 + 2, tg], True)
            S.act(ov, psT[:, 0:256].rearrange('p (a b) -> p a b', a=2), AF.Identity, scale=DNG[:, l:l + 1])
            ps_pool.release(psT)
            ut_pool.release(T2)

        def run_interleaved(gens):
            gens = list(gens)
            while gens:
                for gq in list(gens):
                    try:
                        next(gq)
                    except StopIteration:
                        gens.remove(gq)

        def branch_d(tl, l):
            T, NS, L, G = tl.T, tl.NS, tl.L, tl.G
            NB, BL = tl.NB, tl.BL
            msk = CMASK[tl.kind]
            H = 3
            wt = wnext('m', OFF_DA)
            for g in range(G):
                ps = ps_pool.alloc()
                for kc in range(KC):
                    S.mm(ps[:, 0:16], X.sub(kc)[:, g * 128:(g + 1) * 128], wt[:, kc, 0:16],
                         start=(kc == 0), stop=(kc == KC - 1))
                t1 = SMALL.alloc()
                S.tt('dve', t1[:, 0:8], ps[:, 0:8], DTB[:, l, :], ALU.add)
                S.act(t1[:, 0:8], t1[:, 0:8], AF.Exp)
                S.act(t1[:, 0:8], t1[:, 0:8], AF.Ln, bias=C_ONE)
                S.tt('dve', GG[:, g, :], t1[:, 0:8], NEGA[:, l, :], ALU.mult, multi=(g > 0))
                S.act(BETA[:, g, :], ps[:, 8:16], AF.Sigmoid, multi=(g > 0))
                ps_pool.release(ps)
                SMALL.release(t1)
                p2 = ps_pool.alloc()
                S.mm(p2[:, 0:8], msk['MUI'], GG[:, g, :])
                S.mm(p2[:, 8:16], msk['BLK'], GG[:, g, :], start=True, stop=True)
                S.copy('act', GC[:, g, :], p2[:, 0:8], multi=(g > 0))
                eg = SMALL.alloc()
                S.act(eg[:, 0:8], p2[:, 0:8], AF.Exp)
                S.tt('dve', S1[:, g, :], eg[:, 0:8], BETA[:, g, :], ALU.mult, multi=(g > 0))
                S.tt('dve', eg[:, 0:8], p2[:, 8:16], GC[:, g, :], ALU.subtract)
                ps_pool.release(p2)
                S.act(S2[:, g, :], eg[:, 0:8], AF.Exp, multi=(g > 0))
                SMALL.release(eg)
                for b in range(NB):
                    S.ts('dve', S2M[:, g, b, :], S2[:, g, :], msk['BM'][:, b:b + 1], multi=(g > 0 or b > 0))
            wdone(wt)
            if tl.kind == 's':
                for s in range(NSS):
                    S.dma(SST.sub(s), st_delta[l, s].rearrange('h k v -> k h v'), ('sst', s))
            elif tl.first:
                for h in range(8):
                    S.ts('dve', V([SST.bufs[l]], SST.t[:, l, h, :]), IDENT, 0.0, multi=(h > 0))
            for half in range(2):
                dslots = [Buf('dq_s%d' % i) for i in range(4)]
                ndg = 0
                for sec in range(3):
                    emit_d = not (tl.kind == 'p' and not tl.last)
                    ps_rows = ps_pool.alloc() if emit_d else None
                    wt = None
                    for cc in range(4):
                        if cc % 2 == 0:
                            if wt is not None:
                                wdone(wt)
                            wt = wnext('m', OFF_D + sec * D + half * 512 + (cc // 2) * 256)
                        qi = sec * 4 + cc
                        hc = sec * 8 + half * 4 + cc
                        ct = CT[qi % 2]
                        hv = hview(ct.v(), NS, H, L)
                        if tl.kind == 'p':
                            if tl.first:
                                S.ts('dve', hv[:, :, 0:H], IDENT[:, 0:NS * H].rearrange('p (s h) -> p s h', s=NS), 0.0)
                            else:
                                S.copy('dve', hv[:, :, 0:H], HD[:, l, hc, :].rearrange('p (s h) -> p s h', s=1))
                        else:
                            S.copy('dve', hv[:, :, 0:H], STD[:, :, hc, :])
                        ps = ps_pool.alloc()
                        mm8(ps[:, 0:T], wt, cc % 2, X, T)
                        S.copy('act', hv[:, :, H:H + L], ps3(ps, NS, L), multi=True)
                        ps_pool.release(ps)
                        ps2 = ps_pool.alloc()
                        for k in range(4):
                            i = ndg % 4
                            first = ndg < 4
                            ndg += 1
                            ap = BB.t[:, 7, i * 128:(i + 1) * 128]
                            wv = V([dslots[i]] + ([BB.bufs[7]] if first else []), ap, True)
                            rv = V([dslots[i]], ap, True)
                            S.ts('dve', wv, IDENT, par(l, (hc % 8), PR_DCW + k * 3 + sec))
                            S.mm(ps3(ps2, NS, L), rv.r(), hv[:, :, k:k + L].r(), start=(k == 0), stop=(k == 3))
                        S.act(SCR.sub(qi)[:, 0:T], ps2[:, 0:T], AF.Silu)
                        ps_pool.release(ps2)
                        if tl.kind == 'p' and not tl.last:
                            S.copy('dve', HD[:, l, hc, :], hv[:, 0, L:L + H])
                        else:
                            src, gt = rows_src(hv[:, :, L:L + H], NS, H)
                            S.transpose(ps_rows[0:NS * H, cc * 128:(cc + 1) * 128], src, IDENT, multi=(cc > 0))
                            if gt is not None:
                                gr_pool.release(gt)
                    if emit_d:
                        c0 = sec * D + half * 512
                        rows_finish(ps_rows, NS * H, 512, o_d[tl.kind][l, :, :, c0:c0 + 512])
                    wdone(wt)
                S.ts('dve', V(dslots + [BB.bufs[7]], BB.t[:, 7, 511:512], True), IDENT[:, 0:1], 0.0)
                for q0 in (0, 4):
                    ts_ = [tmp_pool.alloc() for _ in range(4)]
                    pss = [ps_pool.alloc() for _ in range(4)]
                    for i in range(4):
                        S.act(ts_[i][:, 0:T], SCR.sub(q0 + i)[:, 0:T], AF.Square)
                    for i in range(4):
                        S.mm(pss[i][:, 0:T], ONES, ts_[i][:, 0:T])
                    for i in range(4):
                        S.act(ts_[i][:, 0:T], pss[i][:, 0:T], AF.Ln, bias=C_L2)
                        ps_pool.release(pss[i])
                    for i in range(4):
                        S.act(ts_[i][:, 0:T], ts_[i][:, 0:T], AF.Exp, scale=-0.5)
                    for i in range(4):
                        src = SCR.sub(q0 + i)[:, 0:T]
                        if q0 == 0:
                            S.stt(src, src, 128.0 ** -0.5, ts_[i][:, 0:T], ALU.mult, ALU.mult)
                        else:
                            S.tt('dve', src, src, ts_[i][:, 0:T], ALU.mult)
                        tmp_pool.release(ts_[i])
                tts = [tmp_pool.alloc() for _ in range(6)]

                def as4(t):
                    return Tile(t.t[:, :].rearrange('p (a b) -> p a b', a=4), t.name + '_v')

                def mk(t):
                    v = as4(t)
                    v.bufs = t.bufs
                    return v
                dsets = [dict(EX=EX, DXL=DXL, DXU=DXU, EGR=EGR),
                         dict(EX=mk(tts[0]), DXL=mk(tts[1]), DXU=mk(tts[2]), EGR=mk(tts[3]))]
                extra = []
                for t in tts[4:6]:
                    for hf in range(2):
                        e = Tile(t.t[:, hf * 256:(hf + 1) * 256].rearrange('p (a b) -> p a b', a=2), t.name + '_e%d' % hf)
                        e.bufs = t.bufs
                        extra.append(e)
                for e in extra:
                    utr_pool.free.append(e)

                def prep_gen(g, ds):
                    EXs, DXLs, DXUs, EGRs = ds['EX'], ds['DXL'], ds['DXU'], ds['EGR']
                    GDs = EGRs
                    for j in range(4):
                        hh = half * 4 + j
                        S.ts('dve', GDs[:, j, :], msk['MUI'], GG[:, g, hh:hh + 1], multi=(j > 0))
                    yield
                    ps = ps_pool.alloc()
                    S.mm(ps[:, :], ONES, GDs.v().rearrange('p a b -> p (a b)'))
                    for j in range(4):
                        hh = half * 4 + j
                        S.ts('dve', EXs[:, j, :], ps[:, j * 128:(j + 1) * 128], GC[:, g, hh:hh + 1], None,
                             ALU.subtract, multi=(j > 0))
                    S.act(EGRs.v().rearrange('p a b -> p (a b)'), ps[:, :], AF.Exp)
                    ps_pool.release(ps)
                    yield
                    exf = EXs.v().rearrange('p a b -> p (a b)')
                    S.stt(exf, exf, -1.0, exf, ALU.mult, ALU.max)
                    S.act(EXs.v(), EXs.v(), AF.Exp, scale=-1.0)
                    yield
                    for j in range(4):
                        S.tt('dve', DXLs[:, j, :], EXs[:, j, :], msk['MLS'], ALU.mult, multi=(j > 0))
                    yield
                    for j in range(4):
                        S.tt('dve', DXUs[:, j, :], EXs[:, j, :], msk['MUI'], ALU.mult, multi=(j > 0))
                    yield

                run_interleaved([prep_gen(0, dsets[0])])
                for g in range(G):
                    gens = [gdn_pair(tl, l, g, half, jp, msk, dsets[g % 2]) for jp in range(2)]
                    if g + 1 < G:
                        gens.append(prep_gen(g + 1, dsets[(g + 1) % 2]))
                    run_interleaved(gens)
                for e in extra:
                    utr_pool.free.remove(e)
                tmp_pool.release(*tts)
            if tl.last:
                if tl.kind == 'p':
                    S.dma(o_delta['p'][l, 0].rearrange('h k v -> k h v'), SST.sub(l), ('sst', l))
                else:
                    for s in range(NSS):
                        S.dma(o_delta['s'][l, s].rearrange('h k v -> k h v'), SST.sub(s), ('sst', s))
            for blk in range(4):
                wt = wnext('m', OFF_DZ + blk * 256)
                for cc in range(2):
                    c = blk * 2 + cc
                    ps = ps_pool.alloc()
                    mm8(ps[:, 0:T], wt, cc, X, T)
                    t = tmp_pool.alloc()
                    S.act(t[:, 0:T], ps[:, 0:T], AF.Silu)
                    ps_pool.release(ps)
                    S.tt('dve', BB.sub(c)[:, 0:T].r(), BB.sub(c)[:, 0:T], t[:, 0:T], ALU.mult)
                    tmp_pool.release(t)
                wdone(wt)
            merge(tl, l, 3)

        def out_proj(tl, l):
            T = tl.T
            for blk in range(4):
                wt = wnext('m', blk * 256)
                for cc in range(2):
                    c = blk * 2 + cc
                    ps = ps_pool.alloc()
                    mm8(ps[:, 0:T], wt, cc, MIX, T)
                    S.stt(BB.sub(c)[:, 0:T], X.sub(c)[:, 0:T], ALPHA, ps[:, 0:T], ALU.mult, ALU.add)
                    ps_pool.release(ps)
                wdone(wt)
            ln_feature(BB, lambda c: X.sub(c)[:, 0:T].r(), T, lambda c: par(l, c, PR_LNG), lambda c: par(l, c, PR_LNB),
                       AF.Identity)

        def load_sample_states(l):
            for s in range(NSS):
                for hf in range(2):
                    cs = slice(hf * 512, (hf + 1) * 512)
                    stg = tmp_pool.alloc()
                    key = ('tmp', tmp_idx[id(stg)])
                    S.dma(stg[0:2, :], st_a[l, s][:, cs], key)
                    S.dma(stg[2:5, :], st_b[l, s][:, cs], key, multi=True)
                    S.dma(stg[5:6, :], st_lru[l, s:s + 1][:, cs], key, multi=True)
                    S.dma(stg[6:36, :], st_c[l, s][:, cs], key, multi=True)
                    ps = ps_pool.alloc()
                    for cc in range(4):
                        S.transpose(ps[:, cc * 36:(cc + 1) * 36], stg[0:36, cc * 128:(cc + 1) * 128], IDENT[0:36, 0:36],
                                    multi=(cc > 0))
                    S.copy('act', STT[:, s, hf * 4:(hf + 1) * 4, :], ps[:, 0:4 * 36].rearrange('p (c k) -> p c k', c=4),
                           multi=(s > 0 or hf > 0))
                    ps_pool.release(ps)
                    tmp_pool.release(stg)
                for sec in range(3):
                    for hf in range(2):
                        stg = tmp_pool.alloc()
                        key = ('tmp', tmp_idx[id(stg)])
                        S.dma(stg[0:3, :], st_d[l, s][:, sec * D + hf * 512:sec * D + (hf + 1) * 512], key)
                        ps = ps_pool.alloc()
                        for cc in range(4):
                            S.transpose(ps[:, cc * 3:(cc + 1) * 3], stg[0:3, cc * 128:(cc + 1) * 128], IDENT[0:3, 0:3],
                                        multi=(cc > 0))
                        h0 = sec * 8 + hf * 4
                        S.copy('act', STD[:, s, h0:h0 + 4, :], ps[:, 0:12].rearrange('p (c k) -> p c k', c=4),
                               multi=(s > 0 or sec > 0 or hf > 0))
                        ps_pool.release(ps)
                        tmp_pool.release(stg)

        XS = V(BA.bufs, BAF[:, 0:4096].rearrange('p (g d) -> p g d', g=4))

        def load_tile(tl):
            G = tl.G
            if tl.kind == 'p':
                src = x_p[tl.idx * TP:(tl.idx + 1) * TP, :].rearrange('(g p) d -> p g d', p=128)
                S.dma(XS, src, 'xs')
            else:
                S.dma(XS[:, 0, :], x_s, 'xs')
            for g in range(G):
                for hf in range(2):
                    S.bn_stats(XST[:, hf, :], XS[:, g, hf * 512:(hf + 1) * 512], multi=(hf > 0))
                S.bn_aggr(XMV[:, 0:2], XST.v().rearrange('p a b -> p (a b)'))
                S.act(XMV[:, 2:3], XMV[:, 1:2], AF.Ln, bias=C_LN)
                S.act(XMV[:, 2:3], XMV[:, 2:3], AF.Exp, scale=-0.5)
                S.ts('dve', XS[:, g, :], XS[:, g, :], XMV[:, 0:1], XMV[:, 2:3], ALU.subtract, ALU.mult)
                for blk in range(2):
                    ps = ps_pool.alloc()
                    for cc in range(4):
                        c = blk * 4 + cc
                        S.transpose(ps[:, cc * 128:(cc + 1) * 128], XS[:, g, c * 128:(c + 1) * 128], IDENT,
                                    multi=(cc > 0))
                    for cc in range(4):
                        c = blk * 4 + cc
                        S.act(X.sub(c)[:, g * 128:(g + 1) * 128].r(), ps[:, cc * 128:(cc + 1) * 128], AF.Identity,
                              scale=PLN[:, c, 0:1], bias=PLN[:, c, 1:2], multi=(g > 0))
                    ps_pool.release(ps)

        def store_tile(tl):
            G = tl.G
            for g in range(G):
                for blk in range(2):
                    ps = ps_pool.alloc()
                    for cc in range(4):
                        c = blk * 4 + cc
                        S.transpose(ps[:, cc * 128:(cc + 1) * 128], X.sub(c)[:, g * 128:(g + 1) * 128], IDENT,
                                    multi=(cc > 0))
                    S.copy('act', XS[:, g, blk * 512:(blk + 1) * 512], ps[:, :], multi=(g > 0 or blk > 0))
                    ps_pool.release(ps)
            if tl.kind == 'p':
                dst = y_p[tl.idx * TP:(tl.idx + 1) * TP, :].rearrange('(g p) d -> p g d', p=128)
                S.dma(dst, XS, 'xs')
            else:
                S.dma(y_s, XS[:, 0, :], 'xs')

        for tl in tiles:
            load_tile(tl)
            for l in range(n_layers):
                if tl.kind == 's':
                    load_sample_states(l)
                branch_a(tl, l)
                branch_b(tl, l)
                branch_c(tl, l)
                branch_d(tl, l)
                out_proj(tl, l)
            store_tile(tl)
        assert not wsched and not winflight
        S.emit()
    return nc


def _consts():
    c = np.zeros((10, 128, 128), np.float32)
    idx = np.arange(128)
    c[0] = np.eye(128)
    c[1] = 1.0
    c[2] = 1.0 / D
    for base, bl in ((3, 64), (6, 32)):
        same = (idx[:, None] // bl) == (idx[None, :] // bl)
        c[base + 0] = same & (idx[None, :] >= idx[:, None])
        c[base + 1] = same & (idx[None, :] < idx[:, None])
        c[base + 2] = same
    for b in range(2):
        c[9, :, b] = (idx // 64) == b
    for b in range(4):
        c[9, :, 2 + b] = (idx // 32) == b
    return c


_NC_CACHE = {}


def kernel(x_prompt, x_sample, state_conv_a, state_conv_b, state_lru, state_conv_c, state_conv_d, state_delta,
           ln_in_g, ln_in_b, w_in, b_gate, a_conv_w, b_conv_w, b_conv_b, b_wx, b_bx, b_wa, b_ba, b_lambda,
           c_conv_w, c_conv_b, c_ln_g, c_ln_b, d_conv_w, d_a_log, d_dt_bias, d_norm_g, w_branch, w_out, ln_g, ln_b):
    f = lambda a: np.ascontiguousarray(np.asarray(a, dtype=np.float32))
    n = 8
    if 'nc' not in _NC_CACHE:
        _NC_CACHE['nc'] = build_program()
    nc = _NC_CACHE['nc']
    rows = np.concatenate([
        f(b_gate), f(a_conv_w), f(b_conv_w), f(b_conv_b)[:, None], f(b_bx).reshape(NL, 1, D),
        f(b_ba).reshape(NL, 1, D), f(b_lambda)[:, None], f(c_conv_w), f(c_conv_b)[:, None], f(c_ln_g)[:, None],
        f(c_ln_b)[:, None], f(d_conv_w).reshape(NL, 4 * 3, D), f(ln_g)[:, None], f(ln_b)[:, None]], axis=1)
    assert rows.shape == (NL, NPAR, D)
    shared = dict(
        ln_in=np.stack([f(ln_in_g), f(ln_in_b)]), w_in=f(w_in), par_rows=np.ascontiguousarray(rows),
        b_wx=f(b_wx), b_wa=f(b_wa), d_alog=f(d_a_log), d_dtb=f(d_dt_bias), d_ng=f(d_norm_g),
        w_br=f(w_branch), w_out=f(w_out), consts=_consts())
    xp, xs = f(x_prompt), f(x_sample)
    sa, sbb, sl, sc, sd, sdl = (f(a) for a in (state_conv_a, state_conv_b, state_lru, state_conv_c, state_conv_d,
                                               state_delta))
    in_maps = []
    for c in range(n):
        q = slice(c * NSS, (c + 1) * NSS)
        m = dict(shared)
        m.update(x_p=xp[c], x_s=np.ascontiguousarray(xs[q].reshape(NSS * LS, D)),
                 st_a=np.ascontiguousarray(sa[:, q]), st_b=np.ascontiguousarray(sbb[:, q]),
                 st_lru=np.ascontiguousarray(sl[:, q]), st_c=np.ascontiguousarray(sc[:, q]),
                 st_d=np.ascontiguousarray(sd[:, q]), st_delta=np.ascontiguousarray(sdl[:, q]))
        in_maps.append(m)
    res = run_bass_kernel_spmd(nc, in_maps, core_ids=list(range(n)))
    R = res.results
    cat = lambda k, ax: np.concatenate([np.asarray(r[k]) for r in R], axis=ax)
    y_prompt = np.stack([np.asarray(r['y_p']) for r in R]).astype(np.float32)
    y_sample = cat('y_s', 0).reshape(n * NSS, LS, D).astype(np.float32)
    outs = [y_prompt, y_sample]
    for pre in ('p', 's'):
        outs.append(cat(pre + '_a', 1))
        outs.append(cat(pre + '_b', 1))
        outs.append(cat(pre + '_lru', 1)[:, :, 0, :])
        outs.append(cat(pre + '_c', 1))
        outs.append(cat(pre + '_d', 1))
        outs.append(cat(pre + '_delta', 1))
    return tuple(np.ascontiguousarray(o, dtype=np.float32) for o in outs)
```

```python
import collections
import contextlib
import numpy as np
import concourse.bass as bass
import concourse.mybir as mybir
from concourse.bass_utils import run_bass_kernel_spmd

F32 = mybir.dt.float32
F32R = mybir.dt.float32r
ALU = mybir.AluOpType
AF = mybir.ActivationFunctionType

D = 1024
NL = 4
KC = 8
SEQ = 4096
TP = 512
NPT = SEQ // TP
NSS = 4
LS = 32
W3 = 3072
OFF_A, OFF_B, OFF_C, OFF_D = 0, 4096, 6144, 9216
OFF_DZ = 12288
OFF_DA = 13312
OFF_G = 13328
N_IN = 17424
ALPHA = (2 * NL) ** 0.25
LN_EPS = 1e-5
RMS_EPS = 1e-6
L2_EPS = 1e-6
LRU_C = 8.0
NPAR = 63
PR_BG, PR_ACW, PR_BCW, PR_BCB, PR_BBX, PR_BBA, PR_LAM, PR_CCW, PR_CCB, PR_CLG, PR_CLB, PR_DCW, PR_LNG, PR_LNB = \
    0, 4, 7, 11, 12, 13, 14, 15, 46, 47, 48, 49, 61, 62

ENGS = ('pe', 'act', 'dve', 'pool', 'sp')
SAME_ENG_DIST = 4


class Buf:
    __slots__ = ('name', 'excl', 'writes', 'reads')

    def __init__(self, name, excl=False):
        self.name = name
        self.excl = excl
        self.writes = []
        self.reads = []


class V:
    __slots__ = ('bufs', 'ap', 'rr')

    def __init__(self, bufs, ap, rr=False):
        self.bufs = tuple(bufs)
        self.ap = ap
        self.rr = rr

    def __getitem__(self, idx):
        return V(self.bufs, self.ap[idx], self.rr)

    def r(self):
        return V(self.bufs, self.ap.bitcast(F32R), self.rr)

    def rearrange(self, pattern, **kw):
        return V(self.bufs, self.ap.rearrange(pattern, **kw), self.rr)

    @property
    def o(self):
        return self.ap.bitcast(F32R) if self.rr else self.ap


class Op:
    __slots__ = ('eng', 'fn', 'waits', 'signal', 'idx', 'sig_val', 'dsem', 'dval')

    def __init__(self, eng, fn):
        self.eng = eng
        self.fn = fn
        self.waits = []
        self.signal = False
        self.idx = -1
        self.sig_val = None
        self.dsem = None
        self.dval = None


class Sched:
    def __init__(self, nc):
        self.nc = nc
        self.ops = {e: [] for e in ENGS}
        self.waited = {e: collections.defaultdict(int) for e in ENGS}
        self.dma_counts = collections.defaultdict(int)

    def _dep(self, op, prod, same_eng_raw=False):
        if prod is op:
            return
        if prod.dsem is not None:
            key = ('d', prod.dsem)
            if self.waited[op.eng][key] >= prod.dval:
                return
            self.waited[op.eng][key] = prod.dval
            op.waits.append((key, prod.dval))
            return
        if prod.eng == op.eng:
            if not same_eng_raw:
                return
            if op.idx - prod.idx >= SAME_ENG_DIST:
                return
        key = ('e', prod.eng)
        if self.waited[op.eng][key] >= prod.idx + 1:
            return
        self.waited[op.eng][key] = prod.idx + 1
        prod.signal = True
        op.waits.append((key, prod))

    @staticmethod
    def _bufs(vs):
        out = []
        for v in vs:
            if v is None:
                continue
            for b in v.bufs:
                if b not in out:
                    out.append(b)
        return out

    def add(self, eng, fn, reads=(), writes=(), dsem=None, multi=False):
        op = Op(eng, fn)
        op.idx = len(self.ops[eng])
        rb = self._bufs(reads)
        wb = self._bufs(writes)
        if dsem is not None:
            self.dma_counts[dsem] += 1
            op.dsem = dsem
            op.dval = 16 * self.dma_counts[dsem]
        saved = {}
        if multi:
            for b in wb:
                if not b.reads:
                    saved[b] = list(b.writes)
                    b.writes = []
        for b in rb:
            if b in wb:
                continue
            for p in b.writes:
                self._dep(op, p, same_eng_raw=True)
            if b.excl:
                for p in b.reads:
                    self._dep(op, p)
        for b in wb:
            for p in b.writes:
                self._dep(op, p, same_eng_raw=(b in rb))
            for p in b.reads:
                self._dep(op, p)
        for b in rb:
            if b not in wb:
                b.reads.append(op)
        for b in wb:
            b.writes = saved.get(b, []) + [op]
            b.reads = []
        self.ops[eng].append(op)
        return op

    def emit(self):
        nc = self.nc
        for e in ENGS:
            n = 0
            for op in self.ops[e]:
                if op.signal:
                    n += 1
                    op.sig_val = n
        dkeys = sorted(self.dma_counts.keys(), key=str)
        with contextlib.ExitStack() as st:
            esem = {e: st.enter_context(nc.semaphore('sem_' + e)) for e in ENGS}
            dsem = {k: st.enter_context(nc.semaphore('dsem_%d' % i)) for i, k in enumerate(dkeys)}
            block = st.enter_context(nc.Block())

            def body(e):
                def f(eng):
                    for op in self.ops[e]:
                        for key, val in op.waits:
                            if key[0] == 'd':
                                eng.wait_ge(dsem[key[1]], val)
                            else:
                                eng.wait_ge(esem[key[1]], val.sig_val)
                        ins = op.fn(eng)
                        if op.dsem is not None:
                            ins.then_inc(dsem[op.dsem], 16)
                        elif op.signal:
                            ins.then_inc(esem[e], 1)
                    if e == 'sp':
                        for k in dkeys:
                            eng.wait_ge(dsem[k], 16 * self.dma_counts[k])
                return f

            block.tensor(body('pe'))
            block.scalar(body('act'))
            block.vector(body('dve'))
            block.gpsimd(body('pool'))
            block.sync(body('sp'))

    def mm(self, out, lhsT, rhs, start=True, stop=True):
        def fn(eng):
            return eng.matmul(out.ap, lhsT.ap, rhs.ap, start=start, stop=stop)
        return self.add('pe', fn, reads=[lhsT, rhs], writes=[out], multi=not start)

    def transpose(self, out, in_, ident, multi=False):
        def fn(eng):
            return eng.transpose(out.ap, in_.ap, ident.ap)
        return self.add('pe', fn, reads=[in_, ident], writes=[out], multi=multi)

    def act(self, out, in_, func, bias=None, scale=None, accum=None, multi=False):
        def fn(eng):
            kw = {}
            if bias is not None:
                kw['bias'] = bias.ap if isinstance(bias, V) else bias
            if scale is not None:
                kw['scale'] = scale.ap if isinstance(scale, V) else scale
            if accum is not None:
                kw['accum_out'] = accum.ap
            return eng.activation(out.o, in_.ap, func, **kw)
        reads = [in_] + [a for a in (bias, scale) if isinstance(a, V)]
        writes = [out] + ([accum] if accum is not None else [])
        return self.add('act', fn, reads=reads, writes=writes, multi=multi)

    def tt(self, e, out, in0, in1, op, multi=False):
        def fn(eng):
            return eng.tensor_tensor(out.o, in0.ap, in1.ap, op)
        return self.add(e, fn, reads=[in0, in1], writes=[out], multi=multi)

    def ts(self, e, out, in0, s1, s2=None, op0=ALU.mult, op1=None, multi=False):
        def fn(eng):
            a1 = s1.ap if isinstance(s1, V) else s1
            a2 = s2.ap if isinstance(s2, V) else s2
            if op1 is None:
                return eng.tensor_scalar(out.o, in0.ap, a1, None, op0)
            return eng.tensor_scalar(out.o, in0.ap, a1, a2, op0, op1)
        reads = [in0] + [a for a in (s1, s2) if isinstance(a, V)]
        return self.add(e, fn, reads=reads, writes=[out], multi=multi)

    def stt(self, out, in0, scalar, in1, op0, op1, multi=False, accum=None):
        def fn(eng):
            sc = scalar.ap if isinstance(scalar, V) else scalar
            if accum is not None:
                return eng.scalar_tensor_tensor(out.o, in0.ap, sc, in1.ap, op0, op1, accum_out=accum.ap)
            return eng.scalar_tensor_tensor(out.o, in0.ap, sc, in1.ap, op0, op1)
        reads = [in0, in1] + ([scalar] if isinstance(scalar, V) else [])
        writes = [out] + ([accum] if accum is not None else [])
        return self.add('dve', fn, reads=reads, writes=writes, multi=multi)

    def scan(self, out, d0, d1, initial, multi=False):
        def fn(eng):
            ini = initial.ap if isinstance(initial, V) else initial
            return eng.tensor_tensor_scan(out.o, d0.ap, d1.ap, ini, ALU.mult, ALU.add)
        reads = [d0, d1] + ([initial] if isinstance(initial, V) else [])
        return self.add('dve', fn, reads=reads, writes=[out], multi=multi)

    def copy(self, e, out, in_, multi=False):
        if e == 'act':
            def fn(eng):
                return eng.copy(out.o, in_.ap)
        else:
            def fn(eng):
                return eng.tensor_copy(out.o, in_.ap)
        return self.add(e, fn, reads=[in_], writes=[out], multi=multi)

    def memset(self, e, out, val, multi=False):
        def fn(eng):
            return eng.memset(out.ap, val)
        return self.add(e, fn, reads=[], writes=[out], multi=multi)

    def recip(self, out, in_, multi=False):
        def fn(eng):
            return eng.reciprocal(out.o, in_.ap)
        return self.add('dve', fn, reads=[in_], writes=[out], multi=multi)

    def bn_stats(self, out, in_, multi=False):
        def fn(eng):
            return eng.bn_stats(out.ap, in_.ap)
        return self.add('dve', fn, reads=[in_], writes=[out], multi=multi)

    def bn_aggr(self, out, in_):
        def fn(eng):
            return eng.bn_aggr(out.ap, in_.ap)
        return self.add('dve', fn, reads=[in_], writes=[out])

    def dma(self, out, in_, dsem, q='sp', multi=False):
        oap = out.o if isinstance(out, V) else out
        iap = in_.ap if isinstance(in_, V) else in_

        def fn(eng):
            return eng.dma_start(out=oap, in_=iap)
        rd = [in_] if isinstance(in_, V) else []
        wr = [out] if isinstance(out, V) else []
        return self.add(q, fn, reads=rd, writes=wr, dsem=dsem, multi=multi)


class Tile:
    def __init__(self, tensor, name, nsub=0, excl=False, rr=False):
        self.t = tensor
        self.name = name
        self.rr = rr
        if nsub:
            self.bufs = [Buf('%s[%d]' % (name, i), excl) for i in range(nsub)]
        else:
            self.bufs = [Buf(name, excl)]

    def v(self):
        return V(self.bufs, self.t[:], self.rr)

    def sub(self, i):
        return V([self.bufs[i]], self.t[:, i], self.rr)

    def __getitem__(self, idx):
        return V(self.bufs, self.t[idx], self.rr)


class TPool:
    def __init__(self, tiles):
        self.free = collections.deque(tiles)

    def alloc(self):
        assert self.free, 'pool exhausted'
        return self.free.popleft()

    def release(self, *ts):
        for t in ts:
            self.free.append(t)


class TileInfo:
    def __init__(self, kind, idx, n_ptiles=NPT):
        self.kind = kind
        self.idx = idx
        if kind == 'p':
            self.T, self.NS, self.L = TP, 1, TP
            self.first = idx == 0
            self.last = idx == n_ptiles - 1
            self.NB, self.BL = 2, 64
        else:
            self.T, self.NS, self.L = NSS * LS, NSS, LS
            self.first = True
            self.last = True
            self.NB, self.BL = 4, 32
        self.G = self.T // 128


def build_program(n_layers=NL, n_ptiles=NPT, do_sample=True):
    nc = bass.Bass('TRN2', target_bir_lowering=False)

    def din(name, shape):
        return nc.dram_tensor(name, list(shape), F32, kind='ExternalInput').ap()

    def dout(name, shape):
        return nc.dram_tensor(name, list(shape), F32, kind='ExternalOutput').ap()

    x_p = din('x_p', [SEQ, D])
    x_s = din('x_s', [NSS * LS, D])
    st_a = din('st_a', [NL, NSS, 2, D])
    st_b = din('st_b', [NL, NSS, 3, D])
    st_lru = din('st_lru', [NL, NSS, D])
    st_c = din('st_c', [NL, NSS, 30, D])
    st_d = din('st_d', [NL, NSS, 3, W3])
    st_delta = din('st_delta', [NL, NSS, 8, 128, 128])
    ln_in = din('ln_in', [2, D])
    w_in = din('w_in', [NL, D, N_IN])
    par_rows = din('par_rows', [NL, NPAR, D])
    b_wx = din('b_wx', [NL, 8, 128, 128])
    b_wa = din('b_wa', [NL, 8, 128, 128])
    d_alog = din('d_alog', [NL, 8])
    d_dtb = din('d_dtb', [NL, 8])
    d_ng = din('d_ng', [NL, 128])
    w_br = din('w_br', [NL, 4, D, D])
    w_out = din('w_out', [NL, D, D])
    consts = din('consts', [10, 128, 128])

    y_p = dout('y_p', [SEQ, D])
    y_s = dout('y_s', [NSS * LS, D])
    o_a = {'p': dout('p_a', [NL, 1, 2, D]), 's': dout('s_a', [NL, NSS, 2, D])}
    o_b = {'p': dout('p_b', [NL, 1, 3, D]), 's': dout('s_b', [NL, NSS, 3, D])}
    o_lru = {'p': dout('p_lru', [NL, 1, 1, D]), 's': dout('s_lru', [NL, NSS, 1, D])}
    o_c = {'p': dout('p_c', [NL, 1, 30, D]), 's': dout('s_c', [NL, NSS, 30, D])}
    o_d = {'p': dout('p_d', [NL, 1, 3, W3]), 's': dout('s_d', [NL, NSS, 3, W3])}
    o_delta = {'p': dout('p_delta', [NL, 1, 8, 128, 128]), 's': dout('s_delta', [NL, NSS, 8, 128, 128])}

    st = contextlib.ExitStack()
    with st:
        S = Sched(nc)

        def sb(name, shape, nsub=0, rr=False):
            return Tile(st.enter_context(nc.sbuf_tensor(name, list(shape), F32)), name, nsub=nsub, rr=rr)

        X = sb('X', [128, KC, TP], nsub=KC, rr=True)
        MIX = sb('MIX', [128, KC, TP], nsub=KC, rr=True)
        BA = sb('BA', [128, KC, TP + 32], nsub=KC)
        BB = sb('BB', [128, KC, TP], nsub=KC, rr=True)
        SCR = sb('SCR', [128, 12, TP], nsub=12, rr=True)
        tmp_pool = TPool([sb('TMP%d' % i, [128, TP]) for i in range(6)])
        w_pool = TPool([sb('WP%d' % i, [128, KC, 256], rr=True) for i in range(4)])
        SST = sb('SST', [128, 4, 8, 128], nsub=4)
        CONST = sb('CONST', [128, 10, 128])
        PAR = sb('PAR', [128, NL, KC, 64])
        PLN = sb('PLN', [128, KC, 2])
        DNG = sb('DNG', [128, NL])
        DTB = sb('DTB', [128, NL, 8])
        NEGA = sb('NEGA', [128, NL, 8])
        CLAM = sb('CLAM', [128, NL, KC])
        CC = sb('CC', [128, 8])
        HA = sb('HA', [128, NL, KC, 2])
        HB = sb('HB', [128, NL, KC, 3])
        HC = sb('HC', [128, NL, KC, 30])
        HD = sb('HD', [128, NL, 24, 3])
        HL = sb('HL', [128, NL, KC])
        STT = sb('STT', [128, NSS, KC, 36])
        STD = sb('STD', [128, NSS, 24, 3])
        gr_pool = TPool([sb('GR%d' % i, [128, 128]) for i in range(4)])
        CT = [sb('CT%d' % i, [128, TP + 8], rr=True) for i in range(2)]
        GB = sb('GB', [128, 4, 16])
        GG = sb('GG', [128, 4, 8])
        BETA = sb('BETA', [128, 4, 8])
        GC = sb('GC', [128, 4, 8])
        S1 = sb('S1', [128, 4, 8])
        S2 = sb('S2', [128, 4, 8])
        S2M = sb('S2M', [128, 4, 4, 8])
        SMALL = TPool([sb('SM%d' % i, [128, 16]) for i in range(6)])
        EX = sb('EX', [128, 4, 128])
        DXL = sb('DXL', [128, 4, 128])
        DXU = sb('DXU', [128, 4, 128])
        EGR = sb('EGR', [128, 4, 128])
        GD = EGR
        ut_pool = TPool([sb('UT%d' % i, [128, 4, 128]) for i in range(4)])
        utr_pool = TPool([sb('UR%d' % i, [128, 2, 128]) for i in range(10)])
        XST = sb('XST', [128, 2, 6])
        XMV = sb('XMV', [128, 4])

        ps_pool = TPool([Tile(st.enter_context(nc.psum_tensor('PS%d' % i, [128, 512], F32)), 'PS%d' % i, excl=True)
                         for i in range(8)])

        IDENT = CONST[:, 0, :]
        ONES = CONST[:, 1, :]
        ONESM = CONST[:, 2, :]
        CMASK = {'p': dict(MUI=CONST[:, 3, :], MLS=CONST[:, 4, :], BLK=CONST[:, 5, :], BM=CONST[:, 9, 0:2]),
                 's': dict(MUI=CONST[:, 6, :], MLS=CONST[:, 7, :], BLK=CONST[:, 8, :], BM=CONST[:, 9, 2:6])}

        def par(l, c, idx):
            return PAR[:, l, c, idx:idx + 1]

        S.dma(CONST.v(), consts.rearrange('k p n -> p k n'), 'const')
        for i, val in enumerate([LN_EPS, RMS_EPS, L2_EPS, 1.0, 0.0]):
            S.memset('dve', CC[:, i:i + 1], val, multi=(i > 0))
        C_LN, C_RMS, C_L2, C_ONE, C_ZERO = (CC[:, i:i + 1] for i in range(5))

        BAF = BA.t[:, :, :].rearrange('p a b -> p (a b)')
        PSTG = V(BA.bufs, BAF[0:64, 0:1024])
        for l in range(n_layers):
            S.dma(PSTG[0:NPAR, :], par_rows[l], 'pstg')
            ps = ps_pool.alloc()
            for c in range(KC):
                S.transpose(ps[:, c * 64:c * 64 + NPAR], PSTG[0:NPAR, c * 128:(c + 1) * 128], IDENT[0:NPAR, 0:NPAR],
                            multi=(c > 0))
            S.copy('act', PAR[:, l, :, 0:NPAR], ps[:, :].rearrange('p (c k) -> p c k', c=KC)[:, :, 0:NPAR],
                   multi=(l > 0))
            ps_pool.release(ps)
        S.dma(PSTG[0:2, :], ln_in, 'pstg')
        ps = ps_pool.alloc()
        for c in range(KC):
            S.transpose(ps[:, c * 2:c * 2 + 2], PSTG[0:2, c * 128:(c + 1) * 128], IDENT[0:2, 0:2], multi=(c > 0))
        S.copy('act', PLN.v(), ps[:, 0:16].rearrange('p (c k) -> p c k', c=KC))
        ps_pool.release(ps)
        S.dma(PSTG[0:NL, 0:128], d_ng, 'pstg')
        ps = ps_pool.alloc()
        S.transpose(ps[:, 0:NL], PSTG[0:NL, 0:128], IDENT[0:NL, 0:NL])
        S.copy('act', DNG.v(), ps[:, 0:NL])
        ps_pool.release(ps)
        for l in range(NL):
            S.dma(DTB[:, l, :], d_dtb[l].partition_broadcast(128), 'dtb', multi=(l > 0))
            S.dma(NEGA[:, l, :], d_alog[l].partition_broadcast(128), 'alog', multi=(l > 0))
        S.act(NEGA.v(), NEGA.v(), AF.Exp)
        S.ts('dve', NEGA.v(), NEGA.v(), -1.0)
        for l in range(n_layers):
            S.act(CLAM[:, l, :], PAR[:, l, :, PR_LAM], AF.Exp, scale=-1.0, multi=(l > 0))
        S.act(CLAM.v(), CLAM.v(), AF.Ln, bias=C_ONE)
        S.ts('dve', CLAM.v(), CLAM.v(), -LRU_C)

        def layer_wdescs(l):
            d = []

            def win(c0, n=256):
                d.append(('m', w_in[l], c0, n))

            def win4(c0):
                for q in range(4):
                    win(c0 + q * 256)
            for br in range(4):
                if br == 0:
                    for sec in (1, 2, 3, 0):
                        win4(OFF_A + sec * D)
                elif br == 1:
                    win4(OFF_B)
                    d.append(('g', b_wx[l]))
                    d.append(('g', b_wa[l]))
                    win4(OFF_B + D)
                elif br == 2:
                    for sec in range(3):
                        win4(OFF_C + sec * D)
                else:
                    win(OFF_DA, 16)
                    for half in range(2):
                        for sec in range(3):
                            win(OFF_D + sec * D + half * 512)
                            win(OFF_D + sec * D + half * 512 + 256)
                    win4(OFF_DZ)
                for blk in range(4):
                    win(OFF_G + br * D + blk * 256)
                    d.append(('m', w_br[l, br], blk * 256, 256))
            for blk in range(4):
                d.append(('m', w_out[l], blk * 256, 256))
            return d

        tiles = [TileInfo('p', i, n_ptiles) for i in range(n_ptiles)] + ([TileInfo('s', 0)] if do_sample else [])
        wsched = collections.deque()
        for tl in tiles:
            for l in range(n_layers):
                wsched.extend(layer_wdescs(l))
        winflight = collections.deque()
        wslot = {id(t): i for i, t in enumerate(w_pool.free)}

        def wissue():
            while wsched and w_pool.free:
                desc = wsched.popleft()
                t = w_pool.alloc()
                if desc[0] == 'm':
                    _, mat, c0, n = desc
                    S.dma(t[:, :, 0:n].r(), mat[:, c0:c0 + n].rearrange('(kc p) n -> p kc n', p=128),
                          ('w', wslot[id(t)]), q='pool')
                else:
                    S.dma(t[:, :, 0:128].r(), desc[1].rearrange('h i j -> i h j'), ('w', wslot[id(t)]), q='pool')
                winflight.append((desc, t))

        def wnext(kind, c0=None):
            if not winflight:
                wissue()
            desc, t = winflight.popleft()
            assert desc[0] == kind and (c0 is None or desc[2] == c0), (desc[0], kind, c0, desc[2:])
            wissue()
            return t

        def wdone(t):
            w_pool.release(t)

        def mm8(ps_v, wt, cc, rhs_tile, T, ncols=128):
            for kc in range(KC):
                S.mm(ps_v, wt[:, kc, cc * 128:cc * 128 + ncols].r(), rhs_tile.sub(kc)[:, 0:T].r(),
                     start=(kc == 0), stop=(kc == KC - 1))

        def hview(tile_v, NS, H, L):
            return tile_v[:, 0:NS * (H + L)].rearrange('p (s l) -> p s l', s=NS)

        def ps3(ps_v, NS, L):
            return ps_v[:, 0:NS * L].rearrange('p (s l) -> p s l', s=NS)

        row_ctr = [0]

        tmp_idx = {id(t): i for i, t in enumerate(tmp_pool.free)}

        def rows_finish(ps, R, n, dst_ap):
            stg = tmp_pool.alloc()
            S.copy('act', stg[0:R, 0:n], ps[0:R, 0:n])
            ps_pool.release(ps)
            S.dma(dst_ap.rearrange('s h n -> (s h) n'), stg[0:R, 0:n], ('tmp', tmp_idx[id(stg)]))
            tmp_pool.release(stg)

        def rows_src(v3, NS_, H_):
            if NS_ == 1:
                return v3[:, 0, :], None
            gt = gr_pool.alloc()
            S.copy('dve', gt[:, 0:NS_ * H_].rearrange('p (s h) -> p s h', s=NS_), v3)
            return gt[:, 0:NS_ * H_], gt

        def emit_rows(src_fn, R, chunks, dst_ap, NS_=1):
            ps = ps_pool.alloc()
            for i, c in enumerate(chunks):
                src, gt = rows_src(src_fn(c), NS_, R // NS_)
                S.transpose(ps[0:R, i * 128:(i + 1) * 128], src, IDENT, multi=(i > 0))
                if gt is not None:
                    gr_pool.release(gt)
            rows_finish(ps, R, len(chunks) * 128, dst_ap)

        def ln_feature(src, dst_fn, T, gcol, bcol, func):
            pm = ps_pool.alloc()
            pq = ps_pool.alloc()
            for c in range(KC):
                S.mm(pm[:, 0:T], ONESM, src.sub(c)[:, 0:T], start=(c == 0), stop=(c == KC - 1))
            for c in range(KC):
                sq = tmp_pool.alloc()
                S.act(sq[:, 0:T], src.sub(c)[:, 0:T], AF.Square)
                S.mm(pq[:, 0:T], ONESM, sq[:, 0:T], start=(c == 0), stop=(c == KC - 1))
                tmp_pool.release(sq)
            mean = tmp_pool.alloc()
            rstd = tmp_pool.alloc()
            S.copy('act', mean[:, 0:T], pm[:, 0:T])
            ps_pool.release(pm)
            S.tt('dve', rstd[:, 0:T], mean[:, 0:T], mean[:, 0:T], ALU.mult)
            S.tt('dve', rstd[:, 0:T], pq[:, 0:T], rstd[:, 0:T], ALU.subtract)
            ps_pool.release(pq)
            S.act(rstd[:, 0:T], rstd[:, 0:T], AF.Ln, bias=C_LN)
            S.act(rstd[:, 0:T], rstd[:, 0:T], AF.Exp, scale=-0.5)
            for c in range(KC):
                t = tmp_pool.alloc()
                S.tt('dve', t[:, 0:T], src.sub(c)[:, 0:T], mean[:, 0:T], ALU.subtract)
                S.tt('dve', t[:, 0:T], t[:, 0:T], rstd[:, 0:T], ALU.mult)
                S.act(dst_fn(c), t[:, 0:T], func, scale=gcol(c), bias=bcol(c))
                tmp_pool.release(t)
            tmp_pool.release(mean, rstd)

        def load_hist(tl, l, H, hist_tile, stt_off, bf=None):
            bf = bf or BA.sub
            NS, L = tl.NS, tl.L
            for c in range(KC):
                hv = hview(bf(c), NS, H, L)[:, :, 0:H]
                if tl.kind == 'p':
                    if tl.first:
                        S.memset('dve', hv, 0.0)
                    else:
                        S.copy('dve', hv, hist_tile[:, l, c, :].rearrange('p (s h) -> p s h', s=1))
                else:
                    S.copy('dve', hv, STT[:, :, c, stt_off:stt_off + H])

        def save_hist(tl, l, H, hist_tile, out_ap, c_list=range(KC), bf=None):
            bf = bf or BA.sub
            NS, L = tl.NS, tl.L
            if tl.kind == 'p' and not tl.last:
                for c in c_list:
                    S.copy('dve', hist_tile[:, l, c, :], hview(bf(c), NS, H, L)[:, 0, L:L + H])
            else:
                for blk in range(2):
                    chunks = list(range(blk * 4, blk * 4 + 4))
                    emit_rows(lambda c: hview(bf(c), NS, H, L)[:, :, L:L + H], NS * H, chunks,
                              out_ap[tl.kind][l, :, :, blk * 512:(blk + 1) * 512], NS_=NS)

        def merge(tl, l, br):
            T = tl.T
            for blk in range(4):
                wg = wnext('m', OFF_G + br * D + blk * 256)
                gs = []
                for cc in range(2):
                    m = blk * 2 + cc
                    ps = ps_pool.alloc()
                    mm8(ps[:, 0:T], wg, cc, X, T)
                    g = tmp_pool.alloc()
                    S.act(g[:, 0:T], ps[:, 0:T], AF.Sigmoid, bias=par(l, m, PR_BG + br))
                    ps_pool.release(ps)
                    gs.append(g)
                wdone(wg)
                wb = wnext('m', blk * 256)
                for cc in range(2):
                    m = blk * 2 + cc
                    ps = ps_pool.alloc()
                    mm8(ps[:, 0:T], wb, cc, BB, T)
                    if br == 0:
                        S.tt('dve', MIX.sub(m)[:, 0:T], gs[cc][:, 0:T], ps[:, 0:T], ALU.mult)
                    else:
                        S.tt('dve', gs[cc][:, 0:T], gs[cc][:, 0:T], ps[:, 0:T], ALU.mult)
                        o = MIX.sub(m)[:, 0:T]
                        S.tt('dve', o.r() if br == 3 else o, MIX.sub(m)[:, 0:T], gs[cc][:, 0:T], ALU.add)
                    ps_pool.release(ps)
                    tmp_pool.release(gs[cc])
                wdone(wb)

        def branch_a(tl, l):
            T, NS, L = tl.T, tl.NS, tl.L
            H = 2
            load_hist(tl, l, H, HA, 0)
            for blk in range(4):
                wt = wnext('m', OFF_A + 1 * D + blk * 256)
                for cc in range(2):
                    c = blk * 2 + cc
                    ps = ps_pool.alloc()
                    mm8(ps[:, 0:T], wt, cc, X, T)
                    S.copy('act', hview(BA.sub(c), NS, H, L)[:, :, H:H + L], ps3(ps, NS, L))
                    ps_pool.release(ps)
                wdone(wt)
            for blk in range(4):
                wt = wnext('m', OFF_A + 2 * D + blk * 256)
                for cc in range(2):
                    c = blk * 2 + cc
                    ps = ps_pool.alloc()
                    mm8(ps[:, 0:T], wt, cc, X, T)
                    hv = hview(BA.sub(c), NS, H, L)
                    S.tt('dve', hv[:, :, H:H + L], hv[:, :, H:H + L], ps3(ps, NS, L), ALU.mult)
                    ps_pool.release(ps)
                    ov = ps3(BB.sub(c), NS, L)
                    S.ts('dve', ov, hv[:, :, 0:L], par(l, c, PR_ACW + 0))
                    for k in (1, 2):
                        S.stt(ov, hv[:, :, k:k + L], par(l, c, PR_ACW + k), ov, ALU.mult, ALU.add)
                wdone(wt)
            save_hist(tl, l, H, HA, o_a)
            for blk in range(4):
                wt = wnext('m', OFF_A + 3 * D + blk * 256)
                for cc in range(2):
                    c = blk * 2 + cc
                    ps = ps_pool.alloc()
                    mm8(ps[:, 0:T], wt, cc, X, T)
                    t = tmp_pool.alloc()
                    S.act(t[:, 0:T], ps[:, 0:T], AF.Silu)
                    ps_pool.release(ps)
                    S.tt('dve', BB.sub(c)[:, 0:T], BB.sub(c)[:, 0:T], t[:, 0:T], ALU.mult)
                    tmp_pool.release(t)
                wdone(wt)
            for blk in range(4):
                wt = wnext('m', OFF_A + 0 * D + blk * 256)
                for cc in range(2):
                    c = blk * 2 + cc
                    ps = ps_pool.alloc()
                    mm8(ps[:, 0:T], wt, cc, X, T)
                    S.tt('dve', BB.sub(c)[:, 0:T].r(), BB.sub(c)[:, 0:T], ps[:, 0:T], ALU.mult)
                    ps_pool.release(ps)
                wdone(wt)
            merge(tl, l, 0)

        def branch_b(tl, l):
            T, NS, L = tl.T, tl.NS, tl.L
            H = 3
            load_hist(tl, l, H, HB, 2)
            for blk in range(4):
                wt = wnext('m', OFF_B + blk * 256)
                for cc in range(2):
                    c = blk * 2 + cc
                    ps = ps_pool.alloc()
                    mm8(ps[:, 0:T], wt, cc, X, T)
                    hv = hview(BA.sub(c), NS, H, L)
                    S.copy('act', hv[:, :, H:H + L], ps3(ps, NS, L))
                    ps_pool.release(ps)
                    ov = ps3(BB.sub(c), NS, L)
                    S.ts('dve', ov, hv[:, :, 0:L], par(l, c, PR_BCW), par(l, c, PR_BCB), ALU.mult, ALU.add)
                    for k in (1, 2, 3):
                        o2 = ov.r() if k == 3 else ov
                        S.stt(o2, hv[:, :, k:k + L], par(l, c, PR_BCW + k), ov, ALU.mult, ALU.add)
                wdone(wt)
            save_hist(tl, l, H, HB, o_b)
            wx = wnext('g')
            wa = wnext('g')
            for c0 in (0, 4):
                gas = [tmp_pool.alloc() for _ in range(4)]
                for i in range(4):
                    c = c0 + i
                    xb = BB.sub(c)[:, 0:T]
                    p1 = ps_pool.alloc()
                    p2 = ps_pool.alloc()
                    S.mm(p1[:, 0:T], wx[:, c, 0:128].r(), xb.r())
                    S.mm(p2[:, 0:T], wa[:, c, 0:128].r(), xb.r())
                    S.act(BA.sub(i)[:, 0:T], p1[:, 0:T], AF.Sigmoid, bias=par(l, c, PR_BBX))
                    S.act(gas[i][:, 0:T], p2[:, 0:T], AF.Sigmoid, bias=par(l, c, PR_BBA))
                    ps_pool.release(p1, p2)
                for i in range(4):
                    c = c0 + i
                    S.act(gas[i][:, 0:T], gas[i][:, 0:T], AF.Exp, scale=CLAM[:, l, c:c + 1])
                for i in range(4):
                    t3 = BA.sub(4 + i)[:, 0:T]
                    S.tt('dve', t3, gas[i][:, 0:T], gas[i][:, 0:T], ALU.mult)
                    S.ts('dve', t3, t3, -1.0, 1.0, ALU.mult, ALU.add)
                    S.ts('dve', t3, t3, 1e-18, None, ALU.max)
                for i in range(4):
                    t3 = BA.sub(4 + i)[:, 0:T]
                    S.act(t3, t3, AF.Ln)
                for i in range(4):
                    t3 = BA.sub(4 + i)[:, 0:T]
                    S.act(t3, t3, AF.Exp, scale=0.5)
                for i in range(4):
                    c = c0 + i
                    gx = BA.sub(i)[:, 0:T]
                    t3 = BA.sub(4 + i)[:, 0:T]
                    xb = BB.sub(c)[:, 0:T]
                    S.tt('dve', gx, gx, t3, ALU.mult)
                    S.tt('dve', gx, gx, xb, ALU.mult)
                    for s_ in range(NS):
                        if tl.kind == 'p':
                            ini = 0.0 if tl.first else HL[:, l, c:c + 1]
                        else:
                            ini = STT[:, s_, c, 5:6]
                        S.scan(BB.sub(c)[:, s_ * L:(s_ + 1) * L], gas[i][:, s_ * L:(s_ + 1) * L],
                               BA.sub(i)[:, s_ * L:(s_ + 1) * L], ini, multi=(s_ > 0))
                    if tl.kind == 'p' and not tl.last:
                        S.copy('dve', HL[:, l, c:c + 1], BB.sub(c)[:, L - 1:L])
                tmp_pool.release(*gas)
            wdone(wx)
            wdone(wa)
            if tl.last:
                for blk in range(2):
                    chunks = list(range(blk * 4, blk * 4 + 4))
                    emit_rows(lambda c: ps3(BB.sub(c), NS, L)[:, :, L - 1:L], NS, chunks,
                              o_lru[tl.kind][l, :, :, blk * 512:(blk + 1) * 512], NS_=NS)
            for blk in range(4):
                wt = wnext('m', OFF_B + D + blk * 256)
                for cc in range(2):
                    c = blk * 2 + cc
                    ps = ps_pool.alloc()
                    mm8(ps[:, 0:T], wt, cc, X, T)
                    t = tmp_pool.alloc()
                    S.act(t[:, 0:T], ps[:, 0:T], AF.Silu)
                    ps_pool.release(ps)
                    S.tt('dve', BB.sub(c)[:, 0:T].r(), BB.sub(c)[:, 0:T], t[:, 0:T], ALU.mult)
                    tmp_pool.release(t)
                wdone(wt)
            merge(tl, l, 1)

        SCRF = SCR.t[:, :, :].rearrange('p a b -> p (a b)')

        def cb(c):
            return V(SCR.bufs, SCRF[:, c * 544:(c + 1) * 544], True)

        def branch_c(tl, l):
            T, NS, L = tl.T, tl.NS, tl.L
            H = 30
            load_hist(tl, l, H, HC, 6, bf=cb)
            for blk in range(4):
                wt = wnext('m', OFF_C + blk * 256)
                for cc in range(2):
                    c = blk * 2 + cc
                    ps = ps_pool.alloc()
                    mm8(ps[:, 0:T], wt, cc, X, T)
                    S.copy('act', hview(cb(c), NS, H, L)[:, :, H:H + L], ps3(ps, NS, L))
                    ps_pool.release(ps)
                wdone(wt)
            for blk in range(4):
                wt = wnext('m', OFF_C + D + blk * 256)
                for cc in range(2):
                    c = blk * 2 + cc
                    ps = ps_pool.alloc()
                    mm8(ps[:, 0:T], wt, cc, X, T)
                    t = tmp_pool.alloc()
                    S.act(t[:, 0:T], ps[:, 0:T], AF.Sigmoid)
                    ps_pool.release(ps)
                    hv = hview(cb(c), NS, H, L)
                    S.tt('dve', hv[:, :, H:H + L], hv[:, :, H:H + L], ps3(t, NS, L), ALU.mult)
                    tmp_pool.release(t)
                wdone(wt)
            NSLOT = 12
            slots = [Buf('dg_s%d' % i) for i in range(NSLOT)]
            n = 0
            for c in range(KC):
                hv = hview(cb(c), NS, H, L)
                ps = ps_pool.alloc()
                for k in range(31):
                    i = n % NSLOT
                    first = n < NSLOT
                    n += 1
                    ap = SCRF[:, 4352 + i * 128:4352 + (i + 1) * 128]
                    wv = V([slots[i]] + (list(SCR.bufs) if first else []), ap, True)
                    rv = V([slots[i]], ap, True)
                    S.ts('dve', wv, IDENT, par(l, c, PR_CCW + k))
                    S.mm(ps3(ps, NS, L), rv.r(), hv[:, :, k:k + L].r(), start=(k == 0), stop=(k == 30))
                S.act(BB.sub(c)[:, 0:T], ps[:, 0:T], AF.Identity, bias=par(l, c, PR_CCB))
                ps_pool.release(ps)
            S.ts('dve', V(slots + list(SCR.bufs), SCRF[:, 6143:6144], True), IDENT[:, 0:1], 0.0)
            save_hist(tl, l, H, HC, o_c, bf=cb)
            ln_feature(BB, lambda c: BB.sub(c)[:, 0:T], T, lambda c: par(l, c, PR_CLG), lambda c: par(l, c, PR_CLB),
                       AF.Silu)
            for blk in range(4):
                wt = wnext('m', OFF_C + 2 * D + blk * 256)
                for cc in range(2):
                    c = blk * 2 + cc
                    ps = ps_pool.alloc()
                    mm8(ps[:, 0:T], wt, cc, X, T)
                    t = tmp_pool.alloc()
                    S.act(t[:, 0:T], ps[:, 0:T], AF.Silu)
                    ps_pool.release(ps)
                    S.tt('dve', BB.sub(c)[:, 0:T].r(), BB.sub(c)[:, 0:T], t[:, 0:T], ALU.mult)
                    tmp_pool.release(t)
                wdone(wt)
            merge(tl, l, 2)

        def gdn_pair(tl, l, g, half, jp, msk, ds):
            NB, BL = tl.NB, tl.BL
            DXL, DXU, EGR = ds['DXL'], ds['DXU'], ds['EGR']
            j0 = 2 * jp
            hh0 = half * 4 + j0
            tg = slice(g * 128, (g + 1) * 128)
            ua, ur = ut_pool.alloc, utr_pool.alloc

            def qT(j):
                return SCR.sub(j0 + j)[:, tg]

            def kT(j):
                return SCR.sub(4 + j0 + j)[:, tg]

            def vT(j):
                return SCR.sub(8 + j0 + j)[:, tg]

            def hs(j):
                return slice(j * 128, (j + 1) * 128)

            def flat(t):
                return t.v().rearrange('p a b -> p (a b)')
            IDENT4 = V(IDENT.bufs, IDENT.ap.unsqueeze(1).broadcast_to([128, 4, 128]))
            PQ = ua()
            NM = ua()

            def Pj(j):
                return PQ[:, j, :]

            def Qj(j):
                return PQ[:, 2 + j, :]

            def Nj(j):
                return NM[:, j, :]

            def Mj(j):
                return NM[:, 2 + j, :]
            psG = ps_pool.alloc()
            for j in range(2):
                S.mm(psG[:, hs(j)], kT(j), kT(j))
            for j in range(2):
                S.stt(Qj(j), psG[:, hs(j)], BETA[:, g, hh0 + j:hh0 + j + 1], DXL[:, j0 + j, :], ALU.mult, ALU.mult,
                      multi=True)
            ps_pool.release(psG)
            yield
            psB = ps_pool.alloc()
            for j in range(2):
                S.transpose(psB[:, hs(j)], Qj(j), IDENT, multi=(j > 0))
            S.copy('act', PQ[:, 0:2, :].rearrange('p a b -> p (a b)'), psB[:, 0:256], multi=True)
            ps_pool.release(psB)
            S.tt('dve', NM[:, 0:2, :], IDENT4[:, 0:2, :], PQ[:, 0:2, :], ALU.subtract)
            yield
            psk = ps_pool.alloc()
            psv = ps_pool.alloc()
            for j in range(2):
                S.transpose(psk[:, hs(j)], kT(j), IDENT, multi=(j > 0))
            for j in range(2):
                S.transpose(psv[:, hs(j)], vT(j), IDENT, multi=(j > 0))
            KBG = ur()
            VB = ur()
            KTMt = ur()
            KTM = KTMt.v()
            for j in range(2):
                S.act(KBG[:, j, :], psk[:, hs(j)], AF.Identity, scale=S1[:, g, hh0 + j:hh0 + j + 1], multi=(j > 0))
            S.copy('dve', KTM.rearrange('p a b -> p (a b)'), psk[:, 0:256], multi=True)
            ps_pool.release(psk)
            for j in range(2):
                S.act(VB[:, j, :], psv[:, hs(j)], AF.Identity, scale=BETA[:, g, hh0 + j:hh0 + j + 1], multi=(j > 0))
            ps_pool.release(psv)
            yield
            psQK = ps_pool.alloc()
            for j in range(2):
                S.mm(psQK[:, hs(j)], kT(j), qT(j))
            PT = ur()
            S.tt('dve', flat(PT), psQK[:, 0:256], DXU[:, j0:j0 + 2, :].rearrange('p a b -> p (a b)'), ALU.mult)
            ps_pool.release(psQK)
            QD = ur()
            for j in range(2):
                S.tt('dve', QD[:, j, :], qT(j), EGR[:, j0 + j, :], ALU.mult, multi=(j > 0))
            yield
            NF = None
            for k in range(1, 7):
                ps1 = ps_pool.alloc() if k <= 5 else None
                ps2 = ps_pool.alloc() if k >= 2 else None
                if k <= 4:
                    for j in range(2):
                        S.mm(ps1[:, hs(j)], Qj(j), Pj(j))
                if k <= 5:
                    for j in range(2):
                        S.mm(ps1[:, hs(2 + j)], Pj(j), Qj(j))
                if k >= 2:
                    for j in range(2):
                        S.mm(ps2[:, hs(j)], Qj(j), Nj(j))
                if k <= 4:
                    S.copy('act', flat(PQ), ps1[:, 0:512])
                elif k == 5:
                    S.copy('act', PQ[:, 2:4, :].rearrange('p a b -> p (a b)'), ps1[:, 256:512])
                if ps1 is not None:
                    ps_pool.release(ps1)
                nfl = NM[:, 0:2, :].rearrange('p a b -> p (a b)')
                if 2 <= k <= 5:
                    S.tt('dve', nfl, ps2[:, 0:256], nfl, ALU.add)
                elif k == 6:
                    NF = ur()
                    S.tt('dve', flat(NF), ps2[:, 0:256], nfl, ALU.add)
                if ps2 is not None:
                    ps_pool.release(ps2)
                yield
            ut_pool.release(PQ, NM)
            T1 = ua()
            UB = V(T1.bufs, T1.t[:, 2:4, :])
            psW = ps_pool.alloc()
            for j in range(2):
                S.mm(psW[:, hs(j)], KBG[:, j, :], NF[:, j, :])
            WT = ur()
            S.copy('act', flat(WT), psW[:, 0:256])
            ps_pool.release(psW)
            psU = ps_pool.alloc()
            for j in range(2):
                S.mm(psU[:, hs(j)], NF[:, j, :], VB[:, j, :])
            S.copy('act', UB.rearrange('p a b -> p (a b)'), psU[:, 0:256], multi=True)
            ps_pool.release(psU)
            utr_pool.release(KBG, VB, NF)
            yield
            T2 = ua()
            O = V(T2.bufs, T2.t[:, 0:2, :])
            sq = V(T2.bufs, T2.t[:, 2:4, :])
            Of = O.rearrange('p a b -> p (a b)')
            for b in range(NB):
                si = l if tl.kind == 'p' else b

                def Sv(j):
                    return V([SST.bufs[si]], SST.t[:, si, hh0 + j, :])
                psWS = ps_pool.alloc()
                for j in range(2):
                    S.mm(psWS[:, hs(j)], WT[:, j, :], Sv(j))
                U = ur()
                S.tt('dve', flat(U), UB.rearrange('p a b -> p (a b)'), psWS[:, 0:256], ALU.subtract)
                ps_pool.release(psWS)
                KE = ur()
                for j in range(2):
                    S.ts('dve', KE[:, j, :], KTM[:, j, :], S2M[:, g, b, hh0 + j:hh0 + j + 1], multi=(j > 0))
                yield
                psO = ps_pool.alloc()
                for j in range(2):
                    S.mm(psO[:, hs(j)], QD[:, j, :], Sv(j), start=True, stop=False)
                    S.mm(psO[:, hs(j)], PT[:, j, :], U[:, j, :], start=False, stop=True)
                psS = ps_pool.alloc()
                for j in range(2):
                    S.mm(psS[:, hs(j)], KE[:, j, :], U[:, j, :])
                if b == 0:
                    S.ts('dve', Of, psO[:, 0:256], msk['BM'][:, b:b + 1])
                else:
                    S.stt(Of, psO[:, 0:256], msk['BM'][:, b:b + 1], Of, ALU.mult, ALU.add)
                ps_pool.release(psO)
                for j in range(2):
                    gend = EGR[:, j0 + j, (b + 1) * BL - 1:(b + 1) * BL]
                    S.stt(Sv(j), Sv(j), gend, psS[:, hs(j)], ALU.mult, ALU.add, multi=(j > 0))
                ps_pool.release(psS)
                utr_pool.release(U, KE)
                yield
            utr_pool.release(WT, PT, QD, KTMt)
            ut_pool.release(T1)
            sm = SMALL.alloc()
            for j in range(2):
                S.stt(sq[:, j, :], O[:, j, :], 1.0, O[:, j, :], ALU.mult, ALU.mult, accum=sm[:, j:j + 1], multi=(j > 0))
            S.act(sm[:, 0:2], sm[:, 0:2], AF.Ln, scale=1.0 / 128.0, bias=C_RMS)
            S.act(sm[:, 0:2], sm[:, 0:2], AF.Exp, scale=-0.5)
            for j in range(2):
                S.ts('dve', sq[:, j, :], O[:, j, :], sm[:, j:j + 1], multi=(j > 0))
            SMALL.release(sm)
            psT = ps_pool.alloc()
            for j in range(2):
                S.transpose(psT[:, hs(j)], sq[:, j, :], IDENT, multi=(j > 0))
            ov = V([BB.bufs[hh0], BB.bufs[hh0 + 1]], BB.t[:, hh0:hh0 + 2, tg], True)
            S.act(ov, psT[:, 0:256].rearrange('p (a b) -> p a b', a=2), AF.Identity, scale=DNG[:, l:l + 1])
            ps_pool.release(psT)
            ut_pool.release(T2)

        def run_interleaved(gens):
            gens = list(gens)
            while gens:
                for gq in list(gens):
                    try:
                        next(gq)
                    except StopIteration:
                        gens.remove(gq)

        def branch_d(tl, l):
            T, NS, L, G = tl.T, tl.NS, tl.L, tl.G
            NB, BL = tl.NB, tl.BL
            msk = CMASK[tl.kind]
            H = 3
            wt = wnext('m', OFF_DA)
            for g in range(G):
                ps = ps_pool.alloc()
                for kc in range(KC):
                    S.mm(ps[:, 0:16], X.sub(kc)[:, g * 128:(g + 1) * 128], wt[:, kc, 0:16],
                         start=(kc == 0), stop=(kc == KC - 1))
                t1 = SMALL.alloc()
                S.tt('dve', t1[:, 0:8], ps[:, 0:8], DTB[:, l, :], ALU.add)
                S.act(t1[:, 0:8], t1[:, 0:8], AF.Exp)
                S.act(t1[:, 0:8], t1[:, 0:8], AF.Ln, bias=C_ONE)
                S.tt('dve', GG[:, g, :], t1[:, 0:8], NEGA[:, l, :], ALU.mult, multi=(g > 0))
                S.act(BETA[:, g, :], ps[:, 8:16], AF.Sigmoid, multi=(g > 0))
                ps_pool.release(ps)
                SMALL.release(t1)
                p2 = ps_pool.alloc()
                S.mm(p2[:, 0:8], msk['MUI'], GG[:, g, :])
                S.mm(p2[:, 8:16], msk['BLK'], GG[:, g, :], start=True, stop=True)
                S.copy('act', GC[:, g, :], p2[:, 0:8], multi=(g > 0))
                eg = SMALL.alloc()
                S.act(eg[:, 0:8], p2[:, 0:8], AF.Exp)
                S.tt('dve', S1[:, g, :], eg[:, 0:8], BETA[:, g, :], ALU.mult, multi=(g > 0))
                S.tt('dve', eg[:, 0:8], p2[:, 8:16], GC[:, g, :], ALU.subtract)
                ps_pool.release(p2)
                S.act(S2[:, g, :], eg[:, 0:8], AF.Exp, multi=(g > 0))
                SMALL.release(eg)
                for b in range(NB):
                    S.ts('dve', S2M[:, g, b, :], S2[:, g, :], msk['BM'][:, b:b + 1], multi=(g > 0 or b > 0))
            wdone(wt)
            if tl.kind == 's':
                for s in range(NSS):
                    S.dma(SST.sub(s), st_delta[l, s].rearrange('h k v -> k h v'), ('sst', s))
            elif tl.first:
                for h in range(8):
                    S.ts('dve', V([SST.bufs[l]], SST.t[:, l, h, :]), IDENT, 0.0, multi=(h > 0))
            for half in range(2):
                dslots = [Buf('dq_s%d' % i) for i in range(4)]
                ndg = 0
                for sec in range(3):
                    emit_d = not (tl.kind == 'p' and not tl.last)
                    ps_rows = ps_pool.alloc() if emit_d else None
                    wt = None
                    for cc in range(4):
                        if cc % 2 == 0:
                            if wt is not None:
                                wdone(wt)
                            wt = wnext('m', OFF_D + sec * D + half * 512 + (cc // 2) * 256)
                        qi = sec * 4 + cc
                        hc = sec * 8 + half * 4 + cc
                        ct = CT[qi % 2]
                        hv = hview(ct.v(), NS, H, L)
                        if tl.kind == 'p':
                            if tl.first:
                                S.ts('dve', hv[:, :, 0:H], IDENT[:, 0:NS * H].rearrange('p (s h) -> p s h', s=NS), 0.0)
                            else:
                                S.copy('dve', hv[:, :, 0:H], HD[:, l, hc, :].rearrange('p (s h) -> p s h', s=1))
                        else:
                            S.copy('dve', hv[:, :, 0:H], STD[:, :, hc, :])
                        ps = ps_pool.alloc()
                        mm8(ps[:, 0:T], wt, cc % 2, X, T)
                        S.copy('act', hv[:, :, H:H + L], ps3(ps, NS, L), multi=True)
                        ps_pool.release(ps)
                        ps2 = ps_pool.alloc()
                        for k in range(4):
                            i = ndg % 4
                            first = ndg < 4
                            ndg += 1
                            ap = BB.t[:, 7, i * 128:(i + 1) * 128]
                            wv = V([dslots[i]] + ([BB.bufs[7]] if first else []), ap, True)
                            rv = V([dslots[i]], ap, True)
                            S.ts('dve', wv, IDENT, par(l, (hc % 8), PR_DCW + k * 3 + sec))
                            S.mm(ps3(ps2, NS, L), rv.r(), hv[:, :, k:k + L].r(), start=(k == 0), stop=(k == 3))
                        S.act(SCR.sub(qi)[:, 0:T], ps2[:, 0:T], AF.Silu)
                        ps_pool.release(ps2)
                        if tl.kind == 'p' and not tl.last:
                            S.copy('dve', HD[:, l, hc, :], hv[:, 0, L:L + H])
                        else:
                            src, gt = rows_src(hv[:, :, L:L + H], NS, H)
                            S.transpose(ps_rows[0:NS * H, cc * 128:(cc + 1) * 128], src, IDENT, multi=(cc > 0))
                            if gt is not None:
                                gr_pool.release(gt)
                    if emit_d:
                        c0 = sec * D + half * 512
                        rows_finish(ps_rows, NS * H, 512, o_d[tl.kind][l, :, :, c0:c0 + 512])
                    wdone(wt)
                S.ts('dve', V(dslots + [BB.bufs[7]], BB.t[:, 7, 511:512], True), IDENT[:, 0:1], 0.0)
                for q0 in (0, 4):
                    ts_ = [tmp_pool.alloc() for _ in range(4)]
                    pss = [ps_pool.alloc() for _ in range(4)]
                    for i in range(4):
                        S.act(ts_[i][:, 0:T], SCR.sub(q0 + i)[:, 0:T], AF.Square)
                    for i in range(4):
                        S.mm(pss[i][:, 0:T], ONES, ts_[i][:, 0:T])
                    for i in range(4):
                        S.act(ts_[i][:, 0:T], pss[i][:, 0:T], AF.Ln, bias=C_L2)
                        ps_pool.release(pss[i])
                    for i in range(4):
                        S.act(ts_[i][:, 0:T], ts_[i][:, 0:T], AF.Exp, scale=-0.5)
                    for i in range(4):
                        src = SCR.sub(q0 + i)[:, 0:T]
                        if q0 == 0:
                            S.stt(src, src, 128.0 ** -0.5, ts_[i][:, 0:T], ALU.mult, ALU.mult)
                        else:
                            S.tt('dve', src, src, ts_[i][:, 0:T], ALU.mult)
                        tmp_pool.release(ts_[i])
                tts = [tmp_pool.alloc() for _ in range(6)]

                def as4(t):
                    return Tile(t.t[:, :].rearrange('p (a b) -> p a b', a=4), t.name + '_v')

                def mk(t):
                    v = as4(t)
                    v.bufs = t.bufs
                    return v
                dsets = [dict(EX=EX, DXL=DXL, DXU=DXU, EGR=EGR),
                         dict(EX=mk(tts[0]), DXL=mk(tts[1]), DXU=mk(tts[2]), EGR=mk(tts[3]))]
                extra = []
                for t in tts[4:6]:
                    for hf in range(2):
                        e = Tile(t.t[:, hf * 256:(hf + 1) * 256].rearrange('p (a b) -> p a b', a=2), t.name + '_e%d' % hf)
                        e.bufs = t.bufs
                        extra.append(e)
                for e in extra:
                    utr_pool.free.append(e)

                def prep_gen(g, ds):
                    EXs, DXLs, DXUs, EGRs = ds['EX'], ds['DXL'], ds['DXU'], ds['EGR']
                    GDs = EGRs
                    for j in range(4):
                        hh = half * 4 + j
                        S.ts('dve', GDs[:, j, :], msk['MUI'], GG[:, g, hh:hh + 1], multi=(j > 0))
                    yield
                    ps = ps_pool.alloc()
                    S.mm(ps[:, :], ONES, GDs.v().rearrange('p a b -> p (a b)'))
                    for j in range(4):
                        hh = half * 4 + j
                        S.ts('dve', EXs[:, j, :], ps[:, j * 128:(j + 1) * 128], GC[:, g, hh:hh + 1], None,
                             ALU.subtract, multi=(j > 0))
                    S.act(EGRs.v().rearrange('p a b -> p (a b)'), ps[:, :], AF.Exp)
                    ps_pool.release(ps)
                    yield
                    exf = EXs.v().rearrange('p a b -> p (a b)')
                    S.stt(exf, exf, -1.0, exf, ALU.mult, ALU.max)
                    S.act(EXs.v(), EXs.v(), AF.Exp, scale=-1.0)
                    yield
                    for j in range(4):
                        S.tt('dve', DXLs[:, j, :], EXs[:, j, :], msk['MLS'], ALU.mult, multi=(j > 0))
                    yield
                    for j in range(4):
                        S.tt('dve', DXUs[:, j, :], EXs[:, j, :], msk['MUI'], ALU.mult, multi=(j > 0))
                    yield

                run_interleaved([prep_gen(0, dsets[0])])
                for g in range(G):
                    gens = [gdn_pair(tl, l, g, half, jp, msk, dsets[g % 2]) for jp in range(2)]
                    if g + 1 < G:
                        gens.append(prep_gen(g + 1, dsets[(g + 1) % 2]))
                    run_interleaved(gens)
                for e in extra:
                    utr_pool.free.remove(e)
                tmp_pool.release(*tts)
            if tl.last:
                if tl.kind == 'p':
                    S.dma(o_delta['p'][l, 0].rearrange('h k v -> k h v'), SST.sub(l), ('sst', l))
                else:
                    for s in range(NSS):
                        S.dma(o_delta['s'][l, s].rearrange('h k v -> k h v'), SST.sub(s), ('sst', s))
            for blk in range(4):
                wt = wnext('m', OFF_DZ + blk * 256)
                for cc in range(2):
                    c = blk * 2 + cc
                    ps = ps_pool.alloc()
                    mm8(ps[:, 0:T], wt, cc, X, T)
                    t = tmp_pool.alloc()
                    S.act(t[:, 0:T], ps[:, 0:T], AF.Silu)
                    ps_pool.release(ps)
                    S.tt('dve', BB.sub(c)[:, 0:T].r(), BB.sub(c)[:, 0:T], t[:, 0:T], ALU.mult)
                    tmp_pool.release(t)
                wdone(wt)
            merge(tl, l, 3)

        def out_proj(tl, l):
            T = tl.T
            for blk in range(4):
                wt = wnext('m', blk * 256)
                for cc in range(2):
                    c = blk * 2 + cc
                    ps = ps_pool.alloc()
                    mm8(ps[:, 0:T], wt, cc, MIX, T)
                    S.stt(BB.sub(c)[:, 0:T], X.sub(c)[:, 0:T], ALPHA, ps[:, 0:T], ALU.mult, ALU.add)
                    ps_pool.release(ps)
                wdone(wt)
            ln_feature(BB, lambda c: X.sub(c)[:, 0:T].r(), T, lambda c: par(l, c, PR_LNG), lambda c: par(l, c, PR_LNB),
                       AF.Identity)

        def load_sample_states(l):
            for s in range(NSS):
                for hf in range(2):
                    cs = slice(hf * 512, (hf + 1) * 512)
                    stg = tmp_pool.alloc()
                    key = ('tmp', tmp_idx[id(stg)])
                    S.dma(stg[0:2, :], st_a[l, s][:, cs], key)
                    S.dma(stg[2:5, :], st_b[l, s][:, cs], key, multi=True)
                    S.dma(stg[5:6, :], st_lru[l, s:s + 1][:, cs], key, multi=True)
                    S.dma(stg[6:36, :], st_c[l, s][:, cs], key, multi=True)
                    ps = ps_pool.alloc()
                    for cc in range(4):
                        S.transpose(ps[:, cc * 36:(cc + 1) * 36], stg[0:36, cc * 128:(cc + 1) * 128], IDENT[0:36, 0:36],
                                    multi=(cc > 0))
                    S.copy('act', STT[:, s, hf * 4:(hf + 1) * 4, :], ps[:, 0:4 * 36].rearrange('p (c k) -> p c k', c=4),
                           multi=(s > 0 or hf > 0))
                    ps_pool.release(ps)
                    tmp_pool.release(stg)
                for sec in range(3):
                    for hf in range(2):
                        stg = tmp_pool.alloc()
                        key = ('tmp', tmp_idx[id(stg)])
                        S.dma(stg[0:3, :], st_d[l, s][:, sec * D + hf * 512:sec * D + (hf + 1) * 512], key)
                        ps = ps_pool.alloc()
                        for cc in range(4):
                            S.transpose(ps[:, cc * 3:(cc + 1) * 3], stg[0:3, cc * 128:(cc + 1) * 128], IDENT[0:3, 0:3],
                                        multi=(cc > 0))
                        h0 = sec * 8 + hf * 4
                        S.copy('act', STD[:, s, h0:h0 + 4, :], ps[:, 0:12].rearrange('p (c k) -> p c k', c=4),
                               multi=(s > 0 or sec > 0 or hf > 0))
                        ps_pool.release(ps)
                        tmp_pool.release(stg)

        XS = V(BA.bufs, BAF[:, 0:4096].rearrange('p (g d) -> p g d', g=4))

        def load_tile(tl):
            G = tl.G
            if tl.kind == 'p':
                src = x_p[tl.idx * TP:(tl.idx + 1) * TP, :].rearrange('(g p) d -> p g d', p=128)
                S.dma(XS, src, 'xs')
            else:
                S.dma(XS[:, 0, :], x_s, 'xs')
            for g in range(G):
                for hf in range(2):
                    S.bn_stats(XST[:, hf, :], XS[:, g, hf * 512:(hf + 1) * 512], multi=(hf > 0))
                S.bn_aggr(XMV[:, 0:2], XST.v().rearrange('p a b -> p (a b)'))
                S.act(XMV[:, 2:3], XMV[:, 1:2], AF.Ln, bias=C_LN)
                S.act(XMV[:, 2:3], XMV[:, 2:3], AF.Exp, scale=-0.5)
                S.ts('dve', XS[:, g, :], XS[:, g, :], XMV[:, 0:1], XMV[:, 2:3], ALU.subtract, ALU.mult)
                for blk in range(2):
                    ps = ps_pool.alloc()
                    for cc in range(4):
                        c = blk * 4 + cc
                        S.transpose(ps[:, cc * 128:(cc + 1) * 128], XS[:, g, c * 128:(c + 1) * 128], IDENT,
                                    multi=(cc > 0))
                    for cc in range(4):
                        c = blk * 4 + cc
                        S.act(X.sub(c)[:, g * 128:(g + 1) * 128].r(), ps[:, cc * 128:(cc + 1) * 128], AF.Identity,
                              scale=PLN[:, c, 0:1], bias=PLN[:, c, 1:2], multi=(g > 0))
                    ps_pool.release(ps)

        def store_tile(tl):
            G = tl.G
            for g in range(G):
                for blk in range(2):
                    ps = ps_pool.alloc()
                    for cc in range(4):
                        c = blk * 4 + cc
                        S.transpose(ps[:, cc * 128:(cc + 1) * 128], X.sub(c)[:, g * 128:(g + 1) * 128], IDENT,
                                    multi=(cc > 0))
                    S.copy('act', XS[:, g, blk * 512:(blk + 1) * 512], ps[:, :], multi=(g > 0 or blk > 0))
                    ps_pool.release(ps)
            if tl.kind == 'p':
                dst = y_p[tl.idx * TP:(tl.idx + 1) * TP, :].rearrange('(g p) d -> p g d', p=128)
                S.dma(dst, XS, 'xs')
            else:
                S.dma(y_s, XS[:, 0, :], 'xs')

        for tl in tiles:
            load_tile(tl)
            for l in range(n_layers):
                if tl.kind == 's':
                    load_sample_states(l)
                branch_a(tl, l)
                branch_b(tl, l)
                branch_c(tl, l)
                branch_d(tl, l)
                out_proj(tl, l)
            store_tile(tl)
        assert not wsched and not winflight
        S.emit()
    return nc


def _consts():
    c = np.zeros((10, 128, 128), np.float32)
    idx = np.arange(128)
    c[0] = np.eye(128)
    c[1] = 1.0
    c[2] = 1.0 / D
    for base, bl in ((3, 64), (6, 32)):
        same = (idx[:, None] // bl) == (idx[None, :] // bl)
        c[base + 0] = same & (idx[None, :] >= idx[:, None])
        c[base + 1] = same & (idx[None, :] < idx[:, None])
        c[base + 2] = same
    for b in range(2):
        c[9, :, b] = (idx // 64) == b
    for b in range(4):
        c[9, :, 2 + b] = (idx // 32) == b
    return c


_NC_CACHE = {}


def kernel(x_prompt, x_sample, state_conv_a, state_conv_b, state_lru, state_conv_c, state_conv_d, state_delta,
           ln_in_g, ln_in_b, w_in, b_gate, a_conv_w, b_conv_w, b_conv_b, b_wx, b_bx, b_wa, b_ba, b_lambda,
           c_conv_w, c_conv_b, c_ln_g, c_ln_b, d_conv_w, d_a_log, d_dt_bias, d_norm_g, w_branch, w_out, ln_g, ln_b):
    f = lambda a: np.ascontiguousarray(np.asarray(a, dtype=np.float32))
    n = 8
    if 'nc' not in _NC_CACHE:
        _NC_CACHE['nc'] = build_program()
    nc = _NC_CACHE['nc']
    rows = np.concatenate([
        f(b_gate), f(a_conv_w), f(b_conv_w), f(b_conv_b)[:, None], f(b_bx).reshape(NL, 1, D),
        f(b_ba).reshape(NL, 1, D), f(b_lambda)[:, None], f(c_conv_w), f(c_conv_b)[:, None], f(c_ln_g)[:, None],
        f(c_ln_b)[:, None], f(d_conv_w).reshape(NL, 4 * 3, D), f(ln_g)[:, None], f(ln_b)[:, None]], axis=1)
    assert rows.shape == (NL, NPAR, D)
    shared = dict(
        ln_in=np.stack([f(ln_in_g), f(ln_in_b)]), w_in=f(w_in), par_rows=np.ascontiguousarray(rows),
        b_wx=f(b_wx), b_wa=f(b_wa), d_alog=f(d_a_log), d_dtb=f(d_dt_bias), d_ng=f(d_norm_g),
        w_br=f(w_branch), w_out=f(w_out), consts=_consts())
    xp, xs = f(x_prompt), f(x_sample)
    sa, sbb, sl, sc, sd, sdl = (f(a) for a in (state_conv_a, state_conv_b, state_lru, state_conv_c, state_conv_d,
                                               state_delta))
    in_maps = []
    for c in range(n):
        q = slice(c * NSS, (c + 1) * NSS)
        m = dict(shared)
        m.update(x_p=xp[c], x_s=np.ascontiguousarray(xs[q].reshape(NSS * LS, D)),
                 st_a=np.ascontiguousarray(sa[:, q]), st_b=np.ascontiguousarray(sbb[:, q]),
                 st_lru=np.ascontiguousarray(sl[:, q]), st_c=np.ascontiguousarray(sc[:, q]),
                 st_d=np.ascontiguousarray(sd[:, q]), st_delta=np.ascontiguousarray(sdl[:, q]))
        in_maps.append(m)
    res = run_bass_kernel_spmd(nc, in_maps, core_ids=list(range(n)))
    R = res.results
    cat = lambda k, ax: np.concatenate([np.asarray(r[k]) for r in R], axis=ax)
    y_prompt = np.stack([np.asarray(r['y_p']) for r in R]).astype(np.float32)
    y_sample = cat('y_s', 0).reshape(n * NSS, LS, D).astype(np.float32)
    outs = [y_prompt, y_sample]
    for pre in ('p', 's'):
        outs.append(cat(pre + '_a', 1))
        outs.append(cat(pre + '_b', 1))
        outs.append(cat(pre + '_lru', 1)[:, :, 0, :])
        outs.append(cat(pre + '_c', 1))
        outs.append(cat(pre + '_d', 1))
        outs.append(cat(pre + '_delta', 1))
    return tuple(np.ascontiguousarray(o, dtype=np.float32) for o in outs)
```
